# Optimizing a Trainium2 kernel written in Bass

```python
import math
import jax
import jax.numpy as jnp
from jax import lax
import numpy as np

D_MODEL = 1024
BATCH = 4
SEQ = 8192
DEPTH = 4

MIX_WIDTH = D_MODEL
GROUP_WIDTH = MIX_WIDTH // 2
CONV_WIDTH = 4
MLSTM_HEADS = 4
MLSTM_HEAD_DIM = GROUP_WIDTH // MLSTM_HEADS
MLSTM_CHUNK = 64
MOBA_HEADS = 4
MOBA_HEAD_DIM = GROUP_WIDTH // MOBA_HEADS
MOBA_BLOCK = 256
MOBA_TOPK = 3
MOBA_Q_CHUNK = 64
REL_BUCKETS = 32
REL_MAX_DISTANCE = 1024
SSM_HEADS = 8
SSM_HEAD_DIM = GROUP_WIDTH // SSM_HEADS
SSM_GROUPS = 2
SSM_STATE = 128
SSM_CHUNK = 128
RWKV_HEAD_DIM = 64
RWKV_HEADS = GROUP_WIDTH // RWKV_HEAD_DIM
RWKV_DECAY_LORA = 64
RWKV_AAA_LORA = 64
RWKV_GATE_LORA = 128
RWKV_LN_EPS = 64e-5
D_FF = 2816
NORM_EPS = 1e-6
N_EVEN = (DEPTH + 1) // 2
N_ODD = DEPTH // 2
AB_IN = 4 * GROUP_WIDTH + 2 * MLSTM_HEADS + 3 * GROUP_WIDTH
SSM_CONV_DIM = GROUP_WIDTH + 2 * SSM_GROUPS * SSM_STATE
RWKV_IN = 3 * GROUP_WIDTH + RWKV_DECAY_LORA + RWKV_AAA_LORA + RWKV_GATE_LORA
CD_IN = GROUP_WIDTH + SSM_CONV_DIM + SSM_HEADS + RWKV_IN

kernel_name = 'hybrid_mlstm_moba_ssd_rwkv7'


def rmsnorm(x, g):
    xf = x.astype(jnp.float32)
    y = xf * lax.rsqrt(jnp.mean(xf * xf, axis=-1, keepdims=True) + NORM_EPS)
    return (y * g.astype(jnp.float32)).astype(x.dtype)


def swiglu(h, w_gate, w_up, w_down):
    a = jnp.einsum('bsd,df->bsf', h, w_gate)
    b = jnp.einsum('bsd,df->bsf', h, w_up)
    return jnp.einsum('bsf,fd->bsd', jax.nn.silu(a) * b, w_down)


def causal_dwconv(x, w, b):
    k, c = w.shape
    y = lax.conv_general_dilated(x, w[:, None, :].astype(x.dtype), window_strides=(1,),
                                 padding=[(k - 1, 0)], dimension_numbers=('NWC', 'WIO', 'NWC'),
                                 feature_group_count=c)
    return y + b.astype(x.dtype)


def split_heads(t, n_heads):
    b, s, _ = t.shape
    return t.reshape(b, s, n_heads, -1).transpose(0, 2, 1, 3)


def merge_heads(t):
    b, n, s, d = t.shape
    return t.transpose(0, 2, 1, 3).reshape(b, s, n * d)


def t5_bucket(rel):
    n = jnp.maximum(rel, 0)
    max_exact = REL_BUCKETS // 2
    nf = jnp.maximum(n, 1).astype(jnp.float32)
    large = max_exact + (jnp.log(nf / max_exact) / math.log(REL_MAX_DISTANCE / max_exact)
                         * (REL_BUCKETS - max_exact)).astype(jnp.int32)
    large = jnp.minimum(large, REL_BUCKETS - 1)
    return jnp.where(n < max_exact, n, large)


def mlstm_chunkwise(q, k, v, i_pre, f_pre):
    bsz, nh, s, dh = q.shape
    nc = s // MLSTM_CHUNK
    def to_chunks(t):
        t = t.astype(jnp.float32)
        return jnp.moveaxis(t.reshape((bsz, nh, nc, MLSTM_CHUNK) + t.shape[3:]), 2, 0)
    qc, kc, vc = to_chunks(q), to_chunks(k), to_chunks(v)
    ic = to_chunks(i_pre)
    lfc = to_chunks(jax.nn.log_sigmoid(f_pre.astype(jnp.float32)))
    causal = jnp.tril(jnp.ones((MLSTM_CHUNK, MLSTM_CHUNK), dtype=bool))

    def step(carry, inp):
        c_mat, n_vec, m = carry
        q_, k_, v_, i_, lf_ = inp
        b = jnp.cumsum(lf_, axis=-1)
        dmat = jnp.where(causal, b[..., :, None] - b[..., None, :] + i_[..., None, :], -jnp.inf)
        m_inter = b + m[..., None]
        m_t = jnp.maximum(m_inter, dmat.max(-1))
        w = jnp.exp(dmat - m_t[..., None]) * jnp.einsum('bhtd,bhsd->bhts', q_, k_)
        inter = jnp.exp(m_inter - m_t)
        num = inter[..., None] * jnp.einsum('bhtk,bhkv->bhtv', q_, c_mat) + jnp.einsum('bhts,bhsv->bhtv', w, v_)
        den = inter * jnp.einsum('bhtk,bhk->bht', q_, n_vec) + w.sum(-1)
        h = num / jnp.maximum(jnp.abs(den), jnp.exp(-m_t))[..., None]
        g = b[..., -1:] - b + i_
        m_new = jnp.maximum(b[..., -1] + m, g.max(-1))
        decay = jnp.exp(b[..., -1] + m - m_new)
        wk = jnp.exp(g - m_new[..., None])[..., None] * k_
        c_mat = decay[..., None, None] * c_mat + jnp.einsum('bhsk,bhsv->bhkv', wk, v_)
        n_vec = decay[..., None] * n_vec + wk.sum(-2)
        return (c_mat, n_vec, m_new), h

    init = (jnp.zeros((bsz, nh, dh, dh), jnp.float32), jnp.zeros((bsz, nh, dh), jnp.float32),
            jnp.zeros((bsz, nh), jnp.float32))
    _, hs = lax.scan(step, init, (qc, kc, vc, ic, lfc))
    return jnp.moveaxis(hs, 0, 2).reshape(bsz, nh, s, dh)


def moba_attention(q, k, v, rel_bias):
    bsz, nh, s, dh = q.shape
    nb = -(-s // MOBA_BLOCK)
    s_pad = nb * MOBA_BLOCK
    pad = ((0, 0), (0, 0), (0, s_pad - s), (0, 0))
    q, k, v = jnp.pad(q, pad), jnp.pad(k, pad), jnp.pad(v, pad)
    kb = k.reshape(bsz, nh, nb, MOBA_BLOCK, dh)
    vb = v.reshape(bsz, nh, nb, MOBA_BLOCK, dh)
    k_mean = kb.astype(jnp.float32).mean(3)
    topk = min(MOBA_TOPK, nb)
    nq = s_pad // MOBA_Q_CHUNK
    qs = jnp.moveaxis(q.reshape(bsz, nh, nq, MOBA_Q_CHUNK, dh), 2, 0)
    scale = dh ** -0.5
    b_idx = jnp.arange(bsz)[:, None, None, None]
    h_idx = jnp.arange(nh)[None, :, None, None]
    h_idx5 = jnp.arange(nh)[None, :, None, None, None]
    offs = jnp.arange(MOBA_BLOCK)
    bias_hb = rel_bias.T

    def chunk(args):
        q_, c = args
        q_pos = c * MOBA_Q_CHUNK + jnp.arange(MOBA_Q_CHUNK)
        own = (c * MOBA_Q_CHUNK) // MOBA_BLOCK
        gate = jnp.einsum('bhqd,bhnd->bhqn', q_.astype(jnp.float32), k_mean)
        gate = jnp.where(jnp.arange(nb) < own, gate, -jnp.inf)
        _, idx = lax.top_k(gate, topk)
        valid = idx < own
        k_sel = kb[b_idx, h_idx, idx]
        v_sel = vb[b_idx, h_idx, idx]
        s_sel = jnp.einsum('bhqd,bhqnkd->bhqnk', q_, k_sel).astype(jnp.float32) * scale
        rel_sel = q_pos[None, None, :, None, None] - (idx[..., None] * MOBA_BLOCK + offs)
        s_sel = s_sel + bias_hb[h_idx5, t5_bucket(rel_sel)].astype(jnp.float32)
        s_sel = jnp.where(valid[..., None], s_sel, -jnp.inf)
        k_own = lax.dynamic_slice_in_dim(k, own * MOBA_BLOCK, MOBA_BLOCK, axis=2)
        v_own = lax.dynamic_slice_in_dim(v, own * MOBA_BLOCK, MOBA_BLOCK, axis=2)
        rel_own = q_pos[:, None] - (own * MOBA_BLOCK + offs)[None, :]
        bias_own = jnp.moveaxis(rel_bias[t5_bucket(rel_own)], -1, 0).astype(jnp.float32)
        s_own = jnp.einsum('bhqd,bhkd->bhqk', q_, k_own).astype(jnp.float32) * scale + bias_own
        s_own = jnp.where(rel_own >= 0, s_own, -jnp.inf)
        logits = jnp.concatenate([s_sel.reshape(bsz, nh, MOBA_Q_CHUNK, topk * MOBA_BLOCK), s_own], axis=-1)
        probs = jax.nn.softmax(logits, axis=-1).astype(v.dtype)
        p_sel = probs[..., :topk * MOBA_BLOCK].reshape(bsz, nh, MOBA_Q_CHUNK, topk, MOBA_BLOCK)
        p_own = probs[..., topk * MOBA_BLOCK:]
        return (jnp.einsum('bhqnk,bhqnkd->bhqd', p_sel, v_sel)
                + jnp.einsum('bhqk,bhkd->bhqd', p_own, v_own))

    out = lax.map(chunk, (qs, jnp.arange(nq)))
    return jnp.moveaxis(out, 0, 2).reshape(bsz, nh, s_pad, dh)[:, :, :s]


def ssd_chunked(xs, dt, a_neg, bm, cm):
    bsz, s, nh, hp = xs.shape
    ng, ns = bm.shape[2], bm.shape[3]
    hg = nh // ng
    nc = s // SSM_CHUNK
    a = jnp.moveaxis((dt * a_neg).reshape(bsz, nc, SSM_CHUNK, ng, hg), 2, -1)
    xdt = (xs * dt[..., None]).reshape(bsz, nc, SSM_CHUNK, ng, hg, hp)
    bc = bm.reshape(bsz, nc, SSM_CHUNK, ng, ns)
    cc = cm.reshape(bsz, nc, SSM_CHUNK, ng, ns)
    acs = jnp.cumsum(a, axis=-1)
    causal = jnp.tril(jnp.ones((SSM_CHUNK, SSM_CHUNK), dtype=bool))
    lmat = jnp.exp(jnp.where(causal, acs[..., :, None] - acs[..., None, :], -jnp.inf))
    cb = jnp.einsum('bctgn,bcsgn->bcgts', cc, bc)
    y_diag = jnp.einsum('bcgts,bcghts,bcsghp->bctghp', cb, lmat, xdt)
    decay_s = jnp.exp(acs[..., -1:] - acs)
    states = jnp.einsum('bcsgn,bcghs,bcsghp->bcghpn', bc, decay_s, xdt)
    chunk_decay = jnp.exp(acs[..., -1])

    def step(h, inp):
        st, dc = inp
        return h * dc[..., None, None] + st, h

    _, h_prev = lax.scan(step, jnp.zeros((bsz, ng, hg, hp, ns), jnp.float32),
                         (jnp.moveaxis(states, 1, 0), jnp.moveaxis(chunk_decay, 1, 0)))
    h_prev = jnp.moveaxis(h_prev, 0, 1)
    y_off = jnp.einsum('bctgn,bcghpn,bcght->bctghp', cc, h_prev, jnp.exp(acs))
    return (y_diag + y_off).reshape(bsz, s, nh, hp)


def rwkv7_time_mix(p, mu, w0, w2, a0, a2, g2, k_k, k_a, r_k, ln_g, ln_b):
    bsz, s, _ = p.shape
    g = GROUP_WIDTH
    nh, hd = RWKV_HEADS, RWKV_HEAD_DIM
    p = p.astype(jnp.float32)
    prev = jnp.pad(p, ((0, 0), (1, 0), (0, 0)))[:, :-1]
    p = p + (prev - p) * mu
    r, k, v = p[..., :g], p[..., g:2 * g], p[..., 2 * g:3 * g]
    o = 3 * g
    wd = p[..., o:o + RWKV_DECAY_LORA]
    ad = p[..., o + RWKV_DECAY_LORA:o + RWKV_DECAY_LORA + RWKV_AAA_LORA]
    gd = p[..., o + RWKV_DECAY_LORA + RWKV_AAA_LORA:]
    log_w = -jax.nn.softplus(-(w0 + jnp.tanh(wd) @ w2)) - 0.5
    decay = jnp.exp(-jnp.exp(log_w))
    a = jax.nn.sigmoid(a0 + ad @ a2)
    gate = jax.nn.sigmoid(gd) @ g2
    heads = lambda t: t.reshape(bsz, s, nh, hd)
    kk = heads(k * k_k)
    kk = kk / jnp.maximum(jnp.sqrt(jnp.sum(kk * kk, axis=-1, keepdims=True)), 1e-12)
    k = k * (1.0 + (a - 1.0) * k_a)
    r_h, k_h, v_h = heads(r), heads(k), heads(v)
    tm = lambda t: jnp.moveaxis(t, 1, 0)

    def step(state, inp):
        r_t, w_t, k_t, v_t, kk_t, a_t = inp
        sa = jnp.einsum('bhvk,bhk->bhv', state, -kk_t)
        state = (state * w_t[:, :, None, :] + sa[..., None] * (kk_t * a_t)[:, :, None, :]
                 + v_t[..., None] * k_t[:, :, None, :])
        return state, jnp.einsum('bhvk,bhk->bhv', state, r_t)

    _, y = lax.scan(step, jnp.zeros((bsz, nh, hd, hd), jnp.float32),
                    (tm(r_h), tm(heads(decay)), tm(k_h), tm(v_h), tm(kk), tm(heads(a))))
    y = jnp.moveaxis(y, 0, 1)
    mean = y.mean(-1, keepdims=True)
    var = jnp.mean((y - mean) ** 2, axis=-1, keepdims=True)
    y = (y - mean) * lax.rsqrt(var + RWKV_LN_EPS) * ln_g.reshape(nh, hd) + ln_b.reshape(nh, hd)
    y = y + jnp.sum(r_h * k_h * r_k, axis=-1, keepdims=True) * v_h
    return y.reshape(bsz, s, g) * gate


def mixer_ab(u, w_in, w_out, conv_w, conv_b, b_i, b_f, h_norm, q_norm, k_norm, rel_bias):
    g = GROUP_WIDTH
    p = jnp.einsum('bsd,de->bse', u, w_in)
    qk = jax.nn.silu(causal_dwconv(p[..., :2 * g], conv_w, conv_b))
    v_m = p[..., 2 * g:3 * g]
    o_m = p[..., 3 * g:4 * g]
    gates = p[..., 4 * g:4 * g + 2 * MLSTM_HEADS].astype(jnp.float32)
    i_pre = (gates[..., :MLSTM_HEADS] + b_i).transpose(0, 2, 1)
    f_pre = (gates[..., MLSTM_HEADS:] + b_f).transpose(0, 2, 1)
    off = 4 * g + 2 * MLSTM_HEADS
    q_a, k_a, v_a = p[..., off:off + g], p[..., off + g:off + 2 * g], p[..., off + 2 * g:off + 3 * g]
    h_m = mlstm_chunkwise(split_heads(qk[..., :g], MLSTM_HEADS),
                          split_heads(qk[..., g:], MLSTM_HEADS) * (MLSTM_HEAD_DIM ** -0.5),
                          split_heads(v_m, MLSTM_HEADS), i_pre, f_pre)
    h_m = rmsnorm(h_m, h_norm.reshape(MLSTM_HEADS, 1, MLSTM_HEAD_DIM))
    h_m = merge_heads(h_m) * jax.nn.sigmoid(o_m.astype(jnp.float32))
    q = rmsnorm(split_heads(q_a, MOBA_HEADS), q_norm)
    k = rmsnorm(split_heads(k_a, MOBA_HEADS), k_norm)
    h_a = merge_heads(moba_attention(q, k, split_heads(v_a, MOBA_HEADS), rel_bias))
    y = jnp.concatenate([h_m, h_a.astype(jnp.float32)], axis=-1).astype(u.dtype)
    return jnp.einsum('bse,ed->bsd', y, w_out)


def mixer_cd(u, w_in, w_out, conv_w, conv_b, dt_bias, a_log, d_skip, ssm_norm,
             mu, w0, w2, a0, a2, g2, k_k, k_a, r_k, ln_g, ln_b):
    bsz, s, _ = u.shape
    g = GROUP_WIDTH
    p = jnp.einsum('bsd,de->bse', u, w_in)
    z = p[..., :g].astype(jnp.float32)
    xbc = jax.nn.silu(causal_dwconv(p[..., g:g + SSM_CONV_DIM], conv_w, conv_b)).astype(jnp.float32)
    dt_off = g + SSM_CONV_DIM
    dt = jax.nn.softplus(p[..., dt_off:dt_off + SSM_HEADS].astype(jnp.float32) + dt_bias)
    rw = p[..., dt_off + SSM_HEADS:]
    xs = xbc[..., :g].reshape(bsz, s, SSM_HEADS, SSM_HEAD_DIM)
    bm = xbc[..., g:g + SSM_GROUPS * SSM_STATE].reshape(bsz, s, SSM_GROUPS, SSM_STATE)
    cm = xbc[..., g + SSM_GROUPS * SSM_STATE:].reshape(bsz, s, SSM_GROUPS, SSM_STATE)
    a_neg = -jnp.exp(a_log.astype(jnp.float32))
    y = ssd_chunked(xs, dt, a_neg, bm, cm) + d_skip[:, None] * xs
    y = y.reshape(bsz, s, g) * jax.nn.silu(z)
    y = rmsnorm(y.reshape(bsz, s, SSM_GROUPS, -1), ssm_norm.reshape(SSM_GROUPS, -1)).reshape(bsz, s, g)
    y2 = rwkv7_time_mix(rw, mu, w0, w2, a0, a2, g2, k_k, k_a, r_k, ln_g, ln_b)
    out = jnp.concatenate([y, y2], axis=-1).astype(u.dtype)
    return jnp.einsum('bse,ed->bsd', out, w_out)


def setup_inputs(seed: int = 0) -> dict:
    key = jax.random.key(seed)
    ks = iter(jax.random.split(key, 48))
    nrm = lambda shape, scale: jax.random.normal(next(ks), shape, jnp.float32) * scale
    gain = lambda shape: 1.0 + 0.02 * jax.random.normal(next(ks), shape, jnp.float32)
    x = nrm((BATCH, SEQ, D_MODEL), 1.0)
    ffn_norm = gain((DEPTH, 2, D_MODEL))
    mix_norm = gain((DEPTH, D_MODEL))
    ffn_w_gate = nrm((DEPTH, 2, D_MODEL, D_FF), D_MODEL ** -0.5)
    ffn_w_up = nrm((DEPTH, 2, D_MODEL, D_FF), D_MODEL ** -0.5)
    ffn_w_down = nrm((DEPTH, 2, D_FF, D_MODEL), D_FF ** -0.5)
    rel_bias = nrm((REL_BUCKETS, MOBA_HEADS), 0.5)
    ab_w_in = nrm((N_EVEN, D_MODEL, AB_IN), D_MODEL ** -0.5)
    ab_w_out = nrm((N_EVEN, MIX_WIDTH, D_MODEL), MIX_WIDTH ** -0.5)
    mlstm_conv_w = nrm((N_EVEN, CONV_WIDTH, 2 * GROUP_WIDTH), CONV_WIDTH ** -0.5)
    mlstm_conv_b = nrm((N_EVEN, 2 * GROUP_WIDTH), 0.02)
    mlstm_b_i = nrm((N_EVEN, MLSTM_HEADS), 0.1)
    mlstm_b_f = jnp.linspace(3.0, 6.0, MLSTM_HEADS)[None, :] + nrm((N_EVEN, MLSTM_HEADS), 0.1)
    mlstm_h_norm = gain((N_EVEN, GROUP_WIDTH))
    moba_q_norm = gain((N_EVEN, MOBA_HEAD_DIM))
    moba_k_norm = gain((N_EVEN, MOBA_HEAD_DIM))
    cd_w_in = nrm((N_ODD, D_MODEL, CD_IN), D_MODEL ** -0.5)
    cd_w_out = nrm((N_ODD, MIX_WIDTH, D_MODEL), MIX_WIDTH ** -0.5)
    ssm_conv_w = nrm((N_ODD, CONV_WIDTH, SSM_CONV_DIM), CONV_WIDTH ** -0.5)
    ssm_conv_b = nrm((N_ODD, SSM_CONV_DIM), 0.02)
    dt0 = jnp.exp(jax.random.uniform(next(ks), (N_ODD, SSM_HEADS), jnp.float32,
                                     math.log(1e-3), math.log(1e-1)))
    ssm_dt_bias = dt0 + jnp.log(-jnp.expm1(-dt0))
    ssm_A_log = jnp.log(jax.random.uniform(next(ks), (N_ODD, SSM_HEADS), jnp.float32, 1.0, 16.0))
    ssm_D = gain((N_ODD, SSM_HEADS))
    ssm_norm = gain((N_ODD, GROUP_WIDTH))
    rwkv_mu = jax.random.uniform(next(ks), (N_ODD, RWKV_IN), jnp.float32)
    rwkv_w0 = jnp.tile(jnp.linspace(-6.0, -1.0, RWKV_HEAD_DIM), RWKV_HEADS)[None, :] + nrm((N_ODD, GROUP_WIDTH), 0.1)
    rwkv_w2 = nrm((N_ODD, RWKV_DECAY_LORA, GROUP_WIDTH), 0.1)
    rwkv_a0 = nrm((N_ODD, GROUP_WIDTH), 0.1)
    rwkv_a2 = nrm((N_ODD, RWKV_AAA_LORA, GROUP_WIDTH), 0.1)
    rwkv_g2 = nrm((N_ODD, RWKV_GATE_LORA, GROUP_WIDTH), RWKV_GATE_LORA ** -0.5)
    rwkv_k_k = 0.85 + nrm((N_ODD, GROUP_WIDTH), 0.02)
    rwkv_k_a = 1.0 + nrm((N_ODD, GROUP_WIDTH), 0.02)
    rwkv_r_k = nrm((N_ODD, RWKV_HEADS, RWKV_HEAD_DIM), 0.1)
    rwkv_ln_g = gain((N_ODD, GROUP_WIDTH))
    rwkv_ln_b = nrm((N_ODD, GROUP_WIDTH), 0.02)
    return {'x': x, 'ffn_norm': ffn_norm, 'mix_norm': mix_norm, 'ffn_w_gate': ffn_w_gate,
            'ffn_w_up': ffn_w_up, 'ffn_w_down': ffn_w_down, 'rel_bias': rel_bias,
            'ab_w_in': ab_w_in, 'ab_w_out': ab_w_out, 'mlstm_conv_w': mlstm_conv_w,
            'mlstm_conv_b': mlstm_conv_b, 'mlstm_b_i': mlstm_b_i, 'mlstm_b_f': mlstm_b_f,
            'mlstm_h_norm': mlstm_h_norm, 'moba_q_norm': moba_q_norm, 'moba_k_norm': moba_k_norm,
            'cd_w_in': cd_w_in, 'cd_w_out': cd_w_out, 'ssm_conv_w': ssm_conv_w, 'ssm_conv_b': ssm_conv_b,
            'ssm_dt_bias': ssm_dt_bias, 'ssm_A_log': ssm_A_log, 'ssm_D': ssm_D, 'ssm_norm': ssm_norm,
            'rwkv_mu': rwkv_mu, 'rwkv_w0': rwkv_w0, 'rwkv_w2': rwkv_w2, 'rwkv_a0': rwkv_a0,
            'rwkv_a2': rwkv_a2, 'rwkv_g2': rwkv_g2, 'rwkv_k_k': rwkv_k_k, 'rwkv_k_a': rwkv_k_a,
            'rwkv_r_k': rwkv_r_k, 'rwkv_ln_g': rwkv_ln_g, 'rwkv_ln_b': rwkv_ln_b}


def reference(x, ffn_norm, mix_norm, ffn_w_gate, ffn_w_up, ffn_w_down, rel_bias,
              ab_w_in, ab_w_out, mlstm_conv_w, mlstm_conv_b, mlstm_b_i, mlstm_b_f,
              mlstm_h_norm, moba_q_norm, moba_k_norm, cd_w_in, cd_w_out, ssm_conv_w, ssm_conv_b,
              ssm_dt_bias, ssm_A_log, ssm_D, ssm_norm, rwkv_mu, rwkv_w0, rwkv_w2, rwkv_a0,
              rwkv_a2, rwkv_g2, rwkv_k_k, rwkv_k_a, rwkv_r_k, rwkv_ln_g, rwkv_ln_b):
    for l in range(DEPTH):
        x = x + 0.5 * swiglu(rmsnorm(x, ffn_norm[l, 0]), ffn_w_gate[l, 0], ffn_w_up[l, 0], ffn_w_down[l, 0])
        u = rmsnorm(x, mix_norm[l])
        j = l // 2
        if l % 2 == 0:
            x = x + mixer_ab(u, ab_w_in[j], ab_w_out[j], mlstm_conv_w[j], mlstm_conv_b[j],
                             mlstm_b_i[j], mlstm_b_f[j], mlstm_h_norm[j], moba_q_norm[j],
                             moba_k_norm[j], rel_bias)
        else:
            x = x + mixer_cd(u, cd_w_in[j], cd_w_out[j], ssm_conv_w[j], ssm_conv_b[j],
                             ssm_dt_bias[j], ssm_A_log[j], ssm_D[j], ssm_norm[j],
                             rwkv_mu[j], rwkv_w0[j], rwkv_w2[j], rwkv_a0[j], rwkv_a2[j],
                             rwkv_g2[j], rwkv_k_k[j], rwkv_k_a[j], rwkv_r_k[j],
                             rwkv_ln_g[j], rwkv_ln_b[j])
        x = x + 0.5 * swiglu(rmsnorm(x, ffn_norm[l, 1]), ffn_w_gate[l, 1], ffn_w_up[l, 1], ffn_w_down[l, 1])
    return x
```

```python
import contextlib
import os
import numpy as np
import concourse.bass as bass
import concourse.mybir as mybir
from concourse.bass_utils import run_bass_kernel_spmd

F32 = mybir.dt.float32
BF16 = mybir.dt.bfloat16
AF = mybir.ActivationFunctionType
ALU = mybir.AluOpType
AX = mybir.AxisListType

D = 1024
DFF = 2816
NF = DFF // 128
G = 512
AB_IN = 3592
CD_IN = 3336
EPS = 1e-6

ENGS = ("pe", "act", "dve", "pool", "sp")


class Prog:
    def __init__(self, nc, same_engine_sync=("act", "dve", "pool")):
        self.nc = nc
        self.ops = {e: [] for e in ENGS}
        self.cnt = {e: 0 for e in ENGS}
        self.last_w = {}
        self.readers = {}
        self.waited = {}
        self.dma_cnt = {}
        self.same_engine_sync = set(same_engine_sync)
        self.sem_names = []
        self.nops = 0

    def _need(self, eng, tok, waits):
        if tok is None:
            return
        sem, val, src = tok
        if src == eng and eng not in self.same_engine_sync:
            return
        k = (eng, sem)
        if self.waited.get(k, 0) >= val:
            return
        self.waited[k] = val
        waits.append((sem, val))

    def op(self, eng, fn, reads=(), writes=(), dma_slot=None):
        waits = []
        for k in reads:
            self._need(eng, self.last_w.get(k), waits)
        for k in writes:
            self._need(eng, self.last_w.get(k), waits)
            for t in self.readers.get(k, ()):
                self._need(eng, t, waits)
        if dma_slot is None:
            self.cnt[eng] += 1
            sem = "c_" + eng
            tok = (sem, self.cnt[eng], eng)
            inc = (sem, 1)
        else:
            sem = "d_" + str(dma_slot)
            self.dma_cnt[sem] = self.dma_cnt.get(sem, 0) + 1
            tok = (sem, 16 * self.dma_cnt[sem], None)
            inc = (sem, 16)
        if sem not in self.sem_names:
            self.sem_names.append(sem)
        for k in writes:
            self.last_w[k] = tok
            self.readers[k] = []
        for k in reads:
            if k not in writes:
                lst = self.readers.setdefault(k, [])
                lst.append(tok)
                if len(lst) > 8:
                    best = {}
                    for t in lst:
                        if t[0] not in best or best[t[0]][1] < t[1]:
                            best[t[0]] = t
                    self.readers[k] = list(best.values())
        self.ops[eng].append((waits, fn, inc))
        self.nops += 1
        return tok

    def barrier(self):
        latest = {}
        for e in ENGS:
            if self.cnt[e]:
                latest["c_" + e] = (self.cnt[e], e)
        for sname, n in self.dma_cnt.items():
            latest[sname] = (16 * n, None)
        for e in ENGS:
            waits = []
            for sname, (v, src) in latest.items():
                if src == e and e not in self.same_engine_sync:
                    continue
                self._need(e, (sname, v, None), waits)
            if waits:
                self.ops[e].append((waits, None, None))

    def final_wait(self, eng, keys):
        waits = []
        for k in keys:
            self._need(eng, self.last_w.get(k), waits)
        self.ops[eng].append((waits, None, None))

    def emit(self):
        nc = self.nc
        with contextlib.ExitStack() as st:
            sems = {}
            for n in self.sem_names:
                sems[n] = st.enter_context(nc.semaphore(n))
            block = st.enter_context(nc.Block())

            def run(eng_name):
                def body(e):
                    for waits, fn, inc in self.ops[eng_name]:
                        for (s, v) in waits:
                            e.wait_ge(sems[s], v)
                        if fn is not None:
                            fn(e).then_inc(sems[inc[0]], inc[1])
                return body

            block.tensor(run("pe"))
            block.scalar(run("act"))
            block.vector(run("dve"))
            block.gpsimd(run("pool"))
            block.sync(run("sp"))


def _keys(lst):
    out = []
    for a in lst:
        if a is None:
            continue
        if isinstance(a, (str, tuple)):
            out.append(a)
        else:
            out.append(a.name)
    return out


class KB:
    def __init__(self, nc, st):
        self.nc = nc
        self.st = st
        self.p = Prog(nc)
        self.phase_st = None
        self.prefix = ""

    def sb(self, name, shape, dt=F32):
        st = self.phase_st if self.phase_st is not None else self.st
        return st.enter_context(self.nc.sbuf_tensor(self.prefix + name, list(shape), dt))

    @contextlib.contextmanager
    def phase(self, prefix):
        with contextlib.ExitStack() as ph:
            self.phase_st = ph
            self.prefix = prefix + "_"
            self.p.barrier()
            try:
                yield
            finally:
                self.phase_st = None
                self.prefix = ""

    def ps(self, name, shape=(128, 512), dt=F32):
        return self.st.enter_context(self.nc.psum_tensor(name, list(shape), dt))

    def dram(self, name, shape, dt=F32, kind="Internal"):
        return self.nc.dram_tensor(name, list(shape), dt, kind=kind).ap()

    def mm(self, out, lhsT, rhs, start=True, stop=True, r=(), w=()):
        self.p.op("pe", lambda e: e.matmul(out, lhsT=lhsT, rhs=rhs, start=start, stop=stop),
                  reads=_keys([lhsT, rhs]) + _keys(r), writes=_keys([out]) + _keys(w))

    def tr(self, out, in_, ident):
        self.p.op("pe", lambda e: e.transpose(out, in_, ident),
                  reads=_keys([in_, ident]), writes=_keys([out]))

    def act(self, out, in_, func, bias=None, scale=1.0, accum_out=None, r=(), w=()):
        def fn(e):
            kw = {}
            if bias is not None:
                kw["bias"] = bias
            if accum_out is not None:
                kw["accum_out"] = accum_out
            return e.activation(out=out, in_=in_, func=func, scale=scale, **kw)
        rd = [in_] + [a for a in (bias, scale) if not isinstance(a, (int, float)) and a is not None]
        self.p.op("act", fn, reads=_keys(rd) + _keys(r), writes=_keys([out, accum_out]) + _keys(w))

    def ts(self, out, in0, s1, op0, s2=None, op1=None, eng="dve", accum_out=None, r=(), w=()):
        def fn(e):
            kw = {}
            if op1 is not None:
                kw["op1"] = op1
            if accum_out is not None:
                kw["accum_out"] = accum_out
            return e.tensor_scalar(out=out, in0=in0, scalar1=s1, scalar2=s2, op0=op0, **kw)
        rd = [in0] + [a for a in (s1, s2) if not isinstance(a, (int, float)) and a is not None]
        self.p.op(eng, fn, reads=_keys(rd) + _keys(r), writes=_keys([out, accum_out]) + _keys(w))

    def tt(self, out, in0, in1, op, eng="dve", r=(), w=()):
        self.p.op(eng, lambda e: e.tensor_tensor(out=out, in0=in0, in1=in1, op=op),
                  reads=_keys([in0, in1]) + _keys(r), writes=_keys([out]) + _keys(w))

    def stt(self, out, in0, scalar, in1, op0, op1, eng="dve", r=(), w=()):
        rd = [in0, in1] + ([scalar] if not isinstance(scalar, (int, float)) else [])
        self.p.op(eng, lambda e: e.scalar_tensor_tensor(out=out, in0=in0, scalar=scalar, in1=in1, op0=op0, op1=op1),
                  reads=_keys(rd) + _keys(r), writes=_keys([out]) + _keys(w))

    def cp(self, out, in_, eng="dve", r=(), w=()):
        if eng == "act":
            self.p.op("act", lambda e: e.copy(out=out, in_=in_), reads=_keys([in_]) + _keys(r), writes=_keys([out]) + _keys(w))
        else:
            self.p.op(eng, lambda e: e.tensor_copy(out=out, in_=in_), reads=_keys([in_]) + _keys(r), writes=_keys([out]) + _keys(w))

    def memset(self, ap, val, eng="dve"):
        self.p.op(eng, lambda e: e.memset(ap, val), reads=[], writes=_keys([ap]))

    def red(self, out, in_, op, axis=AX.X, eng="dve"):
        self.p.op(eng, lambda e: e.tensor_reduce(out=out, in_=in_, axis=axis, op=op),
                  reads=_keys([in_]), writes=_keys([out]))

    def dma(self, out, in_, slot, r=None, w=None, eng="sp", slow=False):
        rk = _keys(r) if r is not None else _keys([in_])
        wk = _keys(w) if w is not None else _keys([out])
        if slow:
            self.p.op(eng, lambda e: e.dma_start(out=out, in_=in_, allow_slow_non_contiguous=True), reads=rk, writes=wk, dma_slot=slot)
        else:
            self.p.op(eng, lambda e: e.dma_start(out=out, in_=in_), reads=rk, writes=wk, dma_slot=slot)


class Model:
    def __init__(self, S, depth=4, parts="fm", layers=None):
        self.layers = layers
        self.S = S
        self.depth = depth
        self.parts = parts

    def build(self):
        nc = bass.Bass("TRN2", target_bir_lowering=False)
        self.nc = nc
        S = self.S
        with contextlib.ExitStack() as st:
            kb = KB(nc, st)
            self.kb = kb
            L = self.depth
            NE, NO = (L + 1) // 2, L // 2
            self.inp = {}

            def ein(name, shape):
                self.inp[name] = nc.dram_tensor(name, list(shape), F32, kind="ExternalInput").ap()
                return self.inp[name]
            ein("x", [S, D])
            ein("ffn_norm", [L, 2, D]); ein("mix_norm", [L, D])
            ein("ffn_w_gate", [L, 2, D, DFF]); ein("ffn_w_up", [L, 2, D, DFF]); ein("ffn_w_down", [L, 2, DFF, D])
            self.out = nc.dram_tensor("out", [S, D], F32, kind="ExternalOutput").ap()
            self.xT = kb.dram("xT", [D, S])
            pool = kb.dram("pool", [5136 * S * int(os.environ.get("POOLX", "1"))], BF16)

            def view(off, shape, dt):
                n = shape[0] * shape[1] * (2 if dt == F32 else 1)
                v = pool[off * S: off * S + n]
                if dt == F32:
                    v = v.bitcast(F32)
                return v.rearrange("(r c) -> r c", c=shape[1])
            self.view = view
            self.scr = dict(
                qkmT=view(0, [1024, S], BF16), qkaT=view(1024, [1024, S], BF16),
                tmv=view(2048, [S, 1024], BF16), qa32=view(3072, [512, S], F32),
                tmo=view(4096, [S, 512], F32), tmg=view(5120, [S, 8], F32),
                yT=kb.dram("yT", [1024, S], BF16), tvec=kb.dram("tvec", [4, TV_N]))
            self.consts()
            self.kmean = kb.sb("kmean", [128, 4, max(S // 256, 8)], F32)
            if os.environ.get("POISON"):
                with kb.phase("poison"):
                    pz = kb.sb("pz", [128, 4096], BF16)
                    kb.memset(pz[:], float("nan"))
                    pv = pool.rearrange("(n p c) -> n p c", p=128, c=S // 8)
                    for i in range(pv.shape[0]):
                        kb.dma(pv[i], pz[:, 0:S // 8], "pz", w=[("poison", i)])
                    yv_ = self.scr["yT"].rearrange("(n p) (a c) -> n a p c", p=128, c=min(S, 4096))
                    for i in range(yv_.shape[0]):
                        for a_ in range(yv_.shape[1]):
                            kb.dma(yv_[i, a_], pz[:, 0:min(S, 4096)], "pz", w=[("poison", "y", i, a_)])
                    xv_ = self.xT.bitcast(BF16).rearrange("(n p) (a c) -> n a p c", p=128, c=4096)
                    for i in range(xv_.shape[0]):
                        for a_ in range(xv_.shape[1]):
                            kb.dma(xv_[i, a_], pz[:], "pz", w=[("poison", "x", i, a_)])
            self.phase_in()
            for l in range(L):
                if self.layers is not None and l not in self.layers:
                    continue
                if "f" in self.parts:
                    self.ffn(l, 0)
                if l % 2 == 0:
                    self.ab_setup(l // 2)
                    if "m" in self.parts:
                        self.ab_inproj(l)
                        self.ab_mlstm(l)
                        self.ab_moba(l)
                        self.outproj(l, "ab_w_out")
                else:
                    if "m" in self.parts:
                        self.cd_setup(l // 2)
                        self.cd_all(l)
                if "f" in self.parts:
                    self.ffn(l, 1)
            self.phase_out()
            kb.p.final_wait("sp", [("out", i) for i in range(S // 512)])
            kb.p.emit()
        return nc

    def consts(self):
        kb = self.kb
        self.ones_bf = kb.sb("ones_bf", [128, 128], BF16)
        kb.memset(self.ones_bf[:], 1.0)
        self.epsc = kb.sb("epsc", [128, 1], F32)
        kb.memset(self.epsc[:], EPS)
        self.ident = kb.sb("ident", [128, 128], F32)
        identd = self.nc.dram_tensor("identd", [128, 128], F32, kind="ExternalInput").ap()
        self.inp["identd"] = identd
        kb.dma(self.ident[:], identd[:, :], "c0")
        L = self.depth
        self.gains = kb.sb("gains", [128, L * 3, 8], F32)
        for l in range(L):
            for j in range(2):
                kb.dma(self.gains[:, l * 3 + j, :], self.inp["ffn_norm"][l, j, :].rearrange("(c p) -> p c", p=128), "c0",
                       w=[("gains", l, j)], slow=True)
            kb.dma(self.gains[:, l * 3 + 2, :], self.inp["mix_norm"][l, :].rearrange("(c p) -> p c", p=128), "c0",
                   w=[("gains", l, 2)], slow=True)
        self.PS = [kb.ps("ps%d" % i) for i in range(7)]
        psb = kb.ps("psb", (128, 512), BF16)
        self.PSB = [psb[:, i * 256:(i + 1) * 256] for i in range(2)]

    def rstd_from_sumsq(self, out, ssq, scale, eps):
        kb = self.kb
        kb.act(out, ssq, AF.Sqrt, bias=self.epsc[:, 0:1] if eps == EPS else eps, scale=scale)
        kb.p.op("dve", lambda e: e.reciprocal(out=out, in_=out), reads=_keys([out]), writes=_keys([out]))

    def phase_in(self):
        with self.kb.phase("pin"):
            self._phase_in()

    def _phase_in(self):
        kb = self.kb
        S = self.S
        xin = [kb.sb("tin%d" % i, [128, 4, D], F32) for i in range(2)]
        xo = [kb.sb("tino%d" % i, [128, 8, 512], F32) for i in range(2)]
        x = self.inp["x"]
        xTv = self.xT.rearrange("(c p) t -> p c t", p=128)
        for it in range(S // 512):
            b = it % 2
            kb.dma(xin[b][:], x[it * 512:(it + 1) * 512, :].rearrange("(j p) d -> p j d", p=128), "tin%d" % b)
            for c in range(8):
                ps = self.PS[c % 2]
                for j in range(4):
                    kb.tr(ps[:, j * 128:(j + 1) * 128], xin[b][:, j, c * 128:(c + 1) * 128], self.ident[:])
                kb.cp(xo[b][:, c, :], ps[:, :], eng="act" if c % 2 else "dve")
            kb.dma(xTv[:, :, it * 512:(it + 1) * 512], xo[b][:], "tino%d" % b, w=[("xT", it)], eng="pool")

    def phase_out(self):
        with self.kb.phase("pout"):
            self._phase_out()

    def _phase_out(self):
        kb = self.kb
        S = self.S
        xin = [kb.sb("tout%d" % i, [128, 8, 512], F32) for i in range(2)]
        xo = [kb.sb("touto%d" % i, [128, 4, D], F32) for i in range(2)]
        xTv = self.xT.rearrange("(c p) t -> p c t", p=128)
        for it in range(S // 512):
            b = it % 2
            kb.dma(xin[b][:], xTv[:, :, it * 512:(it + 1) * 512], "tout%d" % b, r=[("xT", it)])
            for j in range(4):
                for h in range(2):
                    ps = self.PS[(j * 2 + h) % 2]
                    for c in range(4):
                        cc = h * 4 + c
                        kb.tr(ps[:, c * 128:(c + 1) * 128], xin[b][:, cc, j * 128:(j + 1) * 128], self.ident[:])
                    kb.cp(xo[b][:, j, h * 512:(h + 1) * 512], ps[:, :], eng="act" if h else "dve")
            kb.dma(self.out[it * 512:(it + 1) * 512, :].rearrange("(j p) d -> p j d", p=128), xo[b][:], "touto%d" % b,
                   w=[("out", it)], eng="pool")

    def ffn(self, l, j):
        with self.kb.phase("f%d%d" % (l, j)):
            self._ffn(l, j)

    def _ffn(self, l, j):
        kb = self.kb
        S = self.S
        nm = "f%d%d" % (l, j)
        NT = S // 512
        wg_in = self.inp["ffn_w_gate"][l, j]
        wu_in = self.inp["ffn_w_up"][l, j]
        wd_in = self.inp["ffn_w_down"][l, j]
        if not hasattr(self, "wgu_s"):
            self.wgu_s = kb.dram("wgu_s", [NF, 128, 2 * 8 * 128], BF16)
        wgu_s = self.wgu_s
        if True:
            self.ffn_bufs = dict(
                cin=[kb.sb("fcin%d" % i, [128, 1408], F32) for i in range(2)],
                cout=[kb.sb("fcout%d" % i, [128, 1408], BF16) for i in range(2)],
                wd=kb.sb("fwd", [128, NF, D], BF16),
                xt=[kb.sb("fx%d" % i, [128, 8, 512], F32) for i in range(2)],
                sq=kb.sb("fsq", [128, 8, 512], BF16),
                h=kb.sb("fh", [128, 8, 512], BF16),
                aT=kb.sb("faT", [128, NF, 512], BF16),
                rstd=kb.sb("frstd", [128, 512], F32),
                sg=[kb.sb("fsg%d" % i, [128, 512], F32) for i in range(2)],
                ring=[kb.sb("fring%d" % i, [128, 2, 8, 128], BF16) for i in range(4)],
                cnt=0,
            )
        fb = self.ffn_bufs
        ci = 0
        for mi, wsrc in enumerate((wg_in, wu_in)):
            for kc in range(8):
                for half in range(2):
                    b = ci % 2
                    ci += 1
                    kb.dma(fb["cin"][b][:], wsrc[kc * 128:(kc + 1) * 128, half * 1408:(half + 1) * 1408], "fcin%d" % b)
                    kb.cp(fb["cout"][b][:], fb["cin"][b][:], eng="pool" if ci % 2 else "dve")
                    dst = wgu_s[half * 11:(half + 1) * 11, :, (mi * 8 + kc) * 128:(mi * 8 + kc + 1) * 128].rearrange("f p m -> p f m")
                    kb.dma(dst, fb["cout"][b][:].rearrange("p (f m) -> p f m", m=128), "fcout%d" % b,
                           w=[("wgu", mi, kc, half)], eng="pool")
        for f in range(NF):
            b = ci % 2
            ci += 1
            kb.dma(fb["cin"][b][:, 0:D], wd_in[f * 128:(f + 1) * 128, :], "fcin%d" % b)
            kb.cp(fb["wd"][:, f, :], fb["cin"][b][:, 0:D], eng="pool" if ci % 2 else "dve")
        wgu_keys = [("wgu", mi, kc, half) for mi in range(2) for kc in range(8) for half in range(2)]
        xTv = self.xT.rearrange("(c p) t -> p c t", p=128)
        gi = l * 3 + j
        PS = self.PS
        for it in range(NT):
            xt = fb["xt"][it % 2]
            kb.dma(xt[:], xTv[:, :, it * 512:(it + 1) * 512], "fx%d" % (it % 2), r=[("xT", it)])
            for c in range(8):
                kb.act(fb["sq"][:, c, :], xt[:, c, :], AF.Square)
            for c in range(8):
                kb.mm(PS[0][:, :], self.ones_bf[:], fb["sq"][:, c, :], start=(c == 0), stop=(c == 7))
            self.rstd_from_sumsq(fb["rstd"][:], PS[0][:, :], 1.0 / D, EPS)
            for c in range(8):
                kb.stt(fb["h"][:, c, :], xt[:, c, :], self.gains[:, gi, c:c + 1], fb["rstd"][:], ALU.mult, ALU.mult,
                       r=[("gains", l, j)])
            for f in range(NF):
                rg = fb["ring"][fb["cnt"] % 4]
                slot = "fring%d" % (fb["cnt"] % 4)
                fb["cnt"] += 1
                kb.dma(rg[:].rearrange("p a k m -> p (a k m)"), wgu_s[f, :, :], slot, r=wgu_keys)
                pg = PS[1 + f % 2]
                pu = PS[3 + f % 2]
                for kc in range(8):
                    kb.mm(pg[:, :], rg[:, 0, kc, :], fb["h"][:, kc, :], start=(kc == 0), stop=(kc == 7))
                for kc in range(8):
                    kb.mm(pu[:, :], rg[:, 1, kc, :], fb["h"][:, kc, :], start=(kc == 0), stop=(kc == 7))
                sg = fb["sg"][f % 2]
                kb.act(sg[:], pg[:, :], AF.Silu)
                kb.tt(fb["aT"][:, f, :], sg[:], pu[:, :], ALU.mult)
            for dc in range(8):
                pd = PS[5 + dc % 2]
                for f in range(NF):
                    kb.mm(pd[:, :], fb["wd"][:, f, dc * 128:(dc + 1) * 128], fb["aT"][:, f, :], start=(f == 0), stop=(f == NF - 1))
                kb.stt(xt[:, dc, :], pd[:, :], 0.5, xt[:, dc, :], ALU.mult, ALU.add)
            kb.dma(xTv[:, :, it * 512:(it + 1) * 512], xt[:], "fxo%d" % (it % 2), w=[("xT", it)], eng="pool")


def t5_bucket_np(rel):
    n = np.maximum(rel, 0)
    nf = np.maximum(n, 1).astype(np.float32)
    large = 16 + (np.log(nf / np.float32(16)) / np.float32(np.log(64.0)) * np.float32(16)).astype(np.int32)
    large = np.minimum(large, 31)
    return np.where(n < 16, n, large)


TV_LO = -511
TV_N = 2048


def host_consts():
    c = {}
    c["identd"] = np.eye(128, dtype=np.float32)
    j = np.arange(128)
    c["triu"] = (j[:, None] <= j[None, :]).astype(np.float32)
    c["antiid"] = np.eye(128, dtype=np.float32)[::-1].copy()
    c["tril_s"] = (j[:, None] > j[None, :]).astype(np.float32)
    rel = np.arange(TV_N) + TV_LO
    oh = np.zeros((33, TV_N), np.float32)
    b = t5_bucket_np(rel)
    for i in range(TV_N):
        if rel[i] >= 0:
            oh[b[i], i] = 1.0
        else:
            oh[32, i] = 1.0
    c["onehot"] = oh
    return c


def _ab_setup(self, j):
    nc = self.nc
    if "ab_w_in" in self.inp:
        return
    NE = (self.depth + 1) // 2

    def ein(name, shape):
        self.inp[name] = nc.dram_tensor(name, list(shape), F32, kind="ExternalInput").ap()
    ein("ab_w_in", [NE, D, AB_IN]); ein("ab_w_out", [NE, D, D])
    ein("mlstm_conv_w", [NE, 4, 1024]); ein("mlstm_conv_b", [NE, 1024])
    ein("mlstm_b_i", [NE, 4]); ein("mlstm_b_f", [NE, 4]); ein("mlstm_h_norm", [NE, 512])
    ein("moba_q_norm", [NE, 128]); ein("moba_k_norm", [NE, 128]); ein("rel_bias", [32, 4])
    ein("triu", [128, 128]); ein("antiid", [128, 128]); ein("onehot", [33, TV_N])


Model.ab_setup = _ab_setup


def _rmsnorm_tile(self, xt, uT, gi, keyg, ps, sq, rstd):
    kb = self.kb
    for c in range(8):
        kb.act(sq[:, c, :], xt[:, c, :], AF.Square)
    for c in range(8):
        kb.mm(ps[:, :], self.ones_bf[:], sq[:, c, :], start=(c == 0), stop=(c == 7))
    self.rstd_from_sumsq(rstd[:], ps[:, :], 1.0 / D, EPS)
    for c in range(8):
        kb.stt(uT[:, c, :], xt[:, c, :], self.gains[:, gi, c:c + 1], rstd[:], ALU.mult, ALU.mult, r=[keyg])


Model.rmsnorm_tile = _rmsnorm_tile


def _load_win(self, wsrc, ncols, win, tmp):
    kb = self.kb
    W = tmp[0].shape[1]
    ci = 0
    for kc in range(8):
        c0 = 0
        while c0 < ncols:
            w = min(W, ncols - c0)
            b = ci % 2
            ci += 1
            kb.dma(tmp[b][:, 0:w], wsrc[kc * 128:(kc + 1) * 128, c0:c0 + w], "wtmp%d" % b)
            kb.cp(win[:, kc, c0:c0 + w], tmp[b][:, 0:w], eng="pool" if ci % 2 else "dve")
            c0 += w


Model.load_win = _load_win


def _ab_inproj(self, l):
    with self.kb.phase("abin%d" % l):
        self._ab_inproj_body(l)


def _ab_inproj_body(self, l):
    kb = self.kb
    S = self.S
    j = l // 2
    NT = S // 512
    PS = self.PS
    I = self.inp
    sc = self.scr
    win = kb.sb("win", [128, 8, AB_IN], BF16)
    tmp = [kb.sb("wtmp%d" % i, [128, 900], F32) for i in range(2)]
    self.load_win(I["ab_w_in"][j], AB_IN, win, tmp)
    cw = kb.sb("cw", [128, 8, 4], F32)
    cb = kb.sb("cb", [128, 8], F32)
    for k in range(4):
        kb.dma(cw[:, :, k], I["mlstm_conv_w"][j, k, :].rearrange("(c p) -> p c", p=128), "c1", slow=True)
    kb.dma(cb[:], I["mlstm_conv_b"][j].rearrange("(c p) -> p c", p=128), "c2", slow=True)
    gq = kb.sb("gq", [128, 2], F32)
    kb.dma(gq[:, 0:1], I["moba_q_norm"][j].rearrange("(p o) -> p o", o=1), "c3", slow=True)
    kb.dma(gq[:, 1:2], I["moba_k_norm"][j].rearrange("(p o) -> p o", o=1), "c4", slow=True)
    kb.ts(gq[:, 0:1], gq[:, 0:1], 128 ** -0.5, ALU.mult)
    xt_b = [kb.sb("x%d" % i, [128, 8, 512], F32) for i in range(2)]
    sq = kb.sb("sq", [128, 8, 512], BF16)
    uT = kb.sb("uT", [128, 8, 512], BF16)
    rstd = kb.sb("rstd", [128, 512], F32)
    pq = [kb.sb("pq%d" % i, [128, 515], F32) for i in range(8)]
    acc = [kb.sb("acc%d" % i, [128, 512], F32) for i in range(2)]
    qkm = [kb.sb("qkm%d" % i, [128, 8, 512], BF16) for i in range(1)] * 2
    sqa = kb.sb("sqa", [128, 512], BF16)
    rs2 = kb.sb("rs2", [128, 512], F32)
    qab = [kb.sb("qab%d" % i, [128, 8, 512], BF16) for i in range(1)] * 2
    qa32 = [kb.sb("qa32_%d" % i, [128, 4, 512], F32) for i in range(1)] * 2
    kn32 = kb.sb("kn32", [128, 512], F32)
    tmv = [kb.sb("tmv%d" % i, [128, 4, 1024], BF16) for i in range(1)] * 2
    tmo = [kb.sb("tmo%d" % i, [128, 4, 512], F32) for i in range(1)] * 2
    tmg = [kb.sb("tmg%d" % i, [128, 4, 8], F32) for i in range(1)] * 2
    xTv = self.xT.rearrange("(c p) t -> p c t", p=128)
    for c in range(8):
        kb.memset(pq[c][:, 0:3], 0.0)
    for it in range(NT):
        b = it % 2
        xt = xt_b[b]
        kb.dma(xt[:], xTv[:, :, it * 512:(it + 1) * 512], "x%d" % b, r=[("xT", it)])
        self.rmsnorm_tile(xt, uT, l * 3 + 2, ("gains", l, 2), PS[0], sq, rstd)
        for c in range(8):
            ps = PS[1 + c % 2]
            col0 = c * 128
            for kc in range(8):
                kb.mm(ps[:, :], win[:, kc, col0:col0 + 128], uT[:, kc, :], start=(kc == 0), stop=(kc == 7))
            if it > 0:
                kb.cp(pq[c][:, 0:3], pq[c][:, 512:515], eng="pool")
            kb.cp(pq[c][:, 3:515], ps[:, :], eng="act")
            a = acc[c % 2]
            kb.ts(a[:], pq[c][:, 0:512], cw[:, c, 0:1], ALU.mult, cb[:, c:c + 1], ALU.add)
            for k in range(1, 4):
                kb.stt(a[:], pq[c][:, k:k + 512], cw[:, c, k:k + 1], a[:], ALU.mult, ALU.add)
            if c < 4:
                kb.act(a[:], a[:], AF.Silu)
                kb.ts(qkm[b][:, c, :], a[:], 128 ** -0.5, ALU.mult, eng="pool")
            else:
                kb.act(qkm[b][:, c, :], a[:], AF.Silu)
        kb.dma(sc["qkmT"].rearrange("(c p) t -> p c t", p=128)[:, :, it * 512:(it + 1) * 512], qkm[b][:], "qkmo%d" % b,
               w=[("qkmT", it)], eng="pool")
        for c in range(8):
            ps = PS[3 + c % 2]
            ps2 = PS[5 + c % 2]
            col0 = 2056 + c * 128
            for kc in range(8):
                kb.mm(ps[:, :], win[:, kc, col0:col0 + 128], uT[:, kc, :], start=(kc == 0), stop=(kc == 7))
            kb.act(sqa[:], ps[:, :], AF.Square)
            kb.mm(ps2[:, :], self.ones_bf[:], sqa[:])
            self.rstd_from_sumsq(rs2[:], ps2[:, :], 1.0 / 128, EPS)
            if c < 4:
                kb.stt(qa32[b][:, c, :], ps[:, :], gq[:, 0:1], rs2[:], ALU.mult, ALU.mult)
                kb.cp(qab[b][:, c, :], qa32[b][:, c, :], eng="pool")
            else:
                kb.stt(kn32[:], ps[:, :], gq[:, 1:2], rs2[:], ALU.mult, ALU.mult)
                kb.cp(qab[b][:, c, :], kn32[:], eng="pool")
                kb.red(self.kmean[:, c - 4, 2 * it:2 * it + 2], kn32[:].rearrange("p (n k) -> p n k", k=256), ALU.add)
        kb.dma(sc["qkaT"].rearrange("(c p) t -> p c t", p=128)[:, :, it * 512:(it + 1) * 512], qab[b][:], "qabo%d" % b,
               w=[("qkaT", it)], eng="pool")
        kb.dma(sc["qa32"].rearrange("(c p) t -> p c t", p=128)[:, :, it * 512:(it + 1) * 512], qa32[b][:], "qa32o%d" % b,
               w=[("qa32", it)], eng="pool")
        for s in range(4):
            lt = [uT[:, kc, s * 128:(s + 1) * 128] for kc in range(8)]
            for gi_, (col0, ncol) in enumerate(((1024, 512), (1536, 512), (2048, 8), (3080, 512))):
                ps = PS[1 + (s * 4 + gi_) % 4]
                for kc in range(8):
                    kb.mm(ps[:, 0:ncol], lt[kc], win[:, kc, col0:col0 + ncol], start=(kc == 0), stop=(kc == 7))
                if gi_ == 0:
                    kb.cp(tmv[b][:, s, 0:512], ps[:, 0:512], eng="act")
                elif gi_ == 1:
                    kb.act(tmo[b][:, s, :], ps[:, 0:512], AF.Sigmoid)
                elif gi_ == 2:
                    kb.cp(tmg[b][:, s, :], ps[:, 0:8], eng="dve")
                else:
                    kb.cp(tmv[b][:, s, 512:1024], ps[:, 0:512], eng="dve")
        tsl = slice(it * 512, (it + 1) * 512)
        kb.dma(sc["tmv"][tsl, :].rearrange("(s p) c -> p s c", p=128), tmv[b][:], "tmvo%d" % b, w=[("tmv", it)], eng="pool")
        kb.dma(sc["tmo"][tsl, :].rearrange("(s p) c -> p s c", p=128), tmo[b][:], "tmoo%d" % b, w=[("tmo", it)], eng="pool")
        kb.dma(sc["tmg"][tsl, :].rearrange("(s p) c -> p s c", p=128), tmg[b][:], "tmgo%d" % b, w=[("tmg", it)], eng="pool", slow=True)
    kb.ts(self.kmean[:], self.kmean[:], 1.0 / 256, ALU.mult)


Model.ab_inproj = _ab_inproj
Model._ab_inproj_body = _ab_inproj_body


def _ab_mlstm(self, l):
    with self.kb.phase("abm%d" % l):
        self._ab_mlstm_body(l)


def _ab_mlstm_body(self, l):
    kb = self.kb
    S = self.S
    j = l // 2
    NT = S // 512
    PS = self.PS
    I = self.inp
    sc = self.scr
    triu = kb.sb("triu", [128, 128], F32)
    kb.dma(triu[:], I["triu"][:, :], "c1")
    identb = kb.sb("identb", [128, 128], BF16)
    kb.cp(identb[:], self.ident[:])
    ones32 = kb.sb("ones32", [128, 128], F32)
    kb.memset(ones32[:], 1.0)
    bif = kb.sb("bif", [128, 8], F32)
    kb.dma(bif[:, 0:4], I["mlstm_b_i"][j:j + 1, :].broadcast_to([128, 4]), "c2", slow=True)
    kb.dma(bif[:, 4:8], I["mlstm_b_f"][j:j + 1, :].broadcast_to([128, 4]), "c3", slow=True)
    hn = kb.sb("hn", [128, 512], F32)
    kb.dma(hn[:], I["mlstm_h_norm"][j:j + 1, :].broadcast_to([128, 512]), "c4", slow=True)
    C = [kb.sb("C%d" % h, [128, 129], F32) for h in range(4)]
    Cb = [kb.sb("Cb%d" % h, [128, 129], BF16) for h in range(4)]
    for h in range(4):
        kb.memset(C[h][:], 0.0)
        kb.memset(Cb[h][:], 0.0, eng="pool")
    qk = [kb.sb("qk%d" % i, [128, 8, 512], BF16) for i in range(2)]
    vm = [kb.sb("vm%d" % i, [128, 4, 1024], BF16) for i in range(2)]
    so = [kb.sb("so%d" % i, [128, 4, 512], F32) for i in range(2)]
    gt = [kb.sb("gt%d" % i, [128, 4, 8], F32) for i in range(2)]
    ipre = kb.sb("ipre", [128, 4], F32)
    lf = kb.sb("lf", [128, 4], F32)
    acol = kb.sb("acol", [128, 4], F32)
    ccol = kb.sb("ccol", [128, 4], F32)
    dec = kb.sb("dec", [128, 4], F32)
    gate = kb.sb("gate", [128, 512], F32)
    vext = kb.sb("vext", [128, 4, 129], BF16)
    va = [kb.sb("va%d" % i, [128, 129], BF16) for i in range(2)]
    sm = [kb.sb("sm%d" % i, [128, 128], BF16) for i in range(2)]
    ktok = [kb.sb("ktok%d" % i, [128, 128], BF16) for i in range(2)]
    den = kb.sb("den", [128, 4], F32)
    den2 = kb.sb("den2", [128, 4], F32)
    hh = [kb.sb("hh%d" % i, [128, 128], F32) for i in range(2)]
    junk = kb.sb("junk", [128, 128], F32)
    ss = kb.sb("ss", [128, 4], F32)
    y = kb.sb("y", [128, 512], BF16)
    yT = [kb.sb("yT%d" % i, [128, 4, 512], BF16) for i in range(2)]
    PSB = self.PSB
    for it in range(NT):
        b = it % 2
        tsl = slice(it * 512, (it + 1) * 512)
        kb.dma(qk[b][:], sc["qkmT"].rearrange("(c p) t -> p c t", p=128)[:, :, tsl], "qk%d" % b, r=[("qkmT", it)])
        kb.dma(vm[b][:], sc["tmv"][tsl, :].rearrange("(s p) c -> p s c", p=128), "vm%d" % b, r=[("tmv", it)])
        kb.dma(so[b][:], sc["tmo"][tsl, :].rearrange("(s p) c -> p s c", p=128), "so%d" % b, r=[("tmo", it)])
        kb.dma(gt[b][:], sc["tmg"][tsl, :].rearrange("(s p) c -> p s c", p=128), "gt%d" % b, r=[("tmg", it)], slow=True)
        for s in range(4):
            csl = slice(s * 128, (s + 1) * 128)
            kb.tt(ipre[:], gt[b][:, s, 0:4], bif[:, 0:4], ALU.add)
            kb.tt(lf[:], gt[b][:, s, 4:8], bif[:, 4:8], ALU.add)
            kb.act(lf[:], lf[:], AF.Exp, scale=-1.0)
            kb.act(lf[:], lf[:], AF.Ln, bias=1.0)
            kb.ts(lf[:], lf[:], -1.0, ALU.mult)
            pg = PS[0]
            kb.mm(pg[:, 0:4], triu[:], lf[:])
            kb.mm(pg[:, 8:12], ones32[:], lf[:])
            kb.act(ccol[:], pg[:, 0:4], AF.Exp)
            kb.act(dec[:], pg[:, 8:12], AF.Exp)
            kb.tt(acol[:], ipre[:], pg[:, 0:4], ALU.subtract)
            kb.act(acol[:], acol[:], AF.Exp)
            kb.cp(vext[:, :, 0:128], vm[b][:, s, 0:512].rearrange("p (h d) -> p h d", d=128), eng="pool")
            kb.memset(vext[:, :, 128:129], 1.0, eng="pool")
            kb.tt(gate[:], so[b][:, s, :], hn[:], ALU.mult, eng="pool")
            for h in range(4):
                qT = qk[b][:, h, csl]
                kT = qk[b][:, 4 + h, csl]
                pS = PS[1 + h % 2]
                kb.mm(pS[:, 0:128], kT, qT)
                kb.tt(sm[h % 2][:], pS[:, 0:128], triu[:], ALU.mult)
                kb.ts(va[h % 2][:], vext[:, h, :], acol[:, h:h + 1], ALU.mult, eng="pool")
                pN = PS[3 + h % 2]
                kb.mm(pN[:, 0:129], sm[h % 2][:], va[h % 2][:], start=True, stop=False)
                kb.mm(pN[:, 0:129], qT, Cb[h][:], start=False, stop=True)
                pT = PSB[h % 2]
                kb.tr(pT[:, 0:128], kT, identb[:])
                kb.cp(ktok[h % 2][:], pT[:, 0:128], eng="act")
                pC = PS[5 + h % 2]
                kb.mm(pC[:, 0:129], ktok[h % 2][:], va[h % 2][:])
                kb.tt(C[h][:], C[h][:], pC[:, 0:129], ALU.add)
                kb.ts(C[h][:], C[h][:], dec[:, h:h + 1], ALU.mult)
                kb.cp(Cb[h][:], C[h][:], eng="act")
                kb.ts(den[:, h:h + 1], pN[:, 128:129], ccol[:, h:h + 1], ALU.mult)
                kb.ts(den2[:, h:h + 1], den[:, h:h + 1], -1.0, ALU.mult)
                kb.tt(den[:, h:h + 1], den[:, h:h + 1], den2[:, h:h + 1], ALU.max)
                kb.ts(den[:, h:h + 1], den[:, h:h + 1], 1.0, ALU.max)
                kb.p.op("dve", (lambda o: (lambda e: e.reciprocal(out=o, in_=o)))(den[:, h:h + 1]), reads=_keys([den]), writes=_keys([den]))
                kb.tt(den[:, h:h + 1], den[:, h:h + 1], ccol[:, h:h + 1], ALU.mult)
                kb.act(hh[h % 2][:], pN[:, 0:128], AF.Copy, scale=den[:, h:h + 1])
                kb.act(junk[:], hh[h % 2][:], AF.Square, accum_out=ss[:, h:h + 1])
                self.rstd_from_sumsq(ss[:, h:h + 1], ss[:, h:h + 1], 1.0 / 128, EPS)
                kb.stt(y[:, h * 128:(h + 1) * 128], hh[h % 2][:], ss[:, h:h + 1], gate[:, h * 128:(h + 1) * 128], ALU.mult, ALU.mult)
            for h in range(4):
                pT = PSB[h % 2]
                kb.tr(pT[:, 0:128], y[:, h * 128:(h + 1) * 128], identb[:])
                kb.cp(yT[b][:, h, csl], pT[:, 0:128], eng="act" if h % 2 else "dve")
        kb.dma(sc["yT"].rearrange("(c p) t -> p c t", p=128)[:, 0:4, tsl], yT[b][:], "yTo%d" % b, w=[("yTm", it)], eng="pool")


Model.ab_mlstm = _ab_mlstm
Model._ab_mlstm_body = _ab_mlstm_body


def _rstd_lnexp(self, out, ssq, scale, eps_ap):
    kb = self.kb
    kb.act(out, ssq, AF.Ln, bias=eps_ap, scale=scale)
    kb.act(out, out, AF.Exp, scale=-0.5)


Model.rstd_lnexp = _rstd_lnexp


def _ab_moba(self, l):
    with self.kb.phase("aba%d" % l):
        self._ab_moba_body(l)


def _ab_moba_body(self, l):
    kb = self.kb
    S = self.S
    NT = S // 512
    NB = S // 256
    PS = self.PS
    PSB = self.PSB
    I = self.inp
    sc = self.scr
    rb = kb.sb("rb", [128, 128], F32)
    kb.memset(rb[:], 0.0)
    kb.memset(rb[32:64, 0:4], -30000.0)
    kb.dma(rb[0:32, 0:4], I["rel_bias"][:, :], "c1", slow=True)
    oh = kb.sb("oh", [128, TV_N], F32)
    kb.memset(oh[:], 0.0)
    kb.dma(oh[0:33, :], I["onehot"][:, :], "c2")
    tv = kb.sb("tv", [4, TV_N], F32)
    for q in range(TV_N // 512):
        kb.mm(PS[q][:, :], rb[:], oh[:, q * 512:(q + 1) * 512])
        kb.cp(tv[:, q * 512:(q + 1) * 512], PS[q][0:4, :])
    kb.dma(sc["tvec"][:, :], tv[:], "c3", w=["tvec"])
    b31 = kb.sb("b31", [128, 4], F32)
    kb.dma(b31[:], I["rel_bias"][31:32, :].broadcast_to([128, 4]), "c4", slow=True)
    antib = kb.sb("antib", [128, 128], BF16)
    anti32 = kb.sb("anti32", [128, 128], F32)
    kb.dma(anti32[:], I["antiid"][:, :], "c5")
    kb.cp(antib[:], anti32[:])
    ident_b = kb.sb("identb", [128, 128], BF16)
    kb.cp(ident_b[:], self.ident[:])
    sel = kb.sb("sel", [128, 32, 128], BF16)
    kb.memset(sel[:], 0.0)
    kb.cp(sel[0:32, :, :], self.ident[0:32, 0:32].unsqueeze(2).broadcast_to([32, 32, 128]))
    deltas = list(range(-384, 897, 128))
    toep = kb.sb("toep", [128, len(deltas), 512], BF16)
    ttmp = [kb.sb("ttmp%d" % i, [128, 512], F32) for i in range(2)]
    kT = kb.sb("kT", [128, S], BF16)
    V = kb.sb("V", [128, S // 128, 128], BF16)
    qT = [kb.sb("qT%d" % i, [128, 512], BF16) for i in range(2)]
    q32 = [kb.sb("q32_%d" % i, [128, 512], F32) for i in range(2)]
    g = kb.sb("g", [128, 32], F32)
    m8 = kb.sb("m8", [128, 8], F32)
    negm = kb.sb("negm", [128, 128], F32)
    kb.memset(negm[:], 0.0)
    negT = kb.sb("negT", [128, 512], BF16)
    kb.memset(negT[:], 0.0)
    pT = [kb.sb("pT%d" % i, [128, 512], BF16) for i in range(3)]
    rec = kb.sb("rec", [128, 512], F32)
    yo = [kb.sb("yo%d" % i, [128, 512], BF16) for i in range(2)]
    tvh = sc["tvec"].tensor
    for h in range(4):
        for di, dl in enumerate(deltas):
            off = h * TV_N + (dl - 127 - TV_LO)
            src = bass.AP(tensor=tvh, offset=off, ap=[[1, 128], [1, 512]])
            kb.dma(ttmp[di % 2][:], src, "ttmp%d" % (di % 2), r=["tvec"])
            kb.cp(toep[:, di, :], ttmp[di % 2][:], eng="pool" if di % 2 else "dve")
        kb.dma(kT[:], sc["qkaT"][(4 + h) * 128:(5 + h) * 128, :], "kT", r=[("qkaT", i) for i in range(NT)])
        kb.dma(V[:], sc["tmv"][:, 512 + h * 128:512 + (h + 1) * 128].rearrange("(n p) d -> p n d", p=128), "V",
               r=[("tmv", i) for i in range(NT)])
        for it in range(NT):
            b = it % 2
            tsl = slice(it * 512, (it + 1) * 512)
            kb.dma(qT[b][:], sc["qkaT"][h * 128:(h + 1) * 128, tsl], "qT%d" % b, r=[("qkaT", it)])
            kb.dma(q32[b][:], sc["qa32"][h * 128:(h + 1) * 128, tsl], "q32%d" % b, r=[("qa32", it)])
            for s in range(4):
                own = 2 * it + s // 2
                kb.memset(negm[:, 0:32], -30000.0)
                if own > 0:
                    kb.mm(PS[0][:, 0:NB], q32[b][:, s * 128:(s + 1) * 128], self.kmean[:, h, 0:NB], r=["kmean"])
                    kb.memset(g[:], -1e30)
                    kb.cp(g[:, 0:own], PS[0][:, 0:own])
                    kb.p.op("dve", (lambda o, i_: (lambda e: e.max(out=o, in_=i_)))(m8[:], g[:]), reads=_keys([g]), writes=_keys([m8]))
                    kb.ts(negm[:, 0:own], g[:, 0:own], m8[:, 2:3], ALU.is_ge, 30000.0, ALU.mult)
                    kb.ts(negm[:, 0:own], negm[:, 0:own], -30000.0, ALU.add)
                kb.memset(negm[:, own:own + 1], 0.0)
                kb.tr(PS[1][:, 0:128], negm[:], self.ident[:])
                kb.cp(negT[0:32, s * 128:(s + 1) * 128], PS[1][0:32, 0:128])
            nkt = 4 * (it + 1)
            pO = PS[5]
            pL = PS[6]
            for kt in range(nkt):
                pS = PS[2 + kt % 3]
                dl = it * 512 - kt * 128
                near = dl <= 896
                kb.mm(pS[:, :], kT[:, kt * 128:(kt + 1) * 128], qT[b][:], start=True, stop=False)
                kb.mm(pS[:, :], sel[:, kt // 2, :], negT[:], start=False, stop=not near)
                if near:
                    kb.mm(pS[:, :], antib[:], toep[:, deltas.index(dl), :], start=False, stop=True)
                    kb.act(pT[kt % 3][:], pS[:, :], AF.Exp)
                else:
                    kb.act(pT[kt % 3][:], pS[:, :], AF.Exp, bias=b31[:, h:h + 1])
                kb.mm(pO[:, :], V[:, kt, :], pT[kt % 3][:], start=(kt == 0), stop=(kt == nkt - 1))
                kb.mm(pL[:, :], self.ones_bf[:], pT[kt % 3][:], start=(kt == 0), stop=(kt == nkt - 1))
            kb.p.op("dve", (lambda o, i_: (lambda e: e.reciprocal(out=o, in_=i_)))(rec[:], pL[:, :]), reads=_keys([pL]), writes=_keys([rec]))
            kb.tt(yo[b][:], pO[:, :], rec[:], ALU.mult)
            kb.dma(sc["yT"][512 + h * 128:512 + (h + 1) * 128, tsl], yo[b][:], "yo%d" % b, w=[("yTa", it, h)], eng="pool")


Model.ab_moba = _ab_moba
Model._ab_moba_body = _ab_moba_body


def _outproj(self, l, wname):
    with self.kb.phase("op%d" % l):
        self._outproj_body(l, wname)


def _outproj_body(self, l, wname):
    kb = self.kb
    S = self.S
    NT = S // 512
    PS = self.PS
    j = l // 2
    wo = kb.sb("wo", [128, 8, D], BF16)
    tmp = [kb.sb("wtmp%d" % i, [128, 1024], F32) for i in range(2)]
    self.load_win(self.inp[wname][j], D, wo, tmp)
    xt_b = [kb.sb("x%d" % i, [128, 8, 512], F32) for i in range(2)]
    yt_b = [kb.sb("y%d" % i, [128, 8, 512], BF16) for i in range(2)]
    xTv = self.xT.rearrange("(c p) t -> p c t", p=128)
    yTv = self.scr["yT"].rearrange("(c p) t -> p c t", p=128)
    for it in range(NT):
        b = it % 2
        tsl = slice(it * 512, (it + 1) * 512)
        kb.dma(xt_b[b][:], xTv[:, :, tsl], "x%d" % b, r=[("xT", it)])
        kb.dma(yt_b[b][:], yTv[:, :, tsl], "y%d" % b, r=[("yTm", it)] + [("yTa", it, h) for h in range(4)])
        for dc in range(8):
            ps = PS[dc % 4]
            for e_ in range(8):
                kb.mm(ps[:, :], wo[:, e_, dc * 128:(dc + 1) * 128], yt_b[b][:, e_, :], start=(e_ == 0), stop=(e_ == 7))
            kb.tt(xt_b[b][:, dc, :], xt_b[b][:, dc, :], ps[:, :], ALU.add)
        kb.dma(xTv[:, :, tsl], xt_b[b][:], "xo%d" % b, w=[("xT", it)], eng="pool")


Model.outproj = _outproj
Model._outproj_body = _outproj_body


def _cd_setup(self, j):
    nc = self.nc
    if "cd_w_in" in self.inp:
        return
    NO = max(self.depth // 2, 1)

    def ein(name, shape):
        if name not in self.inp:
            self.inp[name] = nc.dram_tensor(name, list(shape), F32, kind="ExternalInput").ap()
    ein("cd_w_in", [NO, D, CD_IN]); ein("cd_w_out", [NO, D, D])
    ein("ssm_conv_w", [NO, 4, 1024]); ein("ssm_conv_b", [NO, 1024])
    ein("ssm_dt_bias", [NO, 8]); ein("ssm_A_log", [NO, 8]); ein("ssm_D", [NO, 8]); ein("ssm_norm", [NO, 512])
    ein("rwkv_mu", [NO, 1792]); ein("rwkv_w0", [NO, 512]); ein("rwkv_w2", [NO, 64, 512])
    ein("rwkv_a0", [NO, 512]); ein("rwkv_a2", [NO, 64, 512]); ein("rwkv_g2", [NO, 128, 512])
    ein("rwkv_k_k", [NO, 512]); ein("rwkv_k_a", [NO, 512]); ein("rwkv_r_k", [NO, 8, 64])
    ein("rwkv_ln_g", [NO, 512]); ein("rwkv_ln_b", [NO, 512])
    ein("triu", [128, 128]); ein("tril_s", [128, 128])
    S = self.S
    kb = self.kb
    view = self.view
    self.scr.update(
        xbcT=view(0, [1024, S], BF16), sz=view(1024, [S, 512], BF16), gg=view(1536, [S, 512], BF16),
        rkv=view(2048, [S, 1536], F32), dtt=view(5120, [S, 8], F32),
        lw=self.out[:, 0:512], aa=self.out[:, 512:1024])


Model.cd_setup = _cd_setup


def _cd_all(self, l):
    stop = int(os.environ.get("CD_STOP", "9"))
    with self.kb.phase("cdin%d" % l):
        self._cd_inproj_body(l)
    if stop >= 2:
        with self.kb.phase("cds%d" % l):
            self._cd_ssd_body(l)
    if stop >= 3:
        with self.kb.phase("cdr%d" % l):
            self._cd_rwkv_body(l)
    if stop >= 4:
        self.outproj(l, "cd_w_out")


Model.cd_all = _cd_all


def _cd_inproj_body(self, l):
    kb = self.kb
    S = self.S
    j = l // 2
    NT = S // 512
    PS = self.PS
    I = self.inp
    sc = self.scr
    NA = 1544
    win = kb.sb("win", [128, 8, NA], BF16)
    w1 = kb.sb("w1", [128, 8, 1792], BF16)
    w2 = kb.sb("w2", [128, 8, 1792], BF16)
    tmp = [kb.sb("wtmp%d" % i, [128, 896], F32) for i in range(2)]
    self.load_win(I["cd_w_in"][j][:, 0:NA], NA, win, tmp)
    mu = kb.sb("mu", [128, 1792], F32)
    kb.dma(mu[:], I["rwkv_mu"][j:j + 1, :].broadcast_to([128, 1792]), "c1", slow=True)
    tm2 = kb.sb("tm2", [128, 896], F32)
    ci = 0
    for kc in range(8):
        for half in range(2):
            b = ci % 2
            ci += 1
            cs = slice(half * 896, (half + 1) * 896)
            kb.dma(tmp[b][:], I["cd_w_in"][j][kc * 128:(kc + 1) * 128, NA + half * 896:NA + (half + 1) * 896], "wtmp%d" % b)
            kb.tt(tm2[:], tmp[b][:], mu[:, cs], ALU.mult)
            kb.cp(w2[:, kc, cs], tm2[:], eng="pool")
            kb.tt(w1[:, kc, cs], tmp[b][:], tm2[:], ALU.subtract)
    cw = kb.sb("cw", [128, 8, 4], F32)
    cb = kb.sb("cb", [128, 8], F32)
    for k in range(4):
        kb.dma(cw[:, :, k], I["ssm_conv_w"][j, k, :].rearrange("(c p) -> p c", p=128), "c2", slow=True)
    kb.dma(cb[:], I["ssm_conv_b"][j].rearrange("(c p) -> p c", p=128), "c3", slow=True)
    dtb = kb.sb("dtb", [128, 8], F32)
    kb.dma(dtb[:], I["ssm_dt_bias"][j:j + 1, :].broadcast_to([128, 8]), "c4", slow=True)
    rep = {}
    for nm_ in ("rwkv_w0", "rwkv_a0"):
        rep[nm_] = kb.sb(nm_, [128, 512], F32)
        kb.dma(rep[nm_][:], I[nm_][j:j + 1, :].broadcast_to([128, 512]), "c5", slow=True)
    lw2 = kb.sb("lw2", [128, 512], F32)
    kb.memset(lw2[:], 0.0)
    la2 = kb.sb("la2", [128, 512], F32)
    kb.memset(la2[:], 0.0)
    lg2 = kb.sb("lg2", [128, 512], F32)
    kb.dma(lw2[0:64, :], I["rwkv_w2"][j], "c6")
    kb.dma(la2[64:128, :], I["rwkv_a2"][j], "c7")
    kb.dma(lg2[:], I["rwkv_g2"][j], "c8")
    xt = kb.sb("x", [128, 8, 512], F32)
    sq2 = [kb.sb("sq%d" % i, [128, 512], BF16) for i in range(2)]
    uT = kb.sb("uT", [128, 8, 516], BF16)
    rstd = kb.sb("rstd", [128, 512], F32)
    pq = [kb.sb("pq%d" % i, [128, 515], F32) for i in range(8)]
    acc = [kb.sb("acc%d" % i, [128, 512], F32) for i in range(2)]
    xbc = kb.sb("xbc", [128, 8, 512], BF16)
    tsz = kb.sb("tsz", [128, 4, 512], BF16)
    tdt = kb.sb("tdt", [128, 4, 8], F32)
    trkv = kb.sb("trkv", [128, 1536], F32)
    lora = kb.sb("lora", [128, 2, 512], F32)
    tl = [kb.sb("tl%d" % i, [128, 512], F32) for i in range(2)] + [kb.sb("tl2", [128, 512], BF16)]
    xTv = self.xT.rearrange("(c p) t -> p c t", p=128)
    for c in range(8):
        kb.memset(pq[c][:, 0:3], 0.0)
        kb.memset(uT[:, c, 0:4], 0.0)
    for it in range(NT):
        tsl = slice(it * 512, (it + 1) * 512)
        if it > 0:
            for c in range(8):
                kb.cp(uT[:, c, 0:4], uT[:, c, 512:516], eng="dve")
        kb.dma(xt[:], xTv[:, :, tsl], "x0", r=[("xT", it)])
        for c in range(8):
            kb.act(sq2[c % 2][:], xt[:, c, :], AF.Square)
            kb.mm(PS[0][:, :], self.ones_bf[:], sq2[c % 2][:], start=(c == 0), stop=(c == 7))
        self.rstd_from_sumsq(rstd[:], PS[0][:, :], 1.0 / D, EPS)
        for c in range(8):
            kb.stt(uT[:, c, 4:516], xt[:, c, :], self.gains[:, l * 3 + 2, c:c + 1], rstd[:], ALU.mult, ALU.mult, r=[("gains", l, 2)])
        cut = int(os.environ.get("CD_CUT", "9"))
        if it > 0 and cut <= 1:
            continue
        for c in range(8):
            ps = PS[1 + c % 2]
            col0 = 512 + c * 128
            for kc in range(8):
                kb.mm(ps[:, :], win[:, kc, col0:col0 + 128], uT[:, kc, 4:516], start=(kc == 0), stop=(kc == 7))
            if it > 0:
                kb.cp(pq[c][:, 0:3], pq[c][:, 512:515], eng="pool")
            kb.cp(pq[c][:, 3:515], ps[:, :], eng="act")
            a = acc[c % 2]
            kb.ts(a[:], pq[c][:, 0:512], cw[:, c, 0:1], ALU.mult, cb[:, c:c + 1], ALU.add)
            for k in range(1, 4):
                kb.stt(a[:], pq[c][:, k:k + 512], cw[:, c, k:k + 1], a[:], ALU.mult, ALU.add)
            kb.act(xbc[:, c, :], a[:], AF.Silu)
        kb.dma(sc["xbcT"].rearrange("(c p) t -> p c t", p=128)[:, :, tsl], xbc[:], "xbco", w=[("xbcT", it)], eng="pool")
        if it > 0 and cut <= 2:
            continue
        for c in range(2):
            ps = PS[3 + c]
            cs = slice(1536 + c * 128, 1536 + (c + 1) * 128)
            for kc in range(8):
                kb.mm(ps[:, :], w1[:, kc, cs], uT[:, kc, 4:516], start=(kc == 0), stop=False)
            for kc in range(8):
                kb.mm(ps[:, :], w2[:, kc, cs], uT[:, kc, 3:515], start=False, stop=(kc == 7))
        kb.act(lora[0:64, 0, :], PS[3][0:64, :], AF.Tanh)
        kb.cp(lora[64:128, 0, :], PS[3][64:128, :], eng="dve")
        kb.act(lora[:, 1, :], PS[4][:, :], AF.Sigmoid)
        if it > 0 and cut <= 3:
            continue
        for s in range(4):
            ssl = slice(s * 128, (s + 1) * 128)
            lt = [uT[:, kc, 4 + s * 128:4 + (s + 1) * 128] for kc in range(8)]
            lp = [uT[:, kc, 3 + s * 128:3 + (s + 1) * 128] for kc in range(8)]
            ps = PS[1]
            for kc in range(8):
                kb.mm(ps[:, :], lt[kc], win[:, kc, 0:512], start=(kc == 0), stop=(kc == 7))
            kb.act(tsz[:, s, :], ps[:, :], AF.Silu)
            ps = PS[2]
            for kc in range(8):
                kb.mm(ps[:, 0:8], lt[kc], win[:, kc, 1536:1544], start=(kc == 0), stop=(kc == 7))
            kb.tt(tdt[:, s, :], ps[:, 0:8], dtb[:], ALU.add)
            kb.act(tdt[:, s, :], tdt[:, s, :], AF.Exp)
            kb.act(tdt[:, s, :], tdt[:, s, :], AF.Ln, bias=1.0)
            if it > 0 and cut <= 4:
                continue
            for q in range(3):
                ps = PS[5 + q % 2]
                cs = slice(q * 512, (q + 1) * 512)
                for kc in range(8):
                    kb.mm(ps[:, :], lt[kc], w1[:, kc, cs], start=(kc == 0), stop=False)
                for kc in range(8):
                    kb.mm(ps[:, :], lp[kc], w2[:, kc, cs], start=False, stop=(kc == 7))
                kb.cp(trkv[:, cs], ps[:, :], eng="act" if q % 2 else "dve")
            kb.dma(sc["rkv"][it * 512 + s * 128:it * 512 + (s + 1) * 128, :], trkv[:], "rkvo", w=[("rkv", it, s)], eng="sp")
            if it > 0 and cut <= 5:
                continue
            kb.mm(PS[1][:, :], lora[:, 0, ssl], lw2[:])
            kb.tt(tl[0][:], PS[1][:, :], rep["rwkv_w0"][:], ALU.add)
            kb.act(tl[0][:], tl[0][:], AF.Sigmoid)
            kb.ts(tl[0][:], tl[0][:], -float(np.exp(-0.5)), ALU.mult)
            kb.dma(sc["lw"][it * 512 + s * 128:it * 512 + (s + 1) * 128, :], tl[0][:], "lwo", w=[("lw", it, s)], eng="sp")
            kb.mm(PS[2][:, :], lora[:, 0, ssl], la2[:])
            kb.tt(tl[1][:], PS[2][:, :], rep["rwkv_a0"][:], ALU.add)
            kb.act(tl[1][:], tl[1][:], AF.Sigmoid)
            kb.dma(sc["aa"][it * 512 + s * 128:it * 512 + (s + 1) * 128, :], tl[1][:], "aao", w=[("aa", it, s)], eng="sp")
            kb.mm(PS[5][:, :], lora[:, 1, ssl], lg2[:])
            kb.cp(tl[2][:], PS[5][:, :], eng="act")
            kb.dma(sc["gg"][it * 512 + s * 128:it * 512 + (s + 1) * 128, :], tl[2][:], "ggo", w=[("gg", it, s)], eng="sp")
        kb.dma(sc["sz"][tsl, :].rearrange("(s p) c -> p s c", p=128), tsz[:], "szo", w=[("sz", it)], eng="pool")
        kb.dma(sc["dtt"][tsl, :].rearrange("(s p) c -> p s c", p=128), tdt[:], "dtto", w=[("dtt", it)], eng="pool", slow=True)


Model._cd_inproj_body = _cd_inproj_body


def _cd_ssd_body(self, l):
    kb = self.kb
    S = self.S
    j = l // 2
    NT = S // 512
    PS = self.PS
    PSB = self.PSB
    I = self.inp
    sc = self.scr
    triu = kb.sb("triu", [128, 128], F32)
    trils = kb.sb("trils", [128, 128], F32)
    kb.dma(triu[:], I["triu"][:, :], "c1")
    kb.dma(trils[:], I["tril_s"][:, :], "c2")
    identb = kb.sb("identb", [128, 128], BF16)
    kb.cp(identb[:], self.ident[:])
    ones32 = kb.sb("ones32", [128, 128], F32)
    kb.memset(ones32[:], 1.0)
    Arep = kb.sb("Arep", [128, 8], F32)
    kb.dma(Arep[:], I["ssm_A_log"][j:j + 1, :].broadcast_to([128, 8]), "c3", slow=True)
    kb.act(Arep[:], Arep[:], AF.Exp)
    kb.ts(Arep[:], Arep[:], -1.0, ALU.mult)
    Drep = kb.sb("Drep", [128, 8], F32)
    kb.dma(Drep[:], I["ssm_D"][j:j + 1, :].broadcast_to([128, 8]), "c4", slow=True)
    nrep = kb.sb("nrep", [128, 512], F32)
    kb.dma(nrep[:], I["ssm_norm"][j:j + 1, :].broadcast_to([128, 512]), "c5", slow=True)
    H = [kb.sb("H%d" % g, [128, 256], F32) for g in range(2)]
    Hb = [kb.sb("Hb%d" % g, [128, 256], BF16) for g in range(2)]
    for g in range(2):
        kb.memset(H[g][:], 0.0)
        kb.memset(Hb[g][:], 0.0, eng="pool")
    xbc = [kb.sb("xbc%d" % i, [128, 8, 512], BF16) for i in range(2)]
    sz = [kb.sb("sz%d" % i, [128, 4, 512], BF16) for i in range(2)]
    dtt = [kb.sb("dtt%d" % i, [128, 4, 8], F32) for i in range(2)]
    a = kb.sb("a", [128, 8], F32)
    R = kb.sb("R", [128, 8, 128], F32)
    Lm = kb.sb("Lm", [128, 8, 128], F32)
    W = kb.sb("W", [128, 8, 128], BF16)
    CBm = [kb.sb("CBm%d" % g, [128, 128], F32) for g in range(2)]
    xtok = kb.sb("xtok", [128, 512], BF16)
    btok = kb.sb("btok", [128, 2, 128], BF16)
    et = kb.sb("et", [128, 8], F32)
    decs = kb.sb("decs", [128, 8], F32)
    decL = kb.sb("decL", [128, 8], F32)
    xdt = kb.sb("xdt", [128, 8, 64], BF16)
    xdd = kb.sb("xdd", [128, 8, 64], BF16)
    t1 = kb.sb("t1", [128, 8, 64], F32)
    yv = kb.sb("yv", [128, 8, 64], F32)
    junk = kb.sb("junk", [128, 256], F32)
    ss = kb.sb("ss", [128, 2], F32)
    yb = kb.sb("yb", [128, 512], F32)
    yT = [kb.sb("yT%d" % i, [128, 4, 512], BF16) for i in range(2)]
    for it in range(int(os.environ.get("SSD_T0", "0")), min(NT, int(os.environ.get("SSD_NT", "999")))):
        b = 0
        tsl = slice(it * 512, (it + 1) * 512)
        kb.dma(xbc[b][:], sc["xbcT"].rearrange("(c p) t -> p c t", p=128)[:, :, tsl], "xbc%d" % b, r=[("xbcT", it)])
        kb.dma(sz[b][:], sc["sz"][tsl, :].rearrange("(s p) c -> p s c", p=128), "sz%d" % b, r=[("sz", it)])
        kb.dma(dtt[b][:], sc["dtt"][tsl, :].rearrange("(s p) c -> p s c", p=128), "dtt%d" % b, r=[("dtt", it)], slow=True)
        scut = int(os.environ.get("SSD_CUT", "99"))
        for s in range(4):
            csl = slice(s * 128, (s + 1) * 128)
            dt = dtt[b][:, s, :]
            if it > 0 and scut <= 0:
                continue
            kb.tt(a[:], dt, Arep[:], ALU.mult)
            kb.tt(R[:], triu[:].unsqueeze(1).broadcast_to([128, 8, 128]), a[:].unsqueeze(2).broadcast_to([128, 8, 128]), ALU.mult)
            for hh in range(2):
                kb.mm(PS[hh][:, :], trils[:], R[:, hh * 4:(hh + 1) * 4, :].rearrange("p h t -> p (h t)"))
                kb.act(Lm[:, hh * 4:(hh + 1) * 4, :].rearrange("p h t -> p (h t)"), PS[hh][:, :], AF.Exp)
            if it > 0 and scut <= 1:
                continue
            kb.mm(PS[2][:, 0:8], triu[:], a[:])
            kb.mm(PS[2][:, 16:24], ones32[:], a[:])
            kb.act(et[:], PS[2][:, 0:8], AF.Exp)
            kb.act(decL[:], PS[2][:, 16:24], AF.Exp)
            kb.cp(decs[:], PS[2][:, 0:8])
            kb.tt(decs[:], PS[2][:, 16:24], decs[:], ALU.subtract)
            kb.act(decs[:], decs[:], AF.Exp)
            if it > 0 and scut <= 2:
                continue
            for g in range(2):
                kb.mm(PS[3][:, g * 128:(g + 1) * 128], xbc[b][:, 4 + g, csl], xbc[b][:, 6 + g, csl])
                kb.tt(CBm[g][:], PS[3][:, g * 128:(g + 1) * 128], triu[:], ALU.mult)
                kb.tt(W[:, g * 4:(g + 1) * 4, :], Lm[:, g * 4:(g + 1) * 4, :], CBm[g][:].unsqueeze(1).broadcast_to([128, 4, 128]), ALU.mult)
            if it > 0 and scut <= 3:
                continue
            for c in range(4):
                kb.tr(PSB[0][:, 0:128], xbc[b][:, c, csl], identb[:])
                kb.cp(xtok[:, c * 128:(c + 1) * 128], PSB[0][:, 0:128], eng="act")
            for g in range(2):
                kb.tr(PSB[1][:, 0:128], xbc[b][:, 4 + g, csl], identb[:])
                kb.cp(btok[:, g, :], PSB[1][:, 0:128], eng="act")
            if it > 0 and scut <= 4:
                continue
            xv = xtok[:].rearrange("p (h d) -> p h d", d=64)
            kb.tt(xdt[:], xv, dt.unsqueeze(2).broadcast_to([128, 8, 64]), ALU.mult)
            kb.tt(xdd[:], xdt[:], decs[:].unsqueeze(2).broadcast_to([128, 8, 64]), ALU.mult)
            for h in range(8):
                kb.mm(PS[4][:, h * 64:(h + 1) * 64], W[:, h, :], xdt[:, h, :])
            if it > 0 and scut <= 5:
                continue
            for g in range(2):
                kb.mm(PS[5][:, g * 256:(g + 1) * 256], xbc[b][:, 6 + g, csl], Hb[g][:])
            kb.tt(t1[:], PS[5][:, :].rearrange("p (h d) -> p h d", d=64), et[:].unsqueeze(2).broadcast_to([128, 8, 64]), ALU.mult)
            kb.tt(yv[:], t1[:], PS[4][:, :].rearrange("p (h d) -> p h d", d=64), ALU.add)
            if it > 0 and scut <= 6:
                continue
            for g in range(2):
                kb.mm(PS[6][:, g * 256:(g + 1) * 256], btok[:, g, :], xdd[:, g * 4:(g + 1) * 4, :].rearrange("p h d -> p (h d)"))
                Hv = H[g][:].rearrange("p (h d) -> p h d", d=64)
                kb.tt(Hv, Hv, decL[:, g * 4:(g + 1) * 4].unsqueeze(2).broadcast_to([128, 4, 64]), ALU.mult)
                kb.tt(H[g][:], H[g][:], PS[6][:, g * 256:(g + 1) * 256], ALU.add)
                kb.cp(Hb[g][:], H[g][:], eng="act")
            if it > 0 and scut <= 7:
                continue
            kb.tt(t1[:], xv, Drep[:].unsqueeze(2).broadcast_to([128, 8, 64]), ALU.mult)
            kb.tt(yv[:], yv[:], t1[:], ALU.add)
            yf = yv[:].rearrange("p h d -> p (h d)")
            kb.tt(yf, yf, sz[b][:, s, :], ALU.mult)
            for g in range(2):
                kb.act(junk[:], yf[:, g * 256:(g + 1) * 256], AF.Square, accum_out=ss[:, g:g + 1])
            self.rstd_lnexp(ss[:], ss[:], 1.0 / 256, self.epsc[:, 0:1])
            for g in range(2):
                kb.stt(yb[:, g * 256:(g + 1) * 256], yf[:, g * 256:(g + 1) * 256], ss[:, g:g + 1], nrep[:, g * 256:(g + 1) * 256], ALU.mult, ALU.mult)
            if it > 0 and scut <= 8:
                continue
            for c in range(4):
                kb.tr(PS[3][:, c * 128:(c + 1) * 128], yb[:, c * 128:(c + 1) * 128], self.ident[:])
            kb.cp(yT[b][:, :, csl], PS[3][:, :].rearrange("p (c t) -> p c t", t=128), eng="act")
        kb.dma(sc["yT"].rearrange("(c p) t -> p c t", p=128)[:, 0:4, tsl], yT[b][:], "yTo%d" % b, w=[("yTm", it)], eng="pool")


Model._cd_ssd_body = _cd_ssd_body


def _cd_rwkv_body(self, l):
    kb = self.kb
    S = self.S
    j = l // 2
    NT = S // 512
    PS = self.PS
    I = self.inp
    sc = self.scr
    HD = 64
    id32 = self.ident
    triu = kb.sb("triu", [128, 128], F32)
    trils = kb.sb("trils", [128, 128], F32)
    kb.dma(triu[:], I["triu"][:, :], "c1")
    kb.dma(trils[:], I["tril_s"][:, :], "c2")
    up_i = triu[0:64, 0:64]
    lo_s = trils[0:64, 0:64]
    up_s = kb.sb("up_s", [64, 64], F32)
    kb.tt(up_s[:], up_i, id32[0:64, 0:64], ALU.subtract)
    ones32 = kb.sb("ones32", [64, 1], F32)
    kb.memset(ones32[:], 1.0)
    rep = {}
    for nm_ in ("rwkv_k_k", "rwkv_k_a", "rwkv_ln_g", "rwkv_ln_b"):
        rep[nm_] = kb.sb(nm_, [64, 512], F32)
        kb.dma(rep[nm_][:], I[nm_][j:j + 1, :].broadcast_to([64, 512]), "c3", slow=True)
    rep["rwkv_r_k"] = kb.sb("rwkv_r_k", [64, 512], F32)
    kb.dma(rep["rwkv_r_k"][:], I["rwkv_r_k"][j:j + 1].rearrange("o h d -> o (h d)").broadcast_to([64, 512]), "c4", slow=True)
    eps24 = kb.sb("eps24", [64, 1], F32)
    kb.memset(eps24[:], 1e-24)
    epsln = kb.sb("epsln", [64, 1], F32)
    kb.memset(epsln[:], 64e-5)
    Sst = kb.sb("Sst", [64, 8, 64], F32)
    kb.memset(Sst[:], 0.0)

    def T(name, shape=(64, 512)):
        return kb.sb(name, list(shape), F32)
    rkv = [T("rkv%d" % i, (64, 1536)) for i in range(2)]
    lw = [T("lw%d" % i) for i in range(2)]
    aa = [T("aa%d" % i) for i in range(2)]
    gg = [kb.sb("gg%d" % i, [64, 512], BF16) for i in range(2)]
    kk = T("kk"); kp = T("kp"); t1 = T("t1"); t2 = T("t2")
    Ep = T("Ep"); Em = T("Em"); Epx = T("Epx")
    At = T("At"); Rt = T("Rt"); Bt = T("Bt"); Kt = T("Kt")
    ss = T("ss", (64, 8)); bon = T("bon", (64, 8)); PL = T("PL", (64, 8))
    fm = {n: T("fm" + n) for n in ("A", "R", "B", "K")}
    X = T("X"); XT = T("XT"); Pm = T("Pm")
    Mrb = T("Mrb"); Mak = T("Mak"); Mrk = T("Mrk")
    rhsu = T("rhsu"); U = T("U"); y = T("y"); yc = T("yc")
    mean = T("mean", (64, 8))
    yo = T("yo")
    yT = [kb.sb("yT%d" % i, [64, 8, 512], BF16) for i in range(2)]
    NCH = S // 64
    b3 = lambda ap: ap.rearrange("p (h d) -> p h d", d=64)
    bc = lambda ap: ap.unsqueeze(2).broadcast_to([64, 8, 64])
    mb = lambda m: m.unsqueeze(1).broadcast_to([64, 8, 64])
    for ch in range(NCH):
        b = ch % 2
        it, s8 = ch // 8, ch % 8
        rows = slice(ch * 64, (ch + 1) * 64)
        rk = [("rkv", ch // 8, (ch % 8) // 2)]
        kb.dma(rkv[b][:], sc["rkv"][rows, :], "rkv%d" % b, r=[("rkv", it, s8 // 2)])
        kb.dma(lw[b][:], sc["lw"][rows, :], "lw%d" % b, r=[("lw", it, s8 // 2)])
        kb.dma(aa[b][:], sc["aa"][rows, :], "aa%d" % b, r=[("aa", it, s8 // 2)])
        kb.dma(gg[b][:], sc["gg"][rows, :], "gg%d" % b, r=[("gg", it, s8 // 2)])
        r_ = rkv[b][:, 0:512]; k_ = rkv[b][:, 512:1024]; v_ = rkv[b][:, 1024:1536]
        a_ = aa[b][:]
        kb.tt(kk[:], k_, rep["rwkv_k_k"][:], ALU.mult)
        kb.tt(t1[:], kk[:], kk[:], ALU.mult, eng="pool")
        kb.red(ss[:], b3(t1[:]), ALU.add)
        self.rstd_lnexp(ss[:], ss[:], 1.0, eps24[:, 0:1])
        kb.tt(b3(kk[:]), b3(kk[:]), bc(ss[:]), ALU.mult)
        kb.stt(t2[:], a_, -1.0, rep["rwkv_k_a"][:], ALU.add, ALU.mult)
        kb.ts(t2[:], t2[:], 1.0, ALU.add)
        kb.tt(kp[:], k_, t2[:], ALU.mult)
        kb.mm(PS[0][0:64, :], up_i, lw[b][:])
        kb.act(Ep[:], PS[0][0:64, :], AF.Exp)
        kb.act(Em[:], PS[0][0:64, :], AF.Exp, scale=-1.0)
        kb.tt(Epx[:], PS[0][0:64, :], lw[b][:], ALU.subtract)
        kb.act(Epx[:], Epx[:], AF.Exp)
        kb.stt(At[:], kk[:], -1.0, Epx[:], ALU.mult, ALU.mult)
        kb.tt(Rt[:], r_, Ep[:], ALU.mult, eng="pool")
        kb.tt(t1[:], kk[:], a_, ALU.mult, eng="pool")
        kb.tt(Bt[:], t1[:], Em[:], ALU.mult)
        kb.tt(Kt[:], kp[:], Em[:], ALU.mult, eng="pool")
        kb.tt(t2[:], r_, kp[:], ALU.mult, eng="pool")
        kb.tt(t2[:], t2[:], rep["rwkv_r_k"][:], ALU.mult, eng="pool")
        kb.red(bon[:], b3(t2[:]), ALU.add)
        for h in range(8):
            kb.mm(PS[0][0:64, h:h + 1], lw[b][:, h * 64:(h + 1) * 64], ones32[:])
        kb.act(PL[:], PS[0][0:64, 0:8], AF.Exp)
        for qi, (nm_, src) in enumerate((("A", At), ("R", Rt), ("B", Bt), ("K", Kt))):
            ps = PS[1 + qi % 2]
            for h in range(8):
                kb.tr(ps[0:64, h * 64:(h + 1) * 64], src[:, h * 64:(h + 1) * 64], id32[0:64, 0:64])
            kb.cp(fm[nm_][:], ps[0:64, :], eng="act" if qi % 2 else "dve")
        fA, fR, fB, fK = (lambda n: (lambda h: fm[n][:, h * 64:(h + 1) * 64])) ("A"), None, None, None
        fA = lambda h: fm["A"][:, h * 64:(h + 1) * 64]
        fR = lambda h: fm["R"][:, h * 64:(h + 1) * 64]
        fB = lambda h: fm["B"][:, h * 64:(h + 1) * 64]
        fK = lambda h: fm["K"][:, h * 64:(h + 1) * 64]
        hs = lambda t_, h: t_[:, h * 64:(h + 1) * 64]
        for h in range(8):
            c = slice(h * 64, (h + 1) * 64)
            kb.mm(PS[0][0:64, c], fA(h), fB(h))
            kb.mm(PS[1][0:64, c], fB(h), fA(h))
            kb.mm(PS[2][0:64, c], fB(h), fR(h))
            kb.mm(PS[3][0:64, c], fK(h), fA(h))
            kb.mm(PS[4][0:64, c], fK(h), fR(h))
        kb.tt(b3(X[:]), b3(PS[0][0:64, :]), mb(lo_s), ALU.mult)
        kb.tt(b3(XT[:]), b3(PS[1][0:64, :]), mb(up_s[:]), ALU.mult)
        kb.tt(b3(Mrb[:]), b3(PS[2][0:64, :]), mb(up_i), ALU.mult)
        kb.tt(b3(Mak[:]), b3(PS[3][0:64, :]), mb(up_s[:]), ALU.mult)
        kb.tt(b3(Mrk[:]), b3(PS[4][0:64, :]), mb(up_i), ALU.mult)
        kb.tt(b3(Pm[:]), b3(XT[:]), mb(id32[0:64, 0:64]), ALU.add)
        for lev in range(5):
            for h in range(8):
                c = slice(h * 64, (h + 1) * 64)
                kb.mm(PS[5][0:64, c], hs(X, h), hs(XT, h))
                if lev < 4:
                    kb.mm(PS[6][0:64, c], hs(XT, h), hs(X, h))
            kb.cp(XT[:], PS[5][0:64, :], eng="act")
            if lev < 4:
                kb.cp(X[:], PS[6][0:64, :], eng="dve")
            else:
                pass
            if lev < 4:
                for h in range(8):
                    c = slice(h * 64, (h + 1) * 64)
                    kb.mm(PS[0][0:64, c], hs(X, h), hs(Pm, h))
            else:
                for h in range(8):
                    c = slice(h * 64, (h + 1) * 64)
                    kb.tr(PS[1][0:64, c], hs(XT, h), id32[0:64, 0:64])
                kb.cp(X[:], PS[1][0:64, :], eng="dve")
                for h in range(8):
                    c = slice(h * 64, (h + 1) * 64)
                    kb.mm(PS[0][0:64, c], hs(X, h), hs(Pm, h))
            kb.tt(Pm[:], Pm[:], PS[0][0:64, :], ALU.add)
        for h in range(8):
            c = slice(h * 64, (h + 1) * 64)
            kb.mm(PS[2][0:64, c], fA(h), Sst[:, h, :], start=True, stop=False)
            kb.mm(PS[2][0:64, c], hs(Mak, h), v_[:, c], start=False, stop=True)
        kb.cp(rhsu[:], PS[2][0:64, :], eng="act")
        for h in range(8):
            c = slice(h * 64, (h + 1) * 64)
            kb.mm(PS[3][0:64, c], hs(Pm, h), hs(rhsu, h))
        kb.cp(U[:], PS[3][0:64, :], eng="dve")
        for h in range(8):
            c = slice(h * 64, (h + 1) * 64)
            kb.mm(PS[4][0:64, c], fR(h), Sst[:, h, :], start=True, stop=False)
            kb.mm(PS[4][0:64, c], hs(Mrb, h), hs(U, h), start=False, stop=False)
            kb.mm(PS[4][0:64, c], hs(Mrk, h), v_[:, c], start=False, stop=True)
        kb.cp(y[:], PS[4][0:64, :], eng="act")
        for h in range(8):
            c = slice(h * 64, (h + 1) * 64)
            kb.mm(PS[5][0:64, c], hs(Bt, h), hs(U, h), start=True, stop=False)
            kb.mm(PS[5][0:64, c], hs(Kt, h), v_[:, c], start=False, stop=True)
        Sf = Sst[:].rearrange("p h d -> p (h d)")
        kb.tt(Sf, Sf, PS[5][0:64, :], ALU.add)
        kb.tt(Sst[:], Sst[:], bc(PL[:]), ALU.mult)
        kb.red(mean[:], b3(y[:]), ALU.add)
        kb.ts(mean[:], mean[:], 1.0 / 64, ALU.mult)
        kb.tt(b3(yc[:]), b3(y[:]), bc(mean[:]), ALU.subtract)
        kb.tt(t1[:], yc[:], yc[:], ALU.mult, eng="pool")
        kb.red(ss[:], b3(t1[:]), ALU.add)
        self.rstd_lnexp(ss[:], ss[:], 1.0 / 64, epsln[:, 0:1])
        kb.tt(b3(yc[:]), b3(yc[:]), bc(ss[:]), ALU.mult)
        kb.tt(yc[:], yc[:], rep["rwkv_ln_g"][:], ALU.mult)
        kb.tt(yc[:], yc[:], rep["rwkv_ln_b"][:], ALU.add)
        kb.tt(b3(t2[:]), b3(v_), bc(bon[:]), ALU.mult, eng="pool")
        kb.tt(yc[:], yc[:], t2[:], ALU.add)
        kb.tt(yo[:], yc[:], gg[b][:], ALU.mult)
        yb = yT[(ch // 8) % 2]
        for h in range(8):
            kb.tr(PS[6][0:64, h * 64:(h + 1) * 64], yo[:, h * 64:(h + 1) * 64], id32[0:64, 0:64])
        kb.cp(yb[:, :, s8 * 64:(s8 + 1) * 64], PS[6][0:64, :].rearrange("p (h t) -> p h t", t=64), eng="act")
        if s8 == 7:
            tsl = slice(it * 512, (it + 1) * 512)
            kb.dma(sc["yT"][512:1024, tsl].rearrange("(h d) t -> d h t", d=64), yb[:], "yTo%d" % (it % 2),
                   w=[("yTa", it, h) for h in range(4)], eng="pool")


Model._cd_rwkv_body = _cd_rwkv_body


_NC_CACHE = {}


def _get_model(S):
    if S not in _NC_CACHE:
        m = Model(S, depth=4, parts="fm")
        nc = m.build()
        _NC_CACHE[S] = (m, nc)
    return _NC_CACHE[S]


def kernel(**inputs):
    x = np.asarray(inputs["x"])
    B, S, _ = x.shape
    m, nc = _get_model(S)
    hc = host_consts()
    shared = {}
    for k in m.inp:
        if k == "x":
            continue
        if k in hc:
            shared[k] = hc[k]
        else:
            shared[k] = np.ascontiguousarray(np.asarray(inputs[k], dtype=np.float32))
    in_maps = []
    for b in range(B):
        d = dict(shared)
        d["x"] = np.ascontiguousarray(x[b].astype(np.float32))
        in_maps.append(d)
    res = run_bass_kernel_spmd(nc, in_maps, core_ids=list(range(B)))
    out = np.stack([np.asarray(res.results[b]["out"]) for b in range(B)], axis=0)
    return out.astype(np.float32)
```

```python
import contextlib
import os
import numpy as np
import concourse.bass as bass
import concourse.mybir as mybir
from concourse.bass_utils import run_bass_kernel_spmd

F32 = mybir.dt.float32
BF16 = mybir.dt.bfloat16
AF = mybir.ActivationFunctionType
ALU = mybir.AluOpType
AX = mybir.AxisListType

D = 1024
DFF = 2816
NF = DFF // 128
G = 512
AB_IN = 3592
CD_IN = 3336
EPS = 1e-6

ENGS = ("pe", "act", "dve", "pool", "sp")


class Prog:
    def __init__(self, nc, same_engine_sync=("act", "dve", "pool")):
        self.nc = nc
        self.ops = {e: [] for e in ENGS}
        self.cnt = {e: 0 for e in ENGS}
        self.last_w = {}
        self.readers = {}
        self.waited = {}
        self.dma_cnt = {}
        self.same_engine_sync = set(same_engine_sync)
        self.sem_names = []
        self.nops = 0

    def _need(self, eng, tok, waits):
        if tok is None:
            return
        sem, val, src = tok
        if src == eng and eng not in self.same_engine_sync:
            return
        k = (eng, sem)
        if self.waited.get(k, 0) >= val:
            return
        self.waited[k] = val
        waits.append((sem, val))

    def op(self, eng, fn, reads=(), writes=(), dma_slot=None):
        waits = []
        for k in reads:
            self._need(eng, self.last_w.get(k), waits)
            if isinstance(k, str) and k.startswith("ps"):
                for t in self.readers.get(k, ()):
                    if t[2] != eng:
                        self._need(eng, t, waits)
        for k in writes:
            self._need(eng, self.last_w.get(k), waits)
            for t in self.readers.get(k, ()):
                self._need(eng, t, waits)
        if dma_slot is None:
            self.cnt[eng] += 1
            sem = "c_" + eng
            tok = (sem, self.cnt[eng], eng)
            inc = (sem, 1)
        else:
            sem = "d_" + str(dma_slot)
            self.dma_cnt[sem] = self.dma_cnt.get(sem, 0) + 1
            tok = (sem, 16 * self.dma_cnt[sem], None)
            inc = (sem, 16)
        if sem not in self.sem_names:
            self.sem_names.append(sem)
        for k in writes:
            self.last_w[k] = tok
            self.readers[k] = []
        for k in reads:
            if k not in writes:
                lst = self.readers.setdefault(k, [])
                lst.append(tok)
                if len(lst) > 8:
                    best = {}
                    for t in lst:
                        if t[0] not in best or best[t[0]][1] < t[1]:
                            best[t[0]] = t
                    self.readers[k] = list(best.values())
        self.ops[eng].append((waits, fn, inc))
        self.nops += 1
        return tok

    def barrier(self):
        latest = {}
        for e in ENGS:
            if self.cnt[e]:
                latest["c_" + e] = (self.cnt[e], e)
        for sname, n in self.dma_cnt.items():
            latest[sname] = (16 * n, None)
        for e in ENGS:
            waits = []
            for sname, (v, src) in latest.items():
                if src == e and e not in self.same_engine_sync:
                    continue
                self._need(e, (sname, v, None), waits)
            if waits:
                self.ops[e].append((waits, None, None))

    def final_wait(self, eng, keys):
        waits = []
        for k in keys:
            self._need(eng, self.last_w.get(k), waits)
        self.ops[eng].append((waits, None, None))

    def emit(self):
        nc = self.nc
        with contextlib.ExitStack() as st:
            sems = {}
            for n in self.sem_names:
                sems[n] = st.enter_context(nc.semaphore(n))
            block = st.enter_context(nc.Block())

            def run(eng_name):
                def body(e):
                    for waits, fn, inc in self.ops[eng_name]:
                        for (s, v) in waits:
                            e.wait_ge(sems[s], v)
                        if fn is not None:
                            fn(e).then_inc(sems[inc[0]], inc[1])
                return body

            block.tensor(run("pe"))
            block.scalar(run("act"))
            block.vector(run("dve"))
            block.gpsimd(run("pool"))
            block.sync(run("sp"))


def _keys(lst):
    out = []
    for a in lst:
        if a is None:
            continue
        if isinstance(a, (str, tuple)):
            out.append(a)
        else:
            out.append(a.name)
    return out


class KB:
    def __init__(self, nc, st):
        self.nc = nc
        self.st = st
        self.p = Prog(nc)
        self.phase_st = None
        self.prefix = ""

    def sb(self, name, shape, dt=F32):
        st = self.phase_st if self.phase_st is not None else self.st
        return st.enter_context(self.nc.sbuf_tensor(self.prefix + name, list(shape), dt))

    @contextlib.contextmanager
    def phase(self, prefix):
        with contextlib.ExitStack() as ph:
            self.phase_st = ph
            self.prefix = prefix + "_"
            self.p.barrier()
            try:
                yield
            finally:
                self.phase_st = None
                self.prefix = ""

    def ps(self, name, shape=(128, 512), dt=F32):
        return self.st.enter_context(self.nc.psum_tensor(name, list(shape), dt))

    def dram(self, name, shape, dt=F32, kind="Internal"):
        return self.nc.dram_tensor(name, list(shape), dt, kind=kind).ap()

    def mm(self, out, lhsT, rhs, start=True, stop=True, r=(), w=()):
        self.p.op("pe", lambda e: e.matmul(out, lhsT=lhsT, rhs=rhs, start=start, stop=stop),
                  reads=_keys([lhsT, rhs]) + _keys(r), writes=_keys([out]) + _keys(w))

    def tr(self, out, in_, ident):
        self.p.op("pe", lambda e: e.transpose(out, in_, ident),
                  reads=_keys([in_, ident]), writes=_keys([out]))

    def act(self, out, in_, func, bias=None, scale=1.0, accum_out=None, r=(), w=()):
        def fn(e):
            kw = {}
            if bias is not None:
                kw["bias"] = bias
            if accum_out is not None:
                kw["accum_out"] = accum_out
            return e.activation(out=out, in_=in_, func=func, scale=scale, **kw)
        rd = [in_] + [a for a in (bias, scale) if not isinstance(a, (int, float)) and a is not None]
        self.p.op("act", fn, reads=_keys(rd) + _keys(r), writes=_keys([out, accum_out]) + _keys(w))

    def ts(self, out, in0, s1, op0, s2=None, op1=None, eng="dve", accum_out=None, r=(), w=()):
        def fn(e):
            kw = {}
            if op1 is not None:
                kw["op1"] = op1
            if accum_out is not None:
                kw["accum_out"] = accum_out
            return e.tensor_scalar(out=out, in0=in0, scalar1=s1, scalar2=s2, op0=op0, **kw)
        rd = [in0] + [a for a in (s1, s2) if not isinstance(a, (int, float)) and a is not None]
        self.p.op(eng, fn, reads=_keys(rd) + _keys(r), writes=_keys([out, accum_out]) + _keys(w))

    def tt(self, out, in0, in1, op, eng="dve", r=(), w=()):
        self.p.op(eng, lambda e: e.tensor_tensor(out=out, in0=in0, in1=in1, op=op),
                  reads=_keys([in0, in1]) + _keys(r), writes=_keys([out]) + _keys(w))

    def stt(self, out, in0, scalar, in1, op0, op1, eng="dve", r=(), w=()):
        rd = [in0, in1] + ([scalar] if not isinstance(scalar, (int, float)) else [])
        self.p.op(eng, lambda e: e.scalar_tensor_tensor(out=out, in0=in0, scalar=scalar, in1=in1, op0=op0, op1=op1),
                  reads=_keys(rd) + _keys(r), writes=_keys([out]) + _keys(w))

    def cp(self, out, in_, eng="dve", r=(), w=()):
        if eng == "act":
            self.p.op("act", lambda e: e.copy(out=out, in_=in_), reads=_keys([in_]) + _keys(r), writes=_keys([out]) + _keys(w))
        else:
            self.p.op(eng, lambda e: e.tensor_copy(out=out, in_=in_), reads=_keys([in_]) + _keys(r), writes=_keys([out]) + _keys(w))

    def memset(self, ap, val, eng="dve"):
        self.p.op(eng, lambda e: e.memset(ap, val), reads=[], writes=_keys([ap]))

    def red(self, out, in_, op, axis=AX.X, eng="dve"):
        self.p.op(eng, lambda e: e.tensor_reduce(out=out, in_=in_, axis=axis, op=op),
                  reads=_keys([in_]), writes=_keys([out]))

    def dma(self, out, in_, slot, r=None, w=None, eng="sp", slow=False):
        rk = _keys(r) if r is not None else _keys([in_])
        wk = _keys(w) if w is not None else _keys([out])
        if slow:
            self.p.op(eng, lambda e: e.dma_start(out=out, in_=in_, allow_slow_non_contiguous=True), reads=rk, writes=wk, dma_slot=slot)
        else:
            self.p.op(eng, lambda e: e.dma_start(out=out, in_=in_), reads=rk, writes=wk, dma_slot=slot)


class Model:
    def __init__(self, S, depth=4, parts="fm", layers=None):
        self.layers = layers
        self.S = S
        self.depth = depth
        self.parts = parts

    def build(self):
        nc = bass.Bass("TRN2", target_bir_lowering=False)
        self.nc = nc
        S = self.S
        with contextlib.ExitStack() as st:
            kb = KB(nc, st)
            self.kb = kb
            L = self.depth
            NE, NO = (L + 1) // 2, L // 2
            self.inp = {}

            def ein(name, shape):
                self.inp[name] = nc.dram_tensor(name, list(shape), F32, kind="ExternalInput").ap()
                return self.inp[name]
            ein("x", [S, D])
            ein("ffn_norm", [L, 2, D]); ein("mix_norm", [L, D])
            ein("ffn_w_gate", [L, 2, D, DFF]); ein("ffn_w_up", [L, 2, D, DFF]); ein("ffn_w_down", [L, 2, DFF, D])
            self.out = nc.dram_tensor("out", [S, D], F32, kind="ExternalOutput").ap()
            self.xT = kb.dram("xT", [D, S])
            pool = kb.dram("pool", [5136 * S * int(os.environ.get("POOLX", "1"))], BF16)

            def view(off, shape, dt):
                n = shape[0] * shape[1] * (2 if dt == F32 else 1)
                v = pool[off * S: off * S + n]
                if dt == F32:
                    v = v.bitcast(F32)
                return v.rearrange("(r c) -> r c", c=shape[1])
            self.view = view
            self.scr = dict(
                qkmT=view(0, [1024, S], BF16), qkaT=view(1024, [1024, S], BF16),
                tmv=view(2048, [S, 1024], BF16), qa32=view(3072, [512, S], F32),
                tmo=view(4096, [S, 512], F32), tmg=view(5120, [S, 8], F32),
                yT=kb.dram("yT", [1024, S], BF16), tvec=kb.dram("tvec", [4, TV_N]))
            self.consts()
            self.kmean = kb.sb("kmean", [128, 4, max(S // 256, 8)], F32)
            if os.environ.get("POISON"):
                with kb.phase("poison"):
                    pz = kb.sb("pz", [128, 4096], BF16)
                    kb.memset(pz[:], float("nan"))
                    pv = pool.rearrange("(n p c) -> n p c", p=128, c=S // 8)
                    for i in range(pv.shape[0]):
                        kb.dma(pv[i], pz[:, 0:S // 8], "pz", w=[("poison", i)])
                    yv_ = self.scr["yT"].rearrange("(n p) (a c) -> n a p c", p=128, c=min(S, 4096))
                    for i in range(yv_.shape[0]):
                        for a_ in range(yv_.shape[1]):
                            kb.dma(yv_[i, a_], pz[:, 0:min(S, 4096)], "pz", w=[("poison", "y", i, a_)])
                    xv_ = self.xT.bitcast(BF16).rearrange("(n p) (a c) -> n a p c", p=128, c=4096)
                    for i in range(xv_.shape[0]):
                        for a_ in range(xv_.shape[1]):
                            kb.dma(xv_[i, a_], pz[:], "pz", w=[("poison", "x", i, a_)])
            self.phase_in()
            for l in range(L):
                if self.layers is not None and l not in self.layers:
                    continue
                if "f" in self.parts:
                    self.ffn(l, 0)
                if l % 2 == 0:
                    self.ab_setup(l // 2)
                    if "m" in self.parts:
                        ab_stop = int(os.environ.get("AB_STOP", "9"))
                        self.ab_inproj(l)
                        if ab_stop >= 2:
                            self.ab_mlstm(l)
                        if ab_stop >= 3:
                            self.ab_moba(l)
                        if ab_stop >= 4:
                            self.outproj(l, "ab_w_out")
                else:
                    if "m" in self.parts:
                        self.cd_setup(l // 2)
                        self.cd_all(l)
                if "f" in self.parts:
                    self.ffn(l, 1)
            self.phase_out()
            kb.p.final_wait("sp", [("out", i) for i in range(S // 512)])
            kb.p.emit()
        return nc

    def consts(self):
        kb = self.kb
        self.ones_bf = kb.sb("ones_bf", [128, 128], BF16)
        kb.memset(self.ones_bf[:], 1.0)
        self.epsc = kb.sb("epsc", [128, 1], F32)
        kb.memset(self.epsc[:], EPS)
        self.ident = kb.sb("ident", [128, 128], F32)
        identd = self.nc.dram_tensor("identd", [128, 128], F32, kind="ExternalInput").ap()
        self.inp["identd"] = identd
        kb.dma(self.ident[:], identd[:, :], "c0")
        L = self.depth
        self.gains = kb.sb("gains", [128, L * 3, 8], F32)
        for l in range(L):
            for j in range(2):
                kb.dma(self.gains[:, l * 3 + j, :], self.inp["ffn_norm"][l, j, :].rearrange("(c p) -> p c", p=128), "c0",
                       w=[("gains", l, j)], slow=True)
            kb.dma(self.gains[:, l * 3 + 2, :], self.inp["mix_norm"][l, :].rearrange("(c p) -> p c", p=128), "c0",
                   w=[("gains", l, 2)], slow=True)
        self.PS = [kb.ps("ps%d" % i) for i in range(7)]
        psb = kb.ps("psb", (128, 512), BF16)
        self.PSB = [psb[:, i * 256:(i + 1) * 256] for i in range(2)]

    def rstd_from_sumsq(self, out, ssq, scale, eps):
        kb = self.kb
        kb.act(out, ssq, AF.Sqrt, bias=self.epsc[:, 0:1] if eps == EPS else eps, scale=scale)
        kb.p.op("dve", lambda e: e.reciprocal(out=out, in_=out), reads=_keys([out]), writes=_keys([out]))

    def phase_in(self):
        with self.kb.phase("pin"):
            self._phase_in()

    def _phase_in(self):
        kb = self.kb
        S = self.S
        xin = [kb.sb("tin%d" % i, [128, 4, D], F32) for i in range(2)]
        xo = [kb.sb("tino%d" % i, [128, 8, 512], F32) for i in range(2)]
        x = self.inp["x"]
        xTv = self.xT.rearrange("(c p) t -> p c t", p=128)
        for it in range(S // 512):
            b = it % 2
            kb.dma(xin[b][:], x[it * 512:(it + 1) * 512, :].rearrange("(j p) d -> p j d", p=128), "tin%d" % b)
            for c in range(8):
                ps = self.PS[c % 2]
                for j in range(4):
                    kb.tr(ps[:, j * 128:(j + 1) * 128], xin[b][:, j, c * 128:(c + 1) * 128], self.ident[:])
                kb.cp(xo[b][:, c, :], ps[:, :], eng="act" if c % 2 else "dve")
            kb.dma(xTv[:, :, it * 512:(it + 1) * 512], xo[b][:], "tino%d" % b, w=[("xT", it)], eng="pool")

    def phase_out(self):
        with self.kb.phase("pout"):
            self._phase_out()

    def _phase_out(self):
        kb = self.kb
        S = self.S
        xin = [kb.sb("tout%d" % i, [128, 8, 512], F32) for i in range(2)]
        xo = [kb.sb("touto%d" % i, [128, 4, D], F32) for i in range(2)]
        xTv = self.xT.rearrange("(c p) t -> p c t", p=128)
        for it in range(S // 512):
            b = it % 2
            kb.dma(xin[b][:], xTv[:, :, it * 512:(it + 1) * 512], "tout%d" % b, r=[("xT", it)])
            for j in range(4):
                for h in range(2):
                    ps = self.PS[(j * 2 + h) % 2]
                    for c in range(4):
                        cc = h * 4 + c
                        kb.tr(ps[:, c * 128:(c + 1) * 128], xin[b][:, cc, j * 128:(j + 1) * 128], self.ident[:])
                    kb.cp(xo[b][:, j, h * 512:(h + 1) * 512], ps[:, :], eng="act" if h else "dve")
            kb.dma(self.out[it * 512:(it + 1) * 512, :].rearrange("(j p) d -> p j d", p=128), xo[b][:], "touto%d" % b,
                   w=[("out", it)], eng="pool")

    def ffn(self, l, j):
        with self.kb.phase("f%d%d" % (l, j)):
            self._ffn(l, j)

    def _ffn(self, l, j):
        kb = self.kb
        S = self.S
        nm = "f%d%d" % (l, j)
        NT = S // 512
        wg_in = self.inp["ffn_w_gate"][l, j]
        wu_in = self.inp["ffn_w_up"][l, j]
        wd_in = self.inp["ffn_w_down"][l, j]
        if not hasattr(self, "wgu_s"):
            self.wgu_s = kb.dram("wgu_s", [NF, 128, 2 * 8 * 128], BF16)
        wgu_s = self.wgu_s
        if True:
            self.ffn_bufs = dict(
                cin=[kb.sb("fcin%d" % i, [128, 1408], F32) for i in range(2)],
                cout=[kb.sb("fcout%d" % i, [128, 1408], BF16) for i in range(2)],
                wd=kb.sb("fwd", [128, NF, D], BF16),
                xt=[kb.sb("fx%d" % i, [128, 8, 512], F32) for i in range(2)],
                sq=kb.sb("fsq", [128, 8, 512], BF16),
                h=kb.sb("fh", [128, 8, 512], BF16),
                aT=kb.sb("faT", [128, NF, 512], BF16),
                rstd=kb.sb("frstd", [128, 512], F32),
                sg=[kb.sb("fsg%d" % i, [128, 512], F32) for i in range(2)],
                ring=[kb.sb("fring%d" % i, [128, 2, 8, 128], BF16) for i in range(4)],
                cnt=0,
            )
        fb = self.ffn_bufs
        ci = 0
        for mi, wsrc in enumerate((wg_in, wu_in)):
            for kc in range(8):
                for half in range(2):
                    b = ci % 2
                    ci += 1
                    kb.dma(fb["cin"][b][:], wsrc[kc * 128:(kc + 1) * 128, half * 1408:(half + 1) * 1408], "fcin%d" % b)
                    kb.cp(fb["cout"][b][:], fb["cin"][b][:], eng="pool" if ci % 2 else "dve")
                    dst = wgu_s[half * 11:(half + 1) * 11, :, (mi * 8 + kc) * 128:(mi * 8 + kc + 1) * 128].rearrange("f p m -> p f m")
                    kb.dma(dst, fb["cout"][b][:].rearrange("p (f m) -> p f m", m=128), "fcout%d" % b,
                           w=[("wgu", mi, kc, half)], eng="pool")
        for f in range(NF):
            b = ci % 2
            ci += 1
            kb.dma(fb["cin"][b][:, 0:D], wd_in[f * 128:(f + 1) * 128, :], "fcin%d" % b)
            kb.cp(fb["wd"][:, f, :], fb["cin"][b][:, 0:D], eng="pool" if ci % 2 else "dve")
        wgu_keys = [("wgu", mi, kc, half) for mi in range(2) for kc in range(8) for half in range(2)]
        xTv = self.xT.rearrange("(c p) t -> p c t", p=128)
        gi = l * 3 + j
        PS = self.PS
        for it in range(NT):
            xt = fb["xt"][it % 2]
            kb.dma(xt[:], xTv[:, :, it * 512:(it + 1) * 512], "fx%d" % (it % 2), r=[("xT", it)])
            for c in range(8):
                kb.act(fb["sq"][:, c, :], xt[:, c, :], AF.Square)
            for c in range(8):
                kb.mm(PS[0][:, :], self.ones_bf[:], fb["sq"][:, c, :], start=(c == 0), stop=(c == 7))
            self.rstd_from_sumsq(fb["rstd"][:], PS[0][:, :], 1.0 / D, EPS)
            for c in range(8):
                kb.stt(fb["h"][:, c, :], xt[:, c, :], self.gains[:, gi, c:c + 1], fb["rstd"][:], ALU.mult, ALU.mult,
                       r=[("gains", l, j)])
            for f in range(NF):
                rg = fb["ring"][fb["cnt"] % 4]
                slot = "fring%d" % (fb["cnt"] % 4)
                fb["cnt"] += 1
                kb.dma(rg[:].rearrange("p a k m -> p (a k m)"), wgu_s[f, :, :], slot, r=wgu_keys)
                pg = PS[1 + f % 2]
                pu = PS[3 + f % 2]
                for kc in range(8):
                    kb.mm(pg[:, :], rg[:, 0, kc, :], fb["h"][:, kc, :], start=(kc == 0), stop=(kc == 7))
                for kc in range(8):
                    kb.mm(pu[:, :], rg[:, 1, kc, :], fb["h"][:, kc, :], start=(kc == 0), stop=(kc == 7))
                sg = fb["sg"][f % 2]
                kb.act(sg[:], pg[:, :], AF.Silu)
                kb.tt(fb["aT"][:, f, :], sg[:], pu[:, :], ALU.mult)
            for dc in range(8):
                pd = PS[5 + dc % 2]
                for f in range(NF):
                    kb.mm(pd[:, :], fb["wd"][:, f, dc * 128:(dc + 1) * 128], fb["aT"][:, f, :], start=(f == 0), stop=(f == NF - 1))
                kb.stt(xt[:, dc, :], pd[:, :], 0.5, xt[:, dc, :], ALU.mult, ALU.add)
            kb.dma(xTv[:, :, it * 512:(it + 1) * 512], xt[:], "fxo%d" % (it % 2), w=[("xT", it)], eng="pool")


def t5_bucket_np(rel):
    n = np.maximum(rel, 0)
    nf = np.maximum(n, 1).astype(np.float32)
    large = 16 + (np.log(nf / np.float32(16)) / np.float32(np.log(64.0)) * np.float32(16)).astype(np.int32)
    large = np.minimum(large, 31)
    return np.where(n < 16, n, large)


TV_LO = -511
TV_N = 2048


def host_consts():
    c = {}
    c["identd"] = np.eye(128, dtype=np.float32)
    j = np.arange(128)
    c["triu"] = (j[:, None] <= j[None, :]).astype(np.float32)
    c["antiid"] = np.eye(128, dtype=np.float32)[::-1].copy()
    c["tril_s"] = (j[:, None] > j[None, :]).astype(np.float32)
    rel = np.arange(TV_N) + TV_LO
    oh = np.zeros((33, TV_N), np.float32)
    b = t5_bucket_np(rel)
    for i in range(TV_N):
        if rel[i] >= 0:
            oh[b[i], i] = 1.0
        else:
            oh[32, i] = 1.0
    c["onehot"] = oh
    return c


def _ab_setup(self, j):
    nc = self.nc
    if "ab_w_in" in self.inp:
        return
    NE = (self.depth + 1) // 2

    def ein(name, shape):
        self.inp[name] = nc.dram_tensor(name, list(shape), F32, kind="ExternalInput").ap()
    ein("ab_w_in", [NE, D, AB_IN]); ein("ab_w_out", [NE, D, D])
    ein("mlstm_conv_w", [NE, 4, 1024]); ein("mlstm_conv_b", [NE, 1024])
    ein("mlstm_b_i", [NE, 4]); ein("mlstm_b_f", [NE, 4]); ein("mlstm_h_norm", [NE, 512])
    ein("moba_q_norm", [NE, 128]); ein("moba_k_norm", [NE, 128]); ein("rel_bias", [32, 4])
    ein("triu", [128, 128]); ein("antiid", [128, 128]); ein("onehot", [33, TV_N])


Model.ab_setup = _ab_setup


def _rmsnorm_tile(self, xt, uT, gi, keyg, ps, sq, rstd):
    kb = self.kb
    for c in range(8):
        kb.act(sq[:, c, :], xt[:, c, :], AF.Square)
    for c in range(8):
        kb.mm(ps[:, :], self.ones_bf[:], sq[:, c, :], start=(c == 0), stop=(c == 7))
    self.rstd_from_sumsq(rstd[:], ps[:, :], 1.0 / D, EPS)
    for c in range(8):
        kb.stt(uT[:, c, :], xt[:, c, :], self.gains[:, gi, c:c + 1], rstd[:], ALU.mult, ALU.mult, r=[keyg])


Model.rmsnorm_tile = _rmsnorm_tile


def _load_win(self, wsrc, ncols, win, tmp):
    kb = self.kb
    W = tmp[0].shape[1]
    ci = 0
    for kc in range(8):
        c0 = 0
        while c0 < ncols:
            w = min(W, ncols - c0)
            b = ci % 2
            ci += 1
            kb.dma(tmp[b][:, 0:w], wsrc[kc * 128:(kc + 1) * 128, c0:c0 + w], "wtmp%d" % b)
            kb.cp(win[:, kc, c0:c0 + w], tmp[b][:, 0:w], eng="pool" if ci % 2 else "dve")
            c0 += w


Model.load_win = _load_win


def _ab_inproj(self, l):
    with self.kb.phase("abin%d" % l):
        self._ab_inproj_body(l)


def _ab_inproj_body(self, l):
    kb = self.kb
    S = self.S
    j = l // 2
    NT = S // 512
    PS = self.PS
    I = self.inp
    sc = self.scr
    win = kb.sb("win", [128, 8, AB_IN], BF16)
    tmp = [kb.sb("wtmp%d" % i, [128, 900], F32) for i in range(2)]
    self.load_win(I["ab_w_in"][j], AB_IN, win, tmp)
    cw = kb.sb("cw", [128, 8, 4], F32)
    cb = kb.sb("cb", [128, 8], F32)
    for k in range(4):
        kb.dma(cw[:, :, k], I["mlstm_conv_w"][j, k, :].rearrange("(c p) -> p c", p=128), "c1", slow=True)
    kb.dma(cb[:], I["mlstm_conv_b"][j].rearrange("(c p) -> p c", p=128), "c2", slow=True)
    gq = kb.sb("gq", [128, 2], F32)
    kb.dma(gq[:, 0:1], I["moba_q_norm"][j].rearrange("(p o) -> p o", o=1), "c3", slow=True)
    kb.dma(gq[:, 1:2], I["moba_k_norm"][j].rearrange("(p o) -> p o", o=1), "c4", slow=True)
    kb.ts(gq[:, 0:1], gq[:, 0:1], 128 ** -0.5, ALU.mult)
    xt_b = [kb.sb("x%d" % i, [128, 8, 512], F32) for i in range(2)]
    sq = kb.sb("sq", [128, 8, 512], BF16)
    uT = kb.sb("uT", [128, 8, 512], BF16)
    rstd = kb.sb("rstd", [128, 512], F32)
    pq = [kb.sb("pq%d" % i, [128, 515], F32) for i in range(8)]
    acc = [kb.sb("acc%d" % i, [128, 512], F32) for i in range(2)]
    qkm = [kb.sb("qkm%d" % i, [128, 8, 512], BF16) for i in range(1)] * 2
    sqa = kb.sb("sqa", [128, 512], BF16)
    rs2 = kb.sb("rs2", [128, 512], F32)
    qab = [kb.sb("qab%d" % i, [128, 8, 512], BF16) for i in range(1)] * 2
    qa32 = [kb.sb("qa32_%d" % i, [128, 4, 512], F32) for i in range(1)] * 2
    kn32 = kb.sb("kn32", [128, 512], F32)
    tmv = [kb.sb("tmv%d" % i, [128, 4, 1024], BF16) for i in range(1)] * 2
    tmo = [kb.sb("tmo%d" % i, [128, 4, 512], F32) for i in range(1)] * 2
    tmg = [kb.sb("tmg%d" % i, [128, 4, 8], F32) for i in range(1)] * 2
    xTv = self.xT.rearrange("(c p) t -> p c t", p=128)
    for c in range(8):
        kb.memset(pq[c][:, 0:3], 0.0)
    for it in range(NT):
        b = it % 2
        xt = xt_b[b]
        kb.dma(xt[:], xTv[:, :, it * 512:(it + 1) * 512], "x%d" % b, r=[("xT", it)])
        self.rmsnorm_tile(xt, uT, l * 3 + 2, ("gains", l, 2), PS[0], sq, rstd)
        for c in range(8):
            ps = PS[1 + c % 2]
            col0 = c * 128
            for kc in range(8):
                kb.mm(ps[:, :], win[:, kc, col0:col0 + 128], uT[:, kc, :], start=(kc == 0), stop=(kc == 7))
            if it > 0:
                kb.cp(pq[c][:, 0:3], pq[c][:, 512:515], eng="pool")
            kb.cp(pq[c][:, 3:515], ps[:, :], eng="act")
            a = acc[c % 2]
            kb.ts(a[:], pq[c][:, 0:512], cw[:, c, 0:1], ALU.mult, cb[:, c:c + 1], ALU.add)
            for k in range(1, 4):
                kb.stt(a[:], pq[c][:, k:k + 512], cw[:, c, k:k + 1], a[:], ALU.mult, ALU.add)
            if c < 4:
                kb.act(a[:], a[:], AF.Silu)
                kb.ts(qkm[b][:, c, :], a[:], 128 ** -0.5, ALU.mult, eng="pool")
            else:
                kb.act(qkm[b][:, c, :], a[:], AF.Silu)
        kb.dma(sc["qkmT"].rearrange("(c p) t -> p c t", p=128)[:, :, it * 512:(it + 1) * 512], qkm[b][:], "qkmo%d" % b,
               w=[("qkmT", it)], eng="pool")
        for c in range(8):
            ps = PS[3 + c % 2]
            ps2 = PS[5 + c % 2]
            col0 = 2056 + c * 128
            for kc in range(8):
                kb.mm(ps[:, :], win[:, kc, col0:col0 + 128], uT[:, kc, :], start=(kc == 0), stop=(kc == 7))
            kb.act(sqa[:], ps[:, :], AF.Square)
            kb.mm(ps2[:, :], self.ones_bf[:], sqa[:])
            self.rstd_from_sumsq(rs2[:], ps2[:, :], 1.0 / 128, EPS)
            if c < 4:
                kb.stt(qa32[b][:, c, :], ps[:, :], gq[:, 0:1], rs2[:], ALU.mult, ALU.mult)
                kb.cp(qab[b][:, c, :], qa32[b][:, c, :], eng="pool")
            else:
                kb.stt(kn32[:], ps[:, :], gq[:, 1:2], rs2[:], ALU.mult, ALU.mult)
                kb.cp(qab[b][:, c, :], kn32[:], eng="pool")
                kb.red(self.kmean[:, c - 4, 2 * it:2 * it + 2], kn32[:].rearrange("p (n k) -> p n k", k=256), ALU.add)
        kb.dma(sc["qkaT"].rearrange("(c p) t -> p c t", p=128)[:, :, it * 512:(it + 1) * 512], qab[b][:], "qabo%d" % b,
               w=[("qkaT", it)], eng="pool")
        kb.dma(sc["qa32"].rearrange("(c p) t -> p c t", p=128)[:, :, it * 512:(it + 1) * 512], qa32[b][:], "qa32o%d" % b,
               w=[("qa32", it)], eng="pool")
        for s in range(4):
            lt = [uT[:, kc, s * 128:(s + 1) * 128] for kc in range(8)]
            for gi_, (col0, ncol) in enumerate(((1024, 512), (1536, 512), (2048, 8), (3080, 512))):
                ps = PS[1 + (s * 4 + gi_) % 4]
                for kc in range(8):
                    kb.mm(ps[:, 0:ncol], lt[kc], win[:, kc, col0:col0 + ncol], start=(kc == 0), stop=(kc == 7))
                if gi_ == 0:
                    kb.cp(tmv[b][:, s, 0:512], ps[:, 0:512], eng="act")
                elif gi_ == 1:
                    kb.act(tmo[b][:, s, :], ps[:, 0:512], AF.Sigmoid)
                elif gi_ == 2:
                    kb.cp(tmg[b][:, s, :], ps[:, 0:8], eng="dve")
                else:
                    kb.cp(tmv[b][:, s, 512:1024], ps[:, 0:512], eng="dve")
        tsl = slice(it * 512, (it + 1) * 512)
        kb.dma(sc["tmv"][tsl, :].rearrange("(s p) c -> p s c", p=128), tmv[b][:], "tmvo%d" % b, w=[("tmv", it)], eng="pool")
        kb.dma(sc["tmo"][tsl, :].rearrange("(s p) c -> p s c", p=128), tmo[b][:], "tmoo%d" % b, w=[("tmo", it)], eng="pool")
        kb.dma(sc["tmg"][tsl, :].rearrange("(s p) c -> p s c", p=128), tmg[b][:], "tmgo%d" % b, w=[("tmg", it)], eng="pool", slow=True)
    kb.ts(self.kmean[:], self.kmean[:], 1.0 / 256, ALU.mult)


Model.ab_inproj = _ab_inproj
Model._ab_inproj_body = _ab_inproj_body


def _ab_mlstm(self, l):
    with self.kb.phase("abm%d" % l):
        self._ab_mlstm_body(l)


def _ab_mlstm_body(self, l):
    kb = self.kb
    S = self.S
    j = l // 2
    NT = S // 512
    PS = self.PS
    I = self.inp
    sc = self.scr
    triu = kb.sb("triu", [128, 128], F32)
    kb.dma(triu[:], I["triu"][:, :], "c1")
    identb = kb.sb("identb", [128, 128], BF16)
    kb.cp(identb[:], self.ident[:])
    ones32 = kb.sb("ones32", [128, 128], F32)
    kb.memset(ones32[:], 1.0)
    bif = kb.sb("bif", [128, 8], F32)
    kb.dma(bif[:, 0:4], I["mlstm_b_i"][j:j + 1, :].broadcast_to([128, 4]), "c2", slow=True)
    kb.dma(bif[:, 4:8], I["mlstm_b_f"][j:j + 1, :].broadcast_to([128, 4]), "c3", slow=True)
    hn = kb.sb("hn", [128, 512], F32)
    kb.dma(hn[:], I["mlstm_h_norm"][j:j + 1, :].broadcast_to([128, 512]), "c4", slow=True)
    C = [kb.sb("C%d" % h, [128, 129], F32) for h in range(4)]
    Cb = [kb.sb("Cb%d" % h, [128, 129], BF16) for h in range(4)]
    for h in range(4):
        kb.memset(C[h][:], 0.0)
        kb.memset(Cb[h][:], 0.0, eng="pool")
    qk = [kb.sb("qk%d" % i, [128, 8, 512], BF16) for i in range(2)]
    vm = [kb.sb("vm%d" % i, [128, 4, 1024], BF16) for i in range(2)]
    so = [kb.sb("so%d" % i, [128, 4, 512], F32) for i in range(2)]
    gt = [kb.sb("gt%d" % i, [128, 4, 8], F32) for i in range(2)]
    ipre = kb.sb("ipre", [128, 4], F32)
    lf = kb.sb("lf", [128, 4], F32)
    acol = kb.sb("acol", [128, 4], F32)
    ccol = kb.sb("ccol", [128, 4], F32)
    dec = kb.sb("dec", [128, 4], F32)
    gate = kb.sb("gate", [128, 512], F32)
    vext = kb.sb("vext", [128, 4, 129], BF16)
    va = [kb.sb("va%d" % i, [128, 129], BF16) for i in range(2)]
    sm = [kb.sb("sm%d" % i, [128, 128], BF16) for i in range(2)]
    ktok = [kb.sb("ktok%d" % i, [128, 128], BF16) for i in range(2)]
    den = kb.sb("den", [128, 4], F32)
    den2 = kb.sb("den2", [128, 4], F32)
    hh = [kb.sb("hh%d" % i, [128, 128], F32) for i in range(2)]
    junk = kb.sb("junk", [128, 128], F32)
    ss = kb.sb("ss", [128, 4], F32)
    y = kb.sb("y", [128, 512], BF16)
    yT = [kb.sb("yT%d" % i, [128, 4, 512], BF16) for i in range(2)]
    PSB = self.PSB
    for it in range(NT):
        b = it % 2
        tsl = slice(it * 512, (it + 1) * 512)
        kb.dma(qk[b][:], sc["qkmT"].rearrange("(c p) t -> p c t", p=128)[:, :, tsl], "qk%d" % b, r=[("qkmT", it)])
        kb.dma(vm[b][:], sc["tmv"][tsl, :].rearrange("(s p) c -> p s c", p=128), "vm%d" % b, r=[("tmv", it)])
        kb.dma(so[b][:], sc["tmo"][tsl, :].rearrange("(s p) c -> p s c", p=128), "so%d" % b, r=[("tmo", it)])
        kb.dma(gt[b][:], sc["tmg"][tsl, :].rearrange("(s p) c -> p s c", p=128), "gt%d" % b, r=[("tmg", it)], slow=True)
        for s in range(4):
            csl = slice(s * 128, (s + 1) * 128)
            kb.tt(ipre[:], gt[b][:, s, 0:4], bif[:, 0:4], ALU.add)
            kb.tt(lf[:], gt[b][:, s, 4:8], bif[:, 4:8], ALU.add)
            kb.act(lf[:], lf[:], AF.Exp, scale=-1.0)
            kb.act(lf[:], lf[:], AF.Ln, bias=1.0)
            kb.ts(lf[:], lf[:], -1.0, ALU.mult)
            pg = PS[0]
            kb.mm(pg[:, 0:4], triu[:], lf[:])
            kb.mm(pg[:, 8:12], ones32[:], lf[:])
            kb.act(ccol[:], pg[:, 0:4], AF.Exp)
            kb.act(dec[:], pg[:, 8:12], AF.Exp)
            kb.tt(acol[:], ipre[:], pg[:, 0:4], ALU.subtract)
            kb.act(acol[:], acol[:], AF.Exp)
            kb.cp(vext[:, :, 0:128], vm[b][:, s, 0:512].rearrange("p (h d) -> p h d", d=128), eng="pool")
            kb.memset(vext[:, :, 128:129], 1.0, eng="pool")
            kb.tt(gate[:], so[b][:, s, :], hn[:], ALU.mult, eng="pool")
            for h in range(4):
                qT = qk[b][:, h, csl]
                kT = qk[b][:, 4 + h, csl]
                pS = PS[1 + h % 2]
                kb.mm(pS[:, 0:128], kT, qT)
                kb.tt(sm[h % 2][:], pS[:, 0:128], triu[:], ALU.mult)
                kb.ts(va[h % 2][:], vext[:, h, :], acol[:, h:h + 1], ALU.mult, eng="pool")
                pN = PS[3 + h % 2]
                kb.mm(pN[:, 0:129], sm[h % 2][:], va[h % 2][:], start=True, stop=False)
                kb.mm(pN[:, 0:129], qT, Cb[h][:], start=False, stop=True)
                pT = PSB[h % 2]
                kb.tr(pT[:, 0:128], kT, identb[:])
                kb.cp(ktok[h % 2][:], pT[:, 0:128], eng="act")
                pC = PS[5 + h % 2]
                kb.mm(pC[:, 0:129], ktok[h % 2][:], va[h % 2][:])
                kb.tt(C[h][:], C[h][:], pC[:, 0:129], ALU.add)
                kb.ts(C[h][:], C[h][:], dec[:, h:h + 1], ALU.mult)
                kb.cp(Cb[h][:], C[h][:], eng="act")
                kb.ts(den[:, h:h + 1], pN[:, 128:129], ccol[:, h:h + 1], ALU.mult)
                kb.ts(den2[:, h:h + 1], den[:, h:h + 1], -1.0, ALU.mult)
                kb.tt(den[:, h:h + 1], den[:, h:h + 1], den2[:, h:h + 1], ALU.max)
                kb.ts(den[:, h:h + 1], den[:, h:h + 1], 1.0, ALU.max)
                kb.p.op("dve", (lambda o: (lambda e: e.reciprocal(out=o, in_=o)))(den[:, h:h + 1]), reads=_keys([den]), writes=_keys([den]))
                kb.tt(den[:, h:h + 1], den[:, h:h + 1], ccol[:, h:h + 1], ALU.mult)
                kb.act(hh[h % 2][:], pN[:, 0:128], AF.Copy, scale=den[:, h:h + 1])
                kb.act(junk[:], hh[h % 2][:], AF.Square, accum_out=ss[:, h:h + 1])
                self.rstd_from_sumsq(ss[:, h:h + 1], ss[:, h:h + 1], 1.0 / 128, EPS)
                kb.stt(y[:, h * 128:(h + 1) * 128], hh[h % 2][:], ss[:, h:h + 1], gate[:, h * 128:(h + 1) * 128], ALU.mult, ALU.mult)
            for h in range(4):
                pT = PSB[h % 2]
                kb.tr(pT[:, 0:128], y[:, h * 128:(h + 1) * 128], identb[:])
                kb.cp(yT[b][:, h, csl], pT[:, 0:128], eng="act" if h % 2 else "dve")
        kb.dma(sc["yT"].rearrange("(c p) t -> p c t", p=128)[:, 0:4, tsl], yT[b][:], "yTo%d" % b, w=[("yTm", it)], eng="pool")


Model.ab_mlstm = _ab_mlstm
Model._ab_mlstm_body = _ab_mlstm_body


def _rstd_lnexp(self, out, ssq, scale, eps_ap):
    kb = self.kb
    kb.act(out, ssq, AF.Ln, bias=eps_ap, scale=scale)
    kb.act(out, out, AF.Exp, scale=-0.5)


Model.rstd_lnexp = _rstd_lnexp


def _ab_moba(self, l):
    with self.kb.phase("aba%d" % l):
        self._ab_moba_body(l)


def _ab_moba_body(self, l):
    kb = self.kb
    S = self.S
    NT = S // 512
    NB = S // 256
    PS = self.PS
    PSB = self.PSB
    I = self.inp
    sc = self.scr
    rb = kb.sb("rb", [128, 128], F32)
    kb.memset(rb[:], 0.0)
    kb.memset(rb[32:64, 0:4], -30000.0)
    kb.dma(rb[0:32, 0:4], I["rel_bias"][:, :], "c1", slow=True)
    oh = kb.sb("oh", [128, TV_N], F32)
    kb.memset(oh[:], 0.0)
    kb.dma(oh[0:33, :], I["onehot"][:, :], "c2")
    tv = kb.sb("tv", [4, TV_N], F32)
    for q in range(TV_N // 512):
        kb.mm(PS[q][:, :], rb[:], oh[:, q * 512:(q + 1) * 512])
        kb.cp(tv[:, q * 512:(q + 1) * 512], PS[q][0:4, :])
    kb.dma(sc["tvec"][:, :], tv[:], "c3", w=["tvec"])
    b31 = kb.sb("b31", [128, 4], F32)
    kb.dma(b31[:], I["rel_bias"][31:32, :].broadcast_to([128, 4]), "c4", slow=True)
    antib = kb.sb("antib", [128, 128], BF16)
    anti32 = kb.sb("anti32", [128, 128], F32)
    kb.dma(anti32[:], I["antiid"][:, :], "c5")
    kb.cp(antib[:], anti32[:])
    ident_b = kb.sb("identb", [128, 128], BF16)
    kb.cp(ident_b[:], self.ident[:])
    sel = kb.sb("sel", [128, 32, 128], BF16)
    kb.memset(sel[:], 0.0)
    kb.cp(sel[0:32, :, :], self.ident[0:32, 0:32].unsqueeze(2).broadcast_to([32, 32, 128]))
    deltas = list(range(-384, 897, 128))
    toep = kb.sb("toep", [128, len(deltas), 512], BF16)
    ttmp = [kb.sb("ttmp%d" % i, [128, 512], F32) for i in range(2)]
    kT = kb.sb("kT", [128, S], BF16)
    V = kb.sb("V", [128, S // 128, 128], BF16)
    qT = [kb.sb("qT%d" % i, [128, 512], BF16) for i in range(2)]
    q32 = [kb.sb("q32_%d" % i, [128, 512], F32) for i in range(2)]
    g = kb.sb("g", [128, 32], F32)
    m8 = kb.sb("m8", [128, 8], F32)
    negm = kb.sb("negm", [128, 128], F32)
    kb.memset(negm[:], 0.0)
    negT = kb.sb("negT", [128, 512], BF16)
    kb.memset(negT[:], 0.0)
    pT = [kb.sb("pT%d" % i, [128, 512], BF16) for i in range(3)]
    rec = kb.sb("rec", [128, 512], F32)
    yo = [kb.sb("yo%d" % i, [128, 512], BF16) for i in range(2)]
    tvh = sc["tvec"].tensor
    for h in range(4):
        for di, dl in enumerate(deltas):
            off = h * TV_N + (dl - 127 - TV_LO)
            src = bass.AP(tensor=tvh, offset=off, ap=[[1, 128], [1, 512]])
            kb.dma(ttmp[di % 2][:], src, "ttmp%d" % (di % 2), r=["tvec"])
            kb.cp(toep[:, di, :], ttmp[di % 2][:], eng="pool" if di % 2 else "dve")
        kb.dma(kT[:], sc["qkaT"][(4 + h) * 128:(5 + h) * 128, :], "kT", r=[("qkaT", i) for i in range(NT)])
        kb.dma(V[:], sc["tmv"][:, 512 + h * 128:512 + (h + 1) * 128].rearrange("(n p) d -> p n d", p=128), "V",
               r=[("tmv", i) for i in range(NT)])
        for it in range(NT):
            b = it % 2
            tsl = slice(it * 512, (it + 1) * 512)
            kb.dma(qT[b][:], sc["qkaT"][h * 128:(h + 1) * 128, tsl], "qT%d" % b, r=[("qkaT", it)])
            kb.dma(q32[b][:], sc["qa32"][h * 128:(h + 1) * 128, tsl], "q32%d" % b, r=[("qa32", it)])
            for s in range(4):
                own = 2 * it + s // 2
                kb.memset(negm[:, 0:32], -30000.0)
                if own > 0:
                    kb.mm(PS[0][:, 0:NB], q32[b][:, s * 128:(s + 1) * 128], self.kmean[:, h, 0:NB], r=["kmean"])
                    kb.memset(g[:], -1e30)
                    kb.cp(g[:, 0:own], PS[0][:, 0:own])
                    kb.p.op("dve", (lambda o, i_: (lambda e: e.max(out=o, in_=i_)))(m8[:], g[:]), reads=_keys([g]), writes=_keys([m8]))
                    kb.ts(negm[:, 0:own], g[:, 0:own], m8[:, 2:3], ALU.is_ge, 30000.0, ALU.mult)
                    kb.ts(negm[:, 0:own], negm[:, 0:own], -30000.0, ALU.add)
                kb.memset(negm[:, own:own + 1], 0.0)
                kb.tr(PS[1][:, 0:128], negm[:], self.ident[:])
                kb.cp(negT[0:32, s * 128:(s + 1) * 128], PS[1][0:32, 0:128])
            nkt = 4 * (it + 1)
            pO = PS[5]
            pL = PS[6]
            def scores(kt):
                pS = PS[2 + kt % 3]
                dl = it * 512 - kt * 128
                near = dl <= 896
                kb.mm(pS[:, :], kT[:, kt * 128:(kt + 1) * 128], qT[b][:], start=True, stop=False)
                kb.mm(pS[:, :], sel[:, kt // 2, :], negT[:], start=False, stop=not near)
                if near:
                    kb.mm(pS[:, :], antib[:], toep[:, deltas.index(dl), :], start=False, stop=True)

            def softmax_pv(kt):
                pS = PS[2 + kt % 3]
                dl = it * 512 - kt * 128
                near = dl <= 896
                if near:
                    kb.act(pT[kt % 3][:], pS[:, :], AF.Exp)
                else:
                    kb.act(pT[kt % 3][:], pS[:, :], AF.Exp, bias=b31[:, h:h + 1])
                kb.mm(pO[:, :], V[:, kt, :], pT[kt % 3][:], start=(kt == 0), stop=(kt == nkt - 1))
                kb.mm(pL[:, :], self.ones_bf[:], pT[kt % 3][:], start=(kt == 0), stop=(kt == nkt - 1))
            scores(0)
            if nkt > 1:
                scores(1)
            for kt in range(nkt):
                if kt + 2 < nkt:
                    scores(kt + 2)
                softmax_pv(kt)
            kb.p.op("dve", (lambda o, i_: (lambda e: e.reciprocal(out=o, in_=i_)))(rec[:], pL[:, :]), reads=_keys([pL]), writes=_keys([rec]))
            kb.tt(yo[b][:], pO[:, :], rec[:], ALU.mult)
            kb.dma(sc["yT"][512 + h * 128:512 + (h + 1) * 128, tsl], yo[b][:], "yo%d" % b, w=[("yTa", it, h)], eng="pool")


Model.ab_moba = _ab_moba
Model._ab_moba_body = _ab_moba_body


def _outproj(self, l, wname):
    with self.kb.phase("op%d" % l):
        self._outproj_body(l, wname)


def _outproj_body(self, l, wname):
    kb = self.kb
    S = self.S
    NT = S // 512
    PS = self.PS
    j = l // 2
    wo = kb.sb("wo", [128, 8, D], BF16)
    tmp = [kb.sb("wtmp%d" % i, [128, 1024], F32) for i in range(2)]
    self.load_win(self.inp[wname][j], D, wo, tmp)
    xt_b = [kb.sb("x%d" % i, [128, 8, 512], F32) for i in range(2)]
    yt_b = [kb.sb("y%d" % i, [128, 8, 512], BF16) for i in range(2)]
    xTv = self.xT.rearrange("(c p) t -> p c t", p=128)
    yTv = self.scr["yT"].rearrange("(c p) t -> p c t", p=128)
    for it in range(NT):
        b = it % 2
        tsl = slice(it * 512, (it + 1) * 512)
        kb.dma(xt_b[b][:], xTv[:, :, tsl], "x%d" % b, r=[("xT", it)])
        kb.dma(yt_b[b][:], yTv[:, :, tsl], "y%d" % b, r=[("yTm", it)] + [("yTa", it, h) for h in range(4)])
        for dc in range(8):
            ps = PS[dc % 4]
            for e_ in range(8):
                kb.mm(ps[:, :], wo[:, e_, dc * 128:(dc + 1) * 128], yt_b[b][:, e_, :], start=(e_ == 0), stop=(e_ == 7))
            kb.tt(xt_b[b][:, dc, :], xt_b[b][:, dc, :], ps[:, :], ALU.add)
        kb.dma(xTv[:, :, tsl], xt_b[b][:], "xo%d" % b, w=[("xT", it)], eng="pool")


Model.outproj = _outproj
Model._outproj_body = _outproj_body


def _cd_setup(self, j):
    nc = self.nc
    if "cd_w_in" in self.inp:
        return
    NO = max(self.depth // 2, 1)

    def ein(name, shape):
        if name not in self.inp:
            self.inp[name] = nc.dram_tensor(name, list(shape), F32, kind="ExternalInput").ap()
    ein("cd_w_in", [NO, D, CD_IN]); ein("cd_w_out", [NO, D, D])
    ein("ssm_conv_w", [NO, 4, 1024]); ein("ssm_conv_b", [NO, 1024])
    ein("ssm_dt_bias", [NO, 8]); ein("ssm_A_log", [NO, 8]); ein("ssm_D", [NO, 8]); ein("ssm_norm", [NO, 512])
    ein("rwkv_mu", [NO, 1792]); ein("rwkv_w0", [NO, 512]); ein("rwkv_w2", [NO, 64, 512])
    ein("rwkv_a0", [NO, 512]); ein("rwkv_a2", [NO, 64, 512]); ein("rwkv_g2", [NO, 128, 512])
    ein("rwkv_k_k", [NO, 512]); ein("rwkv_k_a", [NO, 512]); ein("rwkv_r_k", [NO, 8, 64])
    ein("rwkv_ln_g", [NO, 512]); ein("rwkv_ln_b", [NO, 512])
    ein("triu", [128, 128]); ein("tril_s", [128, 128])
    S = self.S
    kb = self.kb
    view = self.view
    self.scr.update(
        xbcT=view(0, [1024, S], BF16), sz=view(1024, [S, 512], BF16), gg=view(1536, [S, 512], BF16),
        rkv=view(2048, [S, 1536], F32), dtt=view(5120, [S, 8], F32),
        lw=self.out[:, 0:512], aa=self.out[:, 512:1024])


Model.cd_setup = _cd_setup


def _cd_all(self, l):
    stop = int(os.environ.get("CD_STOP", "9"))
    with self.kb.phase("cdin%d" % l):
        self._cd_inproj_body(l)
    if stop >= 2:
        with self.kb.phase("cds%d" % l):
            self._cd_ssd_body(l)
    if stop >= 3:
        with self.kb.phase("cdr%d" % l):
            self._cd_rwkv_body(l)
    if stop >= 4:
        self.outproj(l, "cd_w_out")


Model.cd_all = _cd_all


def _cd_inproj_body(self, l):
    kb = self.kb
    S = self.S
    j = l // 2
    NT = S // 512
    PS = self.PS
    I = self.inp
    sc = self.scr
    NA = 1544
    win = kb.sb("win", [128, 8, NA], BF16)
    w1 = kb.sb("w1", [128, 8, 1792], BF16)
    w2 = kb.sb("w2", [128, 8, 1792], BF16)
    tmp = [kb.sb("wtmp%d" % i, [128, 896], F32) for i in range(2)]
    self.load_win(I["cd_w_in"][j][:, 0:NA], NA, win, tmp)
    mu = kb.sb("mu", [128, 1792], F32)
    kb.dma(mu[:], I["rwkv_mu"][j:j + 1, :].broadcast_to([128, 1792]), "c1", slow=True)
    tm2 = kb.sb("tm2", [128, 896], F32)
    ci = 0
    for kc in range(8):
        for half in range(2):
            b = ci % 2
            ci += 1
            cs = slice(half * 896, (half + 1) * 896)
            kb.dma(tmp[b][:], I["cd_w_in"][j][kc * 128:(kc + 1) * 128, NA + half * 896:NA + (half + 1) * 896], "wtmp%d" % b)
            kb.tt(tm2[:], tmp[b][:], mu[:, cs], ALU.mult)
            kb.cp(w2[:, kc, cs], tm2[:], eng="pool")
            kb.tt(w1[:, kc, cs], tmp[b][:], tm2[:], ALU.subtract)
    cw = kb.sb("cw", [128, 8, 4], F32)
    cb = kb.sb("cb", [128, 8], F32)
    for k in range(4):
        kb.dma(cw[:, :, k], I["ssm_conv_w"][j, k, :].rearrange("(c p) -> p c", p=128), "c2", slow=True)
    kb.dma(cb[:], I["ssm_conv_b"][j].rearrange("(c p) -> p c", p=128), "c3", slow=True)
    dtb = kb.sb("dtb", [128, 8], F32)
    kb.dma(dtb[:], I["ssm_dt_bias"][j:j + 1, :].broadcast_to([128, 8]), "c4", slow=True)
    rep = {}
    for nm_ in ("rwkv_w0", "rwkv_a0"):
        rep[nm_] = kb.sb(nm_, [128, 512], F32)
        kb.dma(rep[nm_][:], I[nm_][j:j + 1, :].broadcast_to([128, 512]), "c5", slow=True)
    lw2 = kb.sb("lw2", [128, 512], F32)
    kb.memset(lw2[:], 0.0)
    la2 = kb.sb("la2", [128, 512], F32)
    kb.memset(la2[:], 0.0)
    lg2 = kb.sb("lg2", [128, 512], F32)
    kb.dma(lw2[0:64, :], I["rwkv_w2"][j], "c6")
    kb.dma(la2[64:128, :], I["rwkv_a2"][j], "c7")
    kb.dma(lg2[:], I["rwkv_g2"][j], "c8")
    xt = kb.sb("x", [128, 8, 512], F32)
    sq2 = [kb.sb("sq%d" % i, [128, 512], BF16) for i in range(2)]
    uT = kb.sb("uT", [128, 8, 516], BF16)
    rstd = kb.sb("rstd", [128, 512], F32)
    pq = [kb.sb("pq%d" % i, [128, 515], F32) for i in range(8)]
    acc = [kb.sb("acc%d" % i, [128, 512], F32) for i in range(2)]
    xbc = kb.sb("xbc", [128, 8, 512], BF16)
    tsz = kb.sb("tsz", [128, 4, 512], BF16)
    tdt = kb.sb("tdt", [128, 4, 8], F32)
    trkv = kb.sb("trkv", [128, 1536], F32)
    lora = kb.sb("lora", [128, 2, 512], F32)
    tl = [kb.sb("tl%d" % i, [128, 512], F32) for i in range(2)] + [kb.sb("tl2", [128, 512], BF16)]
    xTv = self.xT.rearrange("(c p) t -> p c t", p=128)
    for c in range(8):
        kb.memset(pq[c][:, 0:3], 0.0)
        kb.memset(uT[:, c, 0:4], 0.0)
    for it in range(NT):
        tsl = slice(it * 512, (it + 1) * 512)
        if it > 0:
            for c in range(8):
                kb.cp(uT[:, c, 0:4], uT[:, c, 512:516], eng="dve")
        kb.dma(xt[:], xTv[:, :, tsl], "x0", r=[("xT", it)])
        for c in range(8):
            kb.act(sq2[c % 2][:], xt[:, c, :], AF.Square)
            kb.mm(PS[0][:, :], self.ones_bf[:], sq2[c % 2][:], start=(c == 0), stop=(c == 7))
        self.rstd_from_sumsq(rstd[:], PS[0][:, :], 1.0 / D, EPS)
        for c in range(8):
            kb.stt(uT[:, c, 4:516], xt[:, c, :], self.gains[:, l * 3 + 2, c:c + 1], rstd[:], ALU.mult, ALU.mult, r=[("gains", l, 2)])
        cut = int(os.environ.get("CD_CUT", "9"))
        if it > 0 and cut <= 1:
            continue
        for c in range(8):
            ps = PS[1 + c % 2]
            col0 = 512 + c * 128
            for kc in range(8):
                kb.mm(ps[:, :], win[:, kc, col0:col0 + 128], uT[:, kc, 4:516], start=(kc == 0), stop=(kc == 7))
            if it > 0:
                kb.cp(pq[c][:, 0:3], pq[c][:, 512:515], eng="pool")
            kb.cp(pq[c][:, 3:515], ps[:, :], eng="act")
            a = acc[c % 2]
            kb.ts(a[:], pq[c][:, 0:512], cw[:, c, 0:1], ALU.mult, cb[:, c:c + 1], ALU.add)
            for k in range(1, 4):
                kb.stt(a[:], pq[c][:, k:k + 512], cw[:, c, k:k + 1], a[:], ALU.mult, ALU.add)
            kb.act(xbc[:, c, :], a[:], AF.Silu)
        kb.dma(sc["xbcT"].rearrange("(c p) t -> p c t", p=128)[:, :, tsl], xbc[:], "xbco", w=[("xbcT", it)], eng="pool")
        if it > 0 and cut <= 2:
            continue
        for c in range(2):
            ps = PS[3 + c]
            cs = slice(1536 + c * 128, 1536 + (c + 1) * 128)
            for kc in range(8):
                kb.mm(ps[:, :], w1[:, kc, cs], uT[:, kc, 4:516], start=(kc == 0), stop=False)
            for kc in range(8):
                kb.mm(ps[:, :], w2[:, kc, cs], uT[:, kc, 3:515], start=False, stop=(kc == 7))
        kb.act(lora[0:64, 0, :], PS[3][0:64, :], AF.Tanh)
        kb.cp(lora[64:128, 0, :], PS[3][64:128, :], eng="dve")
        kb.act(lora[:, 1, :], PS[4][:, :], AF.Sigmoid)
        if it > 0 and cut <= 3:
            continue
        for s in range(4):
            ssl = slice(s * 128, (s + 1) * 128)
            lt = [uT[:, kc, 4 + s * 128:4 + (s + 1) * 128] for kc in range(8)]
            lp = [uT[:, kc, 3 + s * 128:3 + (s + 1) * 128] for kc in range(8)]
            ps = PS[1]
            for kc in range(8):
                kb.mm(ps[:, :], lt[kc], win[:, kc, 0:512], start=(kc == 0), stop=(kc == 7))
            kb.act(tsz[:, s, :], ps[:, :], AF.Silu)
            ps = PS[2]
            for kc in range(8):
                kb.mm(ps[:, 0:8], lt[kc], win[:, kc, 1536:1544], start=(kc == 0), stop=(kc == 7))
            kb.tt(tdt[:, s, :], ps[:, 0:8], dtb[:], ALU.add)
            kb.act(tdt[:, s, :], tdt[:, s, :], AF.Exp)
            kb.act(tdt[:, s, :], tdt[:, s, :], AF.Ln, bias=1.0)
            if it > 0 and cut <= 4:
                continue
            for q in range(3):
                ps = PS[5 + q % 2]
                cs = slice(q * 512, (q + 1) * 512)
                for kc in range(8):
                    kb.mm(ps[:, :], lt[kc], w1[:, kc, cs], start=(kc == 0), stop=False)
                for kc in range(8):
                    kb.mm(ps[:, :], lp[kc], w2[:, kc, cs], start=False, stop=(kc == 7))
                kb.cp(trkv[:, cs], ps[:, :], eng="act" if q % 2 else "dve")
            kb.dma(sc["rkv"][it * 512 + s * 128:it * 512 + (s + 1) * 128, :], trkv[:], "rkvo", w=[("rkv", it, s)], eng="sp")
            if it > 0 and cut <= 5:
                continue
            kb.mm(PS[1][:, :], lora[:, 0, ssl], lw2[:])
            kb.tt(tl[0][:], PS[1][:, :], rep["rwkv_w0"][:], ALU.add)
            kb.act(tl[0][:], tl[0][:], AF.Sigmoid)
            kb.ts(tl[0][:], tl[0][:], -float(np.exp(-0.5)), ALU.mult)
            kb.dma(sc["lw"][it * 512 + s * 128:it * 512 + (s + 1) * 128, :], tl[0][:], "lwo", w=[("lw", it, s)], eng="sp")
            kb.mm(PS[2][:, :], lora[:, 0, ssl], la2[:])
            kb.tt(tl[1][:], PS[2][:, :], rep["rwkv_a0"][:], ALU.add)
            kb.act(tl[1][:], tl[1][:], AF.Sigmoid)
            kb.dma(sc["aa"][it * 512 + s * 128:it * 512 + (s + 1) * 128, :], tl[1][:], "aao", w=[("aa", it, s)], eng="sp")
            kb.mm(PS[5][:, :], lora[:, 1, ssl], lg2[:])
            kb.cp(tl[2][:], PS[5][:, :], eng="act")
            kb.dma(sc["gg"][it * 512 + s * 128:it * 512 + (s + 1) * 128, :], tl[2][:], "ggo", w=[("gg", it, s)], eng="sp")
        kb.dma(sc["sz"][tsl, :].rearrange("(s p) c -> p s c", p=128), tsz[:], "szo", w=[("sz", it)], eng="pool")
        kb.dma(sc["dtt"][tsl, :].rearrange("(s p) c -> p s c", p=128), tdt[:], "dtto", w=[("dtt", it)], eng="pool", slow=True)


Model._cd_inproj_body = _cd_inproj_body


def _cd_ssd_body(self, l):
    kb = self.kb
    S = self.S
    j = l // 2
    NT = S // 512
    PS = self.PS
    PSB = self.PSB
    I = self.inp
    sc = self.scr
    triu = kb.sb("triu", [128, 128], F32)
    trils = kb.sb("trils", [128, 128], F32)
    kb.dma(triu[:], I["triu"][:, :], "c1")
    kb.dma(trils[:], I["tril_s"][:, :], "c2")
    identb = kb.sb("identb", [128, 128], BF16)
    kb.cp(identb[:], self.ident[:])
    ones32 = kb.sb("ones32", [128, 128], F32)
    kb.memset(ones32[:], 1.0)
    Arep = kb.sb("Arep", [128, 8], F32)
    kb.dma(Arep[:], I["ssm_A_log"][j:j + 1, :].broadcast_to([128, 8]), "c3", slow=True)
    kb.act(Arep[:], Arep[:], AF.Exp)
    kb.ts(Arep[:], Arep[:], -1.0, ALU.mult)
    Drep = kb.sb("Drep", [128, 8], F32)
    kb.dma(Drep[:], I["ssm_D"][j:j + 1, :].broadcast_to([128, 8]), "c4", slow=True)
    nrep = kb.sb("nrep", [128, 512], F32)
    kb.dma(nrep[:], I["ssm_norm"][j:j + 1, :].broadcast_to([128, 512]), "c5", slow=True)
    H = [kb.sb("H%d" % g, [128, 256], F32) for g in range(2)]
    Hb = [kb.sb("Hb%d" % g, [128, 256], BF16) for g in range(2)]
    for g in range(2):
        kb.memset(H[g][:], 0.0)
        kb.memset(Hb[g][:], 0.0, eng="pool")
    xbc = [kb.sb("xbc%d" % i, [128, 8, 512], BF16) for i in range(2)]
    sz = [kb.sb("sz%d" % i, [128, 4, 512], BF16) for i in range(2)]
    dtt = [kb.sb("dtt%d" % i, [128, 4, 8], F32) for i in range(2)]
    a = kb.sb("a", [128, 8], F32)
    R = kb.sb("R", [128, 8, 128], F32)
    Lm = kb.sb("Lm", [128, 8, 128], F32)
    W = kb.sb("W", [128, 8, 128], BF16)
    CBm = [kb.sb("CBm%d" % g, [128, 128], F32) for g in range(2)]
    xtok = kb.sb("xtok", [128, 512], BF16)
    btok = kb.sb("btok", [128, 2, 128], BF16)
    et = kb.sb("et", [128, 8], F32)
    decs = kb.sb("decs", [128, 8], F32)
    decL = kb.sb("decL", [128, 8], F32)
    xdt = kb.sb("xdt", [128, 8, 64], BF16)
    xdd = kb.sb("xdd", [128, 8, 64], BF16)
    t1 = kb.sb("t1", [128, 8, 64], F32)
    yv = kb.sb("yv", [128, 8, 64], F32)
    junk = kb.sb("junk", [128, 256], F32)
    ss = kb.sb("ss", [128, 2], F32)
    yb = kb.sb("yb", [128, 512], F32)
    yT = [kb.sb("yT%d" % i, [128, 4, 512], BF16) for i in range(2)]
    for it in range(int(os.environ.get("SSD_T0", "0")), min(NT, int(os.environ.get("SSD_NT", "999")))):
        b = 0
        tsl = slice(it * 512, (it + 1) * 512)
        kb.dma(xbc[b][:], sc["xbcT"].rearrange("(c p) t -> p c t", p=128)[:, :, tsl], "xbc%d" % b, r=[("xbcT", it)])
        kb.dma(sz[b][:], sc["sz"][tsl, :].rearrange("(s p) c -> p s c", p=128), "sz%d" % b, r=[("sz", it)])
        kb.dma(dtt[b][:], sc["dtt"][tsl, :].rearrange("(s p) c -> p s c", p=128), "dtt%d" % b, r=[("dtt", it)], slow=True)
        scut = int(os.environ.get("SSD_CUT", "99"))
        for s in range(4):
            csl = slice(s * 128, (s + 1) * 128)
            dt = dtt[b][:, s, :]
            if it > 0 and scut <= 0:
                continue
            kb.tt(a[:], dt, Arep[:], ALU.mult)
            kb.tt(R[:], triu[:].unsqueeze(1).broadcast_to([128, 8, 128]), a[:].unsqueeze(2).broadcast_to([128, 8, 128]), ALU.mult)
            for hh in range(2):
                kb.mm(PS[hh][:, :], trils[:], R[:, hh * 4:(hh + 1) * 4, :].rearrange("p h t -> p (h t)"))
                kb.act(Lm[:, hh * 4:(hh + 1) * 4, :].rearrange("p h t -> p (h t)"), PS[hh][:, :], AF.Exp)
            if it > 0 and scut <= 1:
                continue
            kb.mm(PS[2][:, 0:8], triu[:], a[:])
            kb.mm(PS[2][:, 16:24], ones32[:], a[:])
            kb.act(et[:], PS[2][:, 0:8], AF.Exp)
            kb.act(decL[:], PS[2][:, 16:24], AF.Exp)
            kb.cp(decs[:], PS[2][:, 0:8])
            kb.tt(decs[:], PS[2][:, 16:24], decs[:], ALU.subtract)
            kb.act(decs[:], decs[:], AF.Exp)
            if it > 0 and scut <= 2:
                continue
            for g in range(2):
                kb.mm(PS[3][:, g * 128:(g + 1) * 128], xbc[b][:, 4 + g, csl], xbc[b][:, 6 + g, csl])
                kb.tt(CBm[g][:], PS[3][:, g * 128:(g + 1) * 128], triu[:], ALU.mult)
                kb.tt(W[:, g * 4:(g + 1) * 4, :], Lm[:, g * 4:(g + 1) * 4, :], CBm[g][:].unsqueeze(1).broadcast_to([128, 4, 128]), ALU.mult)
            if it > 0 and scut <= 3:
                continue
            for c in range(4):
                kb.tr(PSB[0][:, 0:128], xbc[b][:, c, csl], identb[:])
                kb.cp(xtok[:, c * 128:(c + 1) * 128], PSB[0][:, 0:128], eng="act")
            for g in range(2):
                kb.tr(PSB[1][:, 0:128], xbc[b][:, 4 + g, csl], identb[:])
                kb.cp(btok[:, g, :], PSB[1][:, 0:128], eng="act")
            if it > 0 and scut <= 4:
                continue
            xv = xtok[:].rearrange("p (h d) -> p h d", d=64)
            kb.tt(xdt[:], xv, dt.unsqueeze(2).broadcast_to([128, 8, 64]), ALU.mult)
            kb.tt(xdd[:], xdt[:], decs[:].unsqueeze(2).broadcast_to([128, 8, 64]), ALU.mult)
            for h in range(8):
                kb.mm(PS[4][:, h * 64:(h + 1) * 64], W[:, h, :], xdt[:, h, :])
            if it > 0 and scut <= 5:
                continue
            for g in range(2):
                kb.mm(PS[5][:, g * 256:(g + 1) * 256], xbc[b][:, 6 + g, csl], Hb[g][:])
            kb.tt(t1[:], PS[5][:, :].rearrange("p (h d) -> p h d", d=64), et[:].unsqueeze(2).broadcast_to([128, 8, 64]), ALU.mult)
            kb.tt(yv[:], t1[:], PS[4][:, :].rearrange("p (h d) -> p h d", d=64), ALU.add)
            if it > 0 and scut <= 6:
                continue
            for g in range(2):
                kb.mm(PS[6][:, g * 256:(g + 1) * 256], btok[:, g, :], xdd[:, g * 4:(g + 1) * 4, :].rearrange("p h d -> p (h d)"))
                Hv = H[g][:].rearrange("p (h d) -> p h d", d=64)
                kb.tt(Hv, Hv, decL[:, g * 4:(g + 1) * 4].unsqueeze(2).broadcast_to([128, 4, 64]), ALU.mult)
                kb.tt(H[g][:], H[g][:], PS[6][:, g * 256:(g + 1) * 256], ALU.add)
                kb.cp(Hb[g][:], H[g][:], eng="act")
            if it > 0 and scut <= 7:
                continue
            kb.tt(t1[:], xv, Drep[:].unsqueeze(2).broadcast_to([128, 8, 64]), ALU.mult)
            kb.tt(yv[:], yv[:], t1[:], ALU.add)
            yf = yv[:].rearrange("p h d -> p (h d)")
            kb.tt(yf, yf, sz[b][:, s, :], ALU.mult)
            for g in range(2):
                kb.act(junk[:], yf[:, g * 256:(g + 1) * 256], AF.Square, accum_out=ss[:, g:g + 1])
            self.rstd_lnexp(ss[:], ss[:], 1.0 / 256, self.epsc[:, 0:1])
            for g in range(2):
                kb.stt(yb[:, g * 256:(g + 1) * 256], yf[:, g * 256:(g + 1) * 256], ss[:, g:g + 1], nrep[:, g * 256:(g + 1) * 256], ALU.mult, ALU.mult)
            if it > 0 and scut <= 8:
                continue
            for c in range(4):
                kb.tr(PS[3][:, c * 128:(c + 1) * 128], yb[:, c * 128:(c + 1) * 128], self.ident[:])
            kb.cp(yT[b][:, :, csl], PS[3][:, :].rearrange("p (c t) -> p c t", t=128), eng="act")
        kb.dma(sc["yT"].rearrange("(c p) t -> p c t", p=128)[:, 0:4, tsl], yT[b][:], "yTo%d" % b, w=[("yTm", it)], eng="pool")


Model._cd_ssd_body = _cd_ssd_body


def _cd_rwkv_body(self, l):
    kb = self.kb
    S = self.S
    j = l // 2
    NT = S // 512
    PS = self.PS
    I = self.inp
    sc = self.scr
    HD = 64
    id32 = self.ident
    triu = kb.sb("triu", [128, 128], F32)
    trils = kb.sb("trils", [128, 128], F32)
    kb.dma(triu[:], I["triu"][:, :], "c1")
    kb.dma(trils[:], I["tril_s"][:, :], "c2")
    up_i = triu[0:64, 0:64]
    lo_s = trils[0:64, 0:64]
    up_s = kb.sb("up_s", [64, 64], F32)
    kb.tt(up_s[:], up_i, id32[0:64, 0:64], ALU.subtract)
    ones32 = kb.sb("ones32", [64, 1], F32)
    kb.memset(ones32[:], 1.0)
    rep = {}
    for nm_ in ("rwkv_k_k", "rwkv_k_a", "rwkv_ln_g", "rwkv_ln_b"):
        rep[nm_] = kb.sb(nm_, [64, 512], F32)
        kb.dma(rep[nm_][:], I[nm_][j:j + 1, :].broadcast_to([64, 512]), "c3", slow=True)
    rep["rwkv_r_k"] = kb.sb("rwkv_r_k", [64, 512], F32)
    kb.dma(rep["rwkv_r_k"][:], I["rwkv_r_k"][j:j + 1].rearrange("o h d -> o (h d)").broadcast_to([64, 512]), "c4", slow=True)
    eps24 = kb.sb("eps24", [64, 1], F32)
    kb.memset(eps24[:], 1e-24)
    epsln = kb.sb("epsln", [64, 1], F32)
    kb.memset(epsln[:], 64e-5)
    Sst = kb.sb("Sst", [64, 8, 64], F32)
    kb.memset(Sst[:], 0.0)

    def T(name, shape=(64, 512)):
        return kb.sb(name, list(shape), F32)

    def T2(name, shape=(64, 512)):
        return [T(name + "0", shape), T(name + "1", shape)]
    rkv = T2("rkv", (64, 1536)); lw = T2("lw"); aa = T2("aa")
    gg = [kb.sb("gg%d" % i, [64, 512], BF16) for i in range(2)]
    kk = T("kk"); kp = T("kp"); t1 = T("t1"); t2 = T("t2")
    Ep = T("Ep"); Em = T("Em"); Epx = T("Epx"); At = T("At"); Rt = T("Rt")
    X = T("X"); XT = T("XT"); ss = T("ss", (64, 8))
    Bt = T2("Bt"); Kt = T2("Kt"); bon = T2("bon", (64, 8)); PL = T2("PL", (64, 8))
    fmA = T2("fmA"); fmR = T2("fmR"); fmB = T2("fmB"); fmK = T2("fmK")
    Pm = T2("Pm"); Mrb = T2("Mrb"); Mak = T2("Mak"); Mrk = T2("Mrk")
    rhsu = T("rhsu"); U = T("U"); y = T("y"); yc = T("yc"); t1b = T("t1b"); t2b = T("t2b")
    mean = T("mean", (64, 8)); ssb = T("ssb", (64, 8)); yo = T("yo")
    yT = [kb.sb("yT%d" % i, [64, 8, 512], BF16) for i in range(2)]
    NCH = S // 64
    b3 = lambda ap: ap.rearrange("p (h d) -> p h d", d=64)
    bc = lambda ap: ap.unsqueeze(2).broadcast_to([64, 8, 64])
    mb = lambda m: m.unsqueeze(1).broadcast_to([64, 8, 64])
    hs = lambda t_, h: t_[:, h * 64:(h + 1) * 64]
    I64 = id32[0:64, 0:64]

    def genA(ch):
        p = ch % 2
        it, s8 = ch // 8, ch % 8
        rows = slice(ch * 64, (ch + 1) * 64)
        kb.dma(rkv[p][:], sc["rkv"][rows, :], "rkv%d" % p, r=[("rkv", it, s8 // 2)])
        kb.dma(lw[p][:], sc["lw"][rows, :], "lw%d" % p, r=[("lw", it, s8 // 2)])
        kb.dma(aa[p][:], sc["aa"][rows, :], "aa%d" % p, r=[("aa", it, s8 // 2)])
        kb.dma(gg[p][:], sc["gg"][rows, :], "gg%d" % p, r=[("gg", it, s8 // 2)])
        r_ = rkv[p][:, 0:512]; k_ = rkv[p][:, 512:1024]
        a_ = aa[p][:]
        kb.mm(PS[0][0:64, :], up_i, lw[p][:])
        for h in range(8):
            kb.mm(PS[1][0:64, h:h + 1], lw[p][:, h * 64:(h + 1) * 64], ones32[:])
        kb.tt(kk[:], k_, rep["rwkv_k_k"][:], ALU.mult)
        kb.tt(t1[:], kk[:], kk[:], ALU.mult, eng="pool")
        kb.red(ss[:], b3(t1[:]), ALU.add)
        yield
        kb.act(Ep[:], PS[0][0:64, :], AF.Exp)
        kb.act(Em[:], PS[0][0:64, :], AF.Exp, scale=-1.0)
        kb.tt(Epx[:], PS[0][0:64, :], lw[p][:], ALU.subtract)
        kb.act(Epx[:], Epx[:], AF.Exp)
        kb.act(PL[p][:], PS[1][0:64, 0:8], AF.Exp)
        self.rstd_lnexp(ss[:], ss[:], 1.0, eps24[:, 0:1])
        kb.tt(b3(kk[:]), b3(kk[:]), bc(ss[:]), ALU.mult)
        kb.stt(t2[:], a_, -1.0, rep["rwkv_k_a"][:], ALU.add, ALU.mult)
        kb.ts(t2[:], t2[:], 1.0, ALU.add)
        kb.tt(kp[:], k_, t2[:], ALU.mult)
        yield
        kb.stt(At[:], kk[:], -1.0, Epx[:], ALU.mult, ALU.mult)
        kb.tt(Rt[:], r_, Ep[:], ALU.mult, eng="pool")
        kb.tt(t1[:], kk[:], a_, ALU.mult, eng="pool")
        kb.tt(Bt[p][:], t1[:], Em[:], ALU.mult)
        kb.tt(Kt[p][:], kp[:], Em[:], ALU.mult, eng="pool")
        kb.tt(t2[:], r_, kp[:], ALU.mult, eng="pool")
        kb.tt(t2[:], t2[:], rep["rwkv_r_k"][:], ALU.mult, eng="pool")
        kb.red(bon[p][:], b3(t2[:]), ALU.add)
        yield
        for qi, (src, dst) in enumerate(((At, fmA[p]), (Rt, fmR[p]), (Bt[p], fmB[p]), (Kt[p], fmK[p]))):
            ps = PS[qi]
            for h in range(8):
                kb.tr(ps[0:64, h * 64:(h + 1) * 64], src[:, h * 64:(h + 1) * 64], I64)
        yield
        for qi, dst in enumerate((fmA[p], fmR[p], fmB[p], fmK[p])):
            kb.cp(dst[:], PS[qi][0:64, :], eng="act" if qi % 2 else "dve")
        yield
        for h in range(8):
            c = slice(h * 64, (h + 1) * 64)
            kb.mm(PS[0][0:64, c], hs(fmA[p], h), hs(fmB[p], h))
            kb.mm(PS[1][0:64, c], hs(fmB[p], h), hs(fmA[p], h))
        for h in range(8):
            c = slice(h * 64, (h + 1) * 64)
            kb.mm(PS[2][0:64, c], hs(fmB[p], h), hs(fmR[p], h))
            kb.mm(PS[3][0:64, c], hs(fmK[p], h), hs(fmA[p], h))
        yield
        kb.tt(b3(X[:]), b3(PS[0][0:64, :]), mb(lo_s), ALU.mult)
        kb.tt(b3(XT[:]), b3(PS[1][0:64, :]), mb(up_s[:]), ALU.mult)
        kb.tt(b3(Mrb[p][:]), b3(PS[2][0:64, :]), mb(up_i), ALU.mult)
        kb.tt(b3(Mak[p][:]), b3(PS[3][0:64, :]), mb(up_s[:]), ALU.mult)
        for h in range(8):
            c = slice(h * 64, (h + 1) * 64)
            kb.mm(PS[0][0:64, c], hs(fmK[p], h), hs(fmR[p], h))
        kb.tt(b3(Pm[p][:]), b3(XT[:]), mb(I64), ALU.add)
        yield
        kb.tt(b3(Mrk[p][:]), b3(PS[0][0:64, :]), mb(up_i), ALU.mult)
        for lev in range(5):
            for h in range(8):
                c = slice(h * 64, (h + 1) * 64)
                kb.mm(PS[1][0:64, c], hs(X, h), hs(XT, h))
                if lev < 4:
                    kb.mm(PS[2][0:64, c], hs(XT, h), hs(X, h))
            yield
            kb.cp(XT[:], PS[1][0:64, :], eng="act")
            if lev < 4:
                kb.cp(X[:], PS[2][0:64, :], eng="dve")
            else:
                for h in range(8):
                    c = slice(h * 64, (h + 1) * 64)
                    kb.tr(PS[2][0:64, c], hs(XT, h), I64)
                yield
                kb.cp(X[:], PS[2][0:64, :], eng="dve")
            for h in range(8):
                c = slice(h * 64, (h + 1) * 64)
                kb.mm(PS[3][0:64, c], hs(X, h), hs(Pm[p], h))
            yield
            kb.tt(Pm[p][:], Pm[p][:], PS[3][0:64, :], ALU.add)

    def genB(ch):
        p = ch % 2
        it, s8 = ch // 8, ch % 8
        v_ = rkv[p][:, 1024:1536]
        for h in range(8):
            c = slice(h * 64, (h + 1) * 64)
            kb.mm(PS[4][0:64, c], hs(fmA[p], h), Sst[:, h, :], start=True, stop=False)
            kb.mm(PS[4][0:64, c], hs(Mak[p], h), v_[:, c], start=False, stop=True)
        yield
        kb.cp(rhsu[:], PS[4][0:64, :], eng="act")
        for h in range(8):
            c = slice(h * 64, (h + 1) * 64)
            kb.mm(PS[5][0:64, c], hs(Pm[p], h), hs(rhsu, h))
        yield
        kb.cp(U[:], PS[5][0:64, :], eng="dve")
        for h in range(8):
            c = slice(h * 64, (h + 1) * 64)
            kb.mm(PS[6][0:64, c], hs(fmR[p], h), Sst[:, h, :], start=True, stop=False)
            kb.mm(PS[6][0:64, c], hs(Mrb[p], h), hs(U, h), start=False, stop=False)
            kb.mm(PS[6][0:64, c], hs(Mrk[p], h), v_[:, c], start=False, stop=True)
        for h in range(8):
            c = slice(h * 64, (h + 1) * 64)
            kb.mm(PS[4][0:64, c], hs(Bt[p], h), hs(U, h), start=True, stop=False)
            kb.mm(PS[4][0:64, c], hs(Kt[p], h), v_[:, c], start=False, stop=True)
        yield
        kb.cp(y[:], PS[6][0:64, :], eng="act")
        Sf = Sst[:].rearrange("p h d -> p (h d)")
        kb.tt(Sf, Sf, PS[4][0:64, :], ALU.add)
        kb.tt(Sst[:], Sst[:], bc(PL[p][:]), ALU.mult)
        kb.red(mean[:], b3(y[:]), ALU.add)
        kb.ts(mean[:], mean[:], 1.0 / 64, ALU.mult)
        kb.tt(b3(yc[:]), b3(y[:]), bc(mean[:]), ALU.subtract)
        kb.tt(t1b[:], yc[:], yc[:], ALU.mult, eng="pool")
        kb.red(ssb[:], b3(t1b[:]), ALU.add)
        yield
        self.rstd_lnexp(ssb[:], ssb[:], 1.0 / 64, epsln[:, 0:1])
        kb.tt(b3(yc[:]), b3(yc[:]), bc(ssb[:]), ALU.mult)
        kb.tt(yc[:], yc[:], rep["rwkv_ln_g"][:], ALU.mult)
        kb.tt(yc[:], yc[:], rep["rwkv_ln_b"][:], ALU.add)
        kb.tt(b3(t2b[:]), b3(v_), bc(bon[p][:]), ALU.mult, eng="pool")
        kb.tt(yc[:], yc[:], t2b[:], ALU.add)
        kb.tt(yo[:], yc[:], gg[p][:], ALU.mult)
        yield
        yb = yT[(ch // 8) % 2]
        for h in range(8):
            kb.tr(PS[5][0:64, h * 64:(h + 1) * 64], yo[:, h * 64:(h + 1) * 64], I64)
        yield
        kb.cp(yb[:, :, s8 * 64:(s8 + 1) * 64], PS[5][0:64, :].rearrange("p (h t) -> p h t", t=64), eng="act")
        if s8 == 7:
            tsl = slice(it * 512, (it + 1) * 512)
            kb.dma(sc["yT"][512:1024, tsl].rearrange("(h d) t -> d h t", d=64), yb[:], "yTo%d" % (it % 2),
                   w=[("yTa", it, h) for h in range(4)], eng="pool")

    def drain(g):
        for _ in g:
            pass
    mode = os.environ.get("RW_MODE", "1")
    if mode == "0":
        import itertools
        na = int(os.environ.get("RW_A", "99")); nb = int(os.environ.get("RW_B", "99"))
        for ch in range(NCH):
            drain(itertools.islice(genA(ch), na))
            drain(itertools.islice(genB(ch), nb))
    else:
        drain(genA(0))
    for ch in range(NCH if mode != "0" else 0):
        gs = [genB(ch)]
        if ch + 1 < NCH:
            gs.append(genA(ch + 1))
        while gs:
            for g in list(gs):
                try:
                    next(g)
                except StopIteration:
                    gs.remove(g)


Model._cd_rwkv_body = _cd_rwkv_body


_NC_CACHE = {}


def _get_model(S):
    if S not in _NC_CACHE:
        m = Model(S, depth=4, parts="fm")
        nc = m.build()
        _NC_CACHE[S] = (m, nc)
    return _NC_CACHE[S]


def kernel(**inputs):
    x = np.asarray(inputs["x"])
    B, S, _ = x.shape
    m, nc = _get_model(S)
    hc = host_consts()
    shared = {}
    for k in m.inp:
        if k == "x":
            continue
        if k in hc:
            shared[k] = hc[k]
        else:
            shared[k] = np.ascontiguousarray(np.asarray(inputs[k], dtype=np.float32))
    in_maps = []
    for b in range(B):
        d = dict(shared)
        d["x"] = np.ascontiguousarray(x[b].astype(np.float32))
        in_maps.append(d)
    res = run_bass_kernel_spmd(nc, in_maps, core_ids=list(range(B)))
    out = np.stack([np.asarray(res.results[b]["out"]) for b in range(B)], axis=0)
    return out.astype(np.float32)
```

```python
import contextlib
import os
import numpy as np
import concourse.bass as bass
import concourse.mybir as mybir
from concourse.bass_utils import run_bass_kernel_spmd

F32 = mybir.dt.float32
BF16 = mybir.dt.bfloat16
AF = mybir.ActivationFunctionType
ALU = mybir.AluOpType
AX = mybir.AxisListType

D = 1024
DFF = 2816
NF = DFF // 128
G = 512
AB_IN = 3592
CD_IN = 3336
EPS = 1e-6

ENGS = ("pe", "act", "dve", "pool", "sp")


class Prog:
    def __init__(self, nc, same_engine_sync=("act", "dve", "pool")):
        self.nc = nc
        self.ops = {e: [] for e in ENGS}
        self.cnt = {e: 0 for e in ENGS}
        self.last_w = {}
        self.readers = {}
        self.waited = {}
        self.dma_cnt = {}
        self.same_engine_sync = set(same_engine_sync)
        self.sem_names = []
        self.nops = 0

    def _need(self, eng, tok, waits):
        if tok is None:
            return
        sem, val, src = tok
        if src == eng and eng not in self.same_engine_sync:
            return
        k = (eng, sem)
        if self.waited.get(k, 0) >= val:
            return
        self.waited[k] = val
        waits.append((sem, val))

    def op(self, eng, fn, reads=(), writes=(), dma_slot=None):
        waits = []
        for k in reads:
            self._need(eng, self.last_w.get(k), waits)
            if isinstance(k, str) and k.startswith("ps"):
                for t in self.readers.get(k, ()):
                    if t[2] != eng:
                        self._need(eng, t, waits)
        for k in writes:
            self._need(eng, self.last_w.get(k), waits)
            for t in self.readers.get(k, ()):
                self._need(eng, t, waits)
        if dma_slot is None:
            self.cnt[eng] += 1
            sem = "c_" + eng
            tok = (sem, self.cnt[eng], eng)
            inc = (sem, 1)
        else:
            sem = "d_" + str(dma_slot)
            self.dma_cnt[sem] = self.dma_cnt.get(sem, 0) + 1
            tok = (sem, 16 * self.dma_cnt[sem], None)
            inc = (sem, 16)
        if sem not in self.sem_names:
            self.sem_names.append(sem)
        for k in writes:
            self.last_w[k] = tok
            self.readers[k] = []
        for k in reads:
            if k not in writes:
                lst = self.readers.setdefault(k, [])
                lst.append(tok)
                if len(lst) > 8:
                    best = {}
                    for t in lst:
                        if t[0] not in best or best[t[0]][1] < t[1]:
                            best[t[0]] = t
                    self.readers[k] = list(best.values())
        self.ops[eng].append((waits, fn, inc))
        self.nops += 1
        return tok

    def barrier(self):
        latest = {}
        for e in ENGS:
            if self.cnt[e]:
                latest["c_" + e] = (self.cnt[e], e)
        for sname, n in self.dma_cnt.items():
            latest[sname] = (16 * n, None)
        for e in ENGS:
            waits = []
            for sname, (v, src) in latest.items():
                if src == e and e not in self.same_engine_sync:
                    continue
                self._need(e, (sname, v, None), waits)
            if waits:
                self.ops[e].append((waits, None, None))

    def final_wait(self, eng, keys):
        waits = []
        for k in keys:
            self._need(eng, self.last_w.get(k), waits)
        self.ops[eng].append((waits, None, None))

    def emit(self):
        nc = self.nc
        with contextlib.ExitStack() as st:
            sems = {}
            for n in self.sem_names:
                sems[n] = st.enter_context(nc.semaphore(n))
            block = st.enter_context(nc.Block())

            def run(eng_name):
                def body(e):
                    for waits, fn, inc in self.ops[eng_name]:
                        for (s, v) in waits:
                            e.wait_ge(sems[s], v)
                        if fn is not None:
                            fn(e).then_inc(sems[inc[0]], inc[1])
                return body

            block.tensor(run("pe"))
            block.scalar(run("act"))
            block.vector(run("dve"))
            block.gpsimd(run("pool"))
            block.sync(run("sp"))


def _keys(lst):
    out = []
    for a in lst:
        if a is None:
            continue
        if isinstance(a, (str, tuple)):
            out.append(a)
        else:
            out.append(a.name)
    return out


class KB:
    def __init__(self, nc, st):
        self.nc = nc
        self.st = st
        self.p = Prog(nc)
        self.phase_st = None
        self.prefix = ""

    def sb(self, name, shape, dt=F32):
        st = self.phase_st if self.phase_st is not None else self.st
        return st.enter_context(self.nc.sbuf_tensor(self.prefix + name, list(shape), dt))

    @contextlib.contextmanager
    def phase(self, prefix):
        with contextlib.ExitStack() as ph:
            self.phase_st = ph
            self.prefix = prefix + "_"
            self.p.barrier()
            try:
                yield
            finally:
                self.phase_st = None
                self.prefix = ""

    def ps(self, name, shape=(128, 512), dt=F32):
        return self.st.enter_context(self.nc.psum_tensor(name, list(shape), dt))

    def dram(self, name, shape, dt=F32, kind="Internal"):
        return self.nc.dram_tensor(name, list(shape), dt, kind=kind).ap()

    def mm(self, out, lhsT, rhs, start=True, stop=True, r=(), w=()):
        self.p.op("pe", lambda e: e.matmul(out, lhsT=lhsT, rhs=rhs, start=start, stop=stop),
                  reads=_keys([lhsT, rhs]) + _keys(r), writes=_keys([out]) + _keys(w))

    def tr(self, out, in_, ident):
        self.p.op("pe", lambda e: e.transpose(out, in_, ident),
                  reads=_keys([in_, ident]), writes=_keys([out]))

    def act(self, out, in_, func, bias=None, scale=1.0, accum_out=None, r=(), w=()):
        def fn(e):
            kw = {}
            if bias is not None:
                kw["bias"] = bias
            if accum_out is not None:
                kw["accum_out"] = accum_out
            return e.activation(out=out, in_=in_, func=func, scale=scale, **kw)
        rd = [in_] + [a for a in (bias, scale) if not isinstance(a, (int, float)) and a is not None]
        self.p.op("act", fn, reads=_keys(rd) + _keys(r), writes=_keys([out, accum_out]) + _keys(w))

    def ts(self, out, in0, s1, op0, s2=None, op1=None, eng="dve", accum_out=None, r=(), w=()):
        def fn(e):
            kw = {}
            if op1 is not None:
                kw["op1"] = op1
            if accum_out is not None:
                kw["accum_out"] = accum_out
            return e.tensor_scalar(out=out, in0=in0, scalar1=s1, scalar2=s2, op0=op0, **kw)
        rd = [in0] + [a for a in (s1, s2) if not isinstance(a, (int, float)) and a is not None]
        self.p.op(eng, fn, reads=_keys(rd) + _keys(r), writes=_keys([out, accum_out]) + _keys(w))

    def tt(self, out, in0, in1, op, eng="dve", r=(), w=()):
        self.p.op(eng, lambda e: e.tensor_tensor(out=out, in0=in0, in1=in1, op=op),
                  reads=_keys([in0, in1]) + _keys(r), writes=_keys([out]) + _keys(w))

    def stt(self, out, in0, scalar, in1, op0, op1, eng="dve", r=(), w=()):
        rd = [in0, in1] + ([scalar] if not isinstance(scalar, (int, float)) else [])
        self.p.op(eng, lambda e: e.scalar_tensor_tensor(out=out, in0=in0, scalar=scalar, in1=in1, op0=op0, op1=op1),
                  reads=_keys(rd) + _keys(r), writes=_keys([out]) + _keys(w))

    def cp(self, out, in_, eng="dve", r=(), w=()):
        if eng == "act":
            self.p.op("act", lambda e: e.copy(out=out, in_=in_), reads=_keys([in_]) + _keys(r), writes=_keys([out]) + _keys(w))
        else:
            self.p.op(eng, lambda e: e.tensor_copy(out=out, in_=in_), reads=_keys([in_]) + _keys(r), writes=_keys([out]) + _keys(w))

    def memset(self, ap, val, eng="dve"):
        self.p.op(eng, lambda e: e.memset(ap, val), reads=[], writes=_keys([ap]))

    def red(self, out, in_, op, axis=AX.X, eng="dve"):
        self.p.op(eng, lambda e: e.tensor_reduce(out=out, in_=in_, axis=axis, op=op),
                  reads=_keys([in_]), writes=_keys([out]))

    def dma(self, out, in_, slot, r=None, w=None, eng="sp", slow=False):
        rk = _keys(r) if r is not None else _keys([in_])
        wk = _keys(w) if w is not None else _keys([out])
        if slow:
            self.p.op(eng, lambda e: e.dma_start(out=out, in_=in_, allow_slow_non_contiguous=True), reads=rk, writes=wk, dma_slot=slot)
        else:
            self.p.op(eng, lambda e: e.dma_start(out=out, in_=in_), reads=rk, writes=wk, dma_slot=slot)


class Model:
    def __init__(self, S, depth=4, parts="fm", layers=None):
        self.layers = layers
        self.S = S
        self.depth = depth
        self.parts = parts

    def build(self):
        nc = bass.Bass("TRN2", target_bir_lowering=False)
        self.nc = nc
        S = self.S
        with contextlib.ExitStack() as st:
            kb = KB(nc, st)
            self.kb = kb
            L = self.depth
            NE, NO = (L + 1) // 2, L // 2
            self.inp = {}

            def ein(name, shape):
                self.inp[name] = nc.dram_tensor(name, list(shape), F32, kind="ExternalInput").ap()
                return self.inp[name]
            ein("x", [S, D])
            ein("ffn_norm", [L, 2, D]); ein("mix_norm", [L, D])
            ein("ffn_w_gate", [L, 2, D, DFF]); ein("ffn_w_up", [L, 2, D, DFF]); ein("ffn_w_down", [L, 2, DFF, D])
            self.out = nc.dram_tensor("out", [S, D], F32, kind="ExternalOutput").ap()
            self.xT = kb.dram("xT", [D, S])
            pool = kb.dram("pool", [5136 * S * int(os.environ.get("POOLX", "1"))], BF16)

            def view(off, shape, dt):
                n = shape[0] * shape[1] * (2 if dt == F32 else 1)
                v = pool[off * S: off * S + n]
                if dt == F32:
                    v = v.bitcast(F32)
                return v.rearrange("(r c) -> r c", c=shape[1])
            self.view = view
            self.scr = dict(
                qkmT=view(0, [1024, S], BF16), qkaT=view(1024, [1024, S], BF16),
                tmv=view(2048, [S, 1024], BF16), qa32=view(3072, [512, S], F32),
                tmo=view(4096, [S, 512], F32), tmg=view(5120, [S, 8], F32),
                yT=kb.dram("yT", [1024, S], BF16), tvec=kb.dram("tvec", [4, TV_N]))
            self.consts()
            self.kmean = kb.sb("kmean", [128, 4, max(S // 256, 8)], F32)
            if os.environ.get("POISON"):
                with kb.phase("poison"):
                    pz = kb.sb("pz", [128, 4096], BF16)
                    kb.memset(pz[:], float("nan"))
                    pv = pool.rearrange("(n p c) -> n p c", p=128, c=S // 8)
                    for i in range(pv.shape[0]):
                        kb.dma(pv[i], pz[:, 0:S // 8], "pz", w=[("poison", i)])
                    yv_ = self.scr["yT"].rearrange("(n p) (a c) -> n a p c", p=128, c=min(S, 4096))
                    for i in range(yv_.shape[0]):
                        for a_ in range(yv_.shape[1]):
                            kb.dma(yv_[i, a_], pz[:, 0:min(S, 4096)], "pz", w=[("poison", "y", i, a_)])
                    xv_ = self.xT.bitcast(BF16).rearrange("(n p) (a c) -> n a p c", p=128, c=4096)
                    for i in range(xv_.shape[0]):
                        for a_ in range(xv_.shape[1]):
                            kb.dma(xv_[i, a_], pz[:], "pz", w=[("poison", "x", i, a_)])
            self.phase_in()
            for l in range(L):
                if self.layers is not None and l not in self.layers:
                    continue
                if "f" in self.parts:
                    self.ffn(l, 0)
                if l % 2 == 0:
                    self.ab_setup(l // 2)
                    if "m" in self.parts:
                        ab_stop = int(os.environ.get("AB_STOP", "9"))
                        self.ab_inproj(l)
                        if ab_stop >= 2:
                            self.ab_mlstm(l)
                        if ab_stop >= 3:
                            self.ab_moba(l)
                        if ab_stop >= 4:
                            self.outproj(l, "ab_w_out")
                else:
                    if "m" in self.parts:
                        self.cd_setup(l // 2)
                        self.cd_all(l)
                if "f" in self.parts:
                    self.ffn(l, 1)
            self.phase_out()
            kb.p.final_wait("sp", [("out", i) for i in range(S // 512)])
            kb.p.emit()
        return nc

    def consts(self):
        kb = self.kb
        self.ones_bf = kb.sb("ones_bf", [128, 128], BF16)
        kb.memset(self.ones_bf[:], 1.0)
        self.epsc = kb.sb("epsc", [128, 1], F32)
        kb.memset(self.epsc[:], EPS)
        self.ident = kb.sb("ident", [128, 128], F32)
        identd = self.nc.dram_tensor("identd", [128, 128], F32, kind="ExternalInput").ap()
        self.inp["identd"] = identd
        kb.dma(self.ident[:], identd[:, :], "c0")
        L = self.depth
        self.gains = kb.sb("gains", [128, L * 3, 8], F32)
        for l in range(L):
            for j in range(2):
                kb.dma(self.gains[:, l * 3 + j, :], self.inp["ffn_norm"][l, j, :].rearrange("(c p) -> p c", p=128), "c0",
                       w=[("gains", l, j)], slow=True)
            kb.dma(self.gains[:, l * 3 + 2, :], self.inp["mix_norm"][l, :].rearrange("(c p) -> p c", p=128), "c0",
                   w=[("gains", l, 2)], slow=True)
        self.PS = [kb.ps("ps%d" % i) for i in range(7)]
        psb = kb.ps("psb", (128, 512), BF16)
        self.PSB = [psb[:, i * 256:(i + 1) * 256] for i in range(2)]

    def rstd_from_sumsq(self, out, ssq, scale, eps):
        kb = self.kb
        kb.act(out, ssq, AF.Sqrt, bias=self.epsc[:, 0:1] if eps == EPS else eps, scale=scale)
        kb.p.op("dve", lambda e: e.reciprocal(out=out, in_=out), reads=_keys([out]), writes=_keys([out]))

    def phase_in(self):
        with self.kb.phase("pin"):
            self._phase_in()

    def _phase_in(self):
        kb = self.kb
        S = self.S
        xin = [kb.sb("tin%d" % i, [128, 4, D], F32) for i in range(2)]
        xo = [kb.sb("tino%d" % i, [128, 8, 512], F32) for i in range(2)]
        x = self.inp["x"]
        xTv = self.xT.rearrange("(c p) t -> p c t", p=128)
        for it in range(S // 512):
            b = it % 2
            kb.dma(xin[b][:], x[it * 512:(it + 1) * 512, :].rearrange("(j p) d -> p j d", p=128), "tin%d" % b)
            for c in range(8):
                ps = self.PS[c % 2]
                for j in range(4):
                    kb.tr(ps[:, j * 128:(j + 1) * 128], xin[b][:, j, c * 128:(c + 1) * 128], self.ident[:])
                kb.cp(xo[b][:, c, :], ps[:, :], eng="act" if c % 2 else "dve")
            kb.dma(xTv[:, :, it * 512:(it + 1) * 512], xo[b][:], "tino%d" % b, w=[("xT", it)], eng="pool")

    def phase_out(self):
        with self.kb.phase("pout"):
            self._phase_out()

    def _phase_out(self):
        kb = self.kb
        S = self.S
        xin = [kb.sb("tout%d" % i, [128, 8, 512], F32) for i in range(2)]
        xo = [kb.sb("touto%d" % i, [128, 4, D], F32) for i in range(2)]
        xTv = self.xT.rearrange("(c p) t -> p c t", p=128)
        for it in range(S // 512):
            b = it % 2
            kb.dma(xin[b][:], xTv[:, :, it * 512:(it + 1) * 512], "tout%d" % b, r=[("xT", it)])
            for j in range(4):
                for h in range(2):
                    ps = self.PS[(j * 2 + h) % 2]
                    for c in range(4):
                        cc = h * 4 + c
                        kb.tr(ps[:, c * 128:(c + 1) * 128], xin[b][:, cc, j * 128:(j + 1) * 128], self.ident[:])
                    kb.cp(xo[b][:, j, h * 512:(h + 1) * 512], ps[:, :], eng="act" if h else "dve")
            kb.dma(self.out[it * 512:(it + 1) * 512, :].rearrange("(j p) d -> p j d", p=128), xo[b][:], "touto%d" % b,
                   w=[("out", it)], eng="pool")

    def ffn(self, l, j):
        with self.kb.phase("f%d%d" % (l, j)):
            self._ffn(l, j)

    def _ffn(self, l, j):
        kb = self.kb
        S = self.S
        nm = "f%d%d" % (l, j)
        NT = S // 512
        wg_in = self.inp["ffn_w_gate"][l, j]
        wu_in = self.inp["ffn_w_up"][l, j]
        wd_in = self.inp["ffn_w_down"][l, j]
        if not hasattr(self, "wgu_s"):
            self.wgu_s = kb.dram("wgu_s", [NF, 128, 2 * 8 * 128], BF16)
        wgu_s = self.wgu_s
        if True:
            self.ffn_bufs = dict(
                cin=[kb.sb("fcin%d" % i, [128, 1408], F32) for i in range(2)],
                cout=[kb.sb("fcout%d" % i, [128, 1408], BF16) for i in range(2)],
                wd=kb.sb("fwd", [128, NF, D], BF16),
                xt=[kb.sb("fx%d" % i, [128, 8, 512], F32) for i in range(2)],
                sq=kb.sb("fsq", [128, 8, 512], BF16),
                h=kb.sb("fh", [128, 8, 512], BF16),
                aT=kb.sb("faT", [128, NF, 512], BF16),
                rstd=kb.sb("frstd", [128, 512], F32),
                sg=[kb.sb("fsg%d" % i, [128, 512], F32) for i in range(2)],
                ring=[kb.sb("fring%d" % i, [128, 2, 8, 128], BF16) for i in range(4)],
                cnt=0,
            )
        fb = self.ffn_bufs
        ci = 0
        for mi, wsrc in enumerate((wg_in, wu_in)):
            for kc in range(8):
                for half in range(2):
                    b = ci % 2
                    ci += 1
                    kb.dma(fb["cin"][b][:], wsrc[kc * 128:(kc + 1) * 128, half * 1408:(half + 1) * 1408], "fcin%d" % b)
                    kb.cp(fb["cout"][b][:], fb["cin"][b][:], eng="pool" if ci % 2 else "dve")
                    dst = wgu_s[half * 11:(half + 1) * 11, :, (mi * 8 + kc) * 128:(mi * 8 + kc + 1) * 128].rearrange("f p m -> p f m")
                    kb.dma(dst, fb["cout"][b][:].rearrange("p (f m) -> p f m", m=128), "fcout%d" % b,
                           w=[("wgu", mi, kc, half)], eng="pool")
        for f in range(NF):
            b = ci % 2
            ci += 1
            kb.dma(fb["cin"][b][:, 0:D], wd_in[f * 128:(f + 1) * 128, :], "fcin%d" % b)
            kb.cp(fb["wd"][:, f, :], fb["cin"][b][:, 0:D], eng="pool" if ci % 2 else "dve")
        wgu_keys = [("wgu", mi, kc, half) for mi in range(2) for kc in range(8) for half in range(2)]
        xTv = self.xT.rearrange("(c p) t -> p c t", p=128)
        gi = l * 3 + j
        PS = self.PS
        for it in range(NT):
            xt = fb["xt"][it % 2]
            kb.dma(xt[:], xTv[:, :, it * 512:(it + 1) * 512], "fx%d" % (it % 2), r=[("xT", it)])
            for c in range(8):
                kb.act(fb["sq"][:, c, :], xt[:, c, :], AF.Square)
            for c in range(8):
                kb.mm(PS[0][:, :], self.ones_bf[:], fb["sq"][:, c, :], start=(c == 0), stop=(c == 7))
            self.rstd_from_sumsq(fb["rstd"][:], PS[0][:, :], 1.0 / D, EPS)
            for c in range(8):
                kb.stt(fb["h"][:, c, :], xt[:, c, :], self.gains[:, gi, c:c + 1], fb["rstd"][:], ALU.mult, ALU.mult,
                       r=[("gains", l, j)])
            for f in range(NF):
                rg = fb["ring"][fb["cnt"] % 4]
                slot = "fring%d" % (fb["cnt"] % 4)
                fb["cnt"] += 1
                kb.dma(rg[:].rearrange("p a k m -> p (a k m)"), wgu_s[f, :, :], slot, r=wgu_keys)
                pg = PS[1 + f % 2]
                pu = PS[3 + f % 2]
                for kc in range(8):
                    kb.mm(pg[:, :], rg[:, 0, kc, :], fb["h"][:, kc, :], start=(kc == 0), stop=(kc == 7))
                for kc in range(8):
                    kb.mm(pu[:, :], rg[:, 1, kc, :], fb["h"][:, kc, :], start=(kc == 0), stop=(kc == 7))
                sg = fb["sg"][f % 2]
                kb.act(sg[:], pg[:, :], AF.Silu)
                kb.tt(fb["aT"][:, f, :], sg[:], pu[:, :], ALU.mult)
            for dc in range(8):
                pd = PS[5 + dc % 2]
                for f in range(NF):
                    kb.mm(pd[:, :], fb["wd"][:, f, dc * 128:(dc + 1) * 128], fb["aT"][:, f, :], start=(f == 0), stop=(f == NF - 1))
                kb.stt(xt[:, dc, :], pd[:, :], 0.5, xt[:, dc, :], ALU.mult, ALU.add)
            kb.dma(xTv[:, :, it * 512:(it + 1) * 512], xt[:], "fxo%d" % (it % 2), w=[("xT", it)], eng="pool")


def t5_bucket_np(rel):
    n = np.maximum(rel, 0)
    nf = np.maximum(n, 1).astype(np.float32)
    large = 16 + (np.log(nf / np.float32(16)) / np.float32(np.log(64.0)) * np.float32(16)).astype(np.int32)
    large = np.minimum(large, 31)
    return np.where(n < 16, n, large)


TV_LO = -511
TV_N = 2048


def host_consts():
    c = {}
    c["identd"] = np.eye(128, dtype=np.float32)
    j = np.arange(128)
    c["triu"] = (j[:, None] <= j[None, :]).astype(np.float32)
    c["antiid"] = np.eye(128, dtype=np.float32)[::-1].copy()
    c["tril_s"] = (j[:, None] > j[None, :]).astype(np.float32)
    rel = np.arange(TV_N) + TV_LO
    oh = np.zeros((33, TV_N), np.float32)
    b = t5_bucket_np(rel)
    for i in range(TV_N):
        if rel[i] >= 0:
            oh[b[i], i] = 1.0
        else:
            oh[32, i] = 1.0
    c["onehot"] = oh
    return c


def _ab_setup(self, j):
    nc = self.nc
    if "ab_w_in" in self.inp:
        return
    NE = (self.depth + 1) // 2

    def ein(name, shape):
        self.inp[name] = nc.dram_tensor(name, list(shape), F32, kind="ExternalInput").ap()
    ein("ab_w_in", [NE, D, AB_IN]); ein("ab_w_out", [NE, D, D])
    ein("mlstm_conv_w", [NE, 4, 1024]); ein("mlstm_conv_b", [NE, 1024])
    ein("mlstm_b_i", [NE, 4]); ein("mlstm_b_f", [NE, 4]); ein("mlstm_h_norm", [NE, 512])
    ein("moba_q_norm", [NE, 128]); ein("moba_k_norm", [NE, 128]); ein("rel_bias", [32, 4])
    ein("triu", [128, 128]); ein("antiid", [128, 128]); ein("onehot", [33, TV_N])


Model.ab_setup = _ab_setup


def _rmsnorm_tile(self, xt, uT, gi, keyg, ps, sq, rstd):
    kb = self.kb
    for c in range(8):
        kb.act(sq[:, c, :], xt[:, c, :], AF.Square)
    for c in range(8):
        kb.mm(ps[:, :], self.ones_bf[:], sq[:, c, :], start=(c == 0), stop=(c == 7))
    self.rstd_from_sumsq(rstd[:], ps[:, :], 1.0 / D, EPS)
    for c in range(8):
        kb.stt(uT[:, c, :], xt[:, c, :], self.gains[:, gi, c:c + 1], rstd[:], ALU.mult, ALU.mult, r=[keyg])


Model.rmsnorm_tile = _rmsnorm_tile


def _load_win(self, wsrc, ncols, win, tmp):
    kb = self.kb
    W = tmp[0].shape[1]
    ci = 0
    for kc in range(8):
        c0 = 0
        while c0 < ncols:
            w = min(W, ncols - c0)
            b = ci % 2
            ci += 1
            kb.dma(tmp[b][:, 0:w], wsrc[kc * 128:(kc + 1) * 128, c0:c0 + w], "wtmp%d" % b)
            kb.cp(win[:, kc, c0:c0 + w], tmp[b][:, 0:w], eng="pool" if ci % 2 else "dve")
            c0 += w


Model.load_win = _load_win


def _ab_inproj(self, l):
    with self.kb.phase("abin%d" % l):
        self._ab_inproj_body(l)


def _ab_inproj_body(self, l):
    kb = self.kb
    S = self.S
    j = l // 2
    NT = S // 512
    PS = self.PS
    I = self.inp
    sc = self.scr
    win = kb.sb("win", [128, 8, AB_IN], BF16)
    tmp = [kb.sb("wtmp%d" % i, [128, 900], F32) for i in range(2)]
    self.load_win(I["ab_w_in"][j], AB_IN, win, tmp)
    cw = kb.sb("cw", [128, 8, 4], F32)
    cb = kb.sb("cb", [128, 8], F32)
    for k in range(4):
        kb.dma(cw[:, :, k], I["mlstm_conv_w"][j, k, :].rearrange("(c p) -> p c", p=128), "c1", slow=True)
    kb.dma(cb[:], I["mlstm_conv_b"][j].rearrange("(c p) -> p c", p=128), "c2", slow=True)
    gq = kb.sb("gq", [128, 2], F32)
    kb.dma(gq[:, 0:1], I["moba_q_norm"][j].rearrange("(p o) -> p o", o=1), "c3", slow=True)
    kb.dma(gq[:, 1:2], I["moba_k_norm"][j].rearrange("(p o) -> p o", o=1), "c4", slow=True)
    kb.ts(gq[:, 0:1], gq[:, 0:1], 128 ** -0.5, ALU.mult)
    xt_b = [kb.sb("x%d" % i, [128, 8, 512], F32) for i in range(2)]
    sq = kb.sb("sq", [128, 8, 512], BF16)
    uT = kb.sb("uT", [128, 8, 512], BF16)
    rstd = kb.sb("rstd", [128, 512], F32)
    pq = [kb.sb("pq%d" % i, [128, 515], F32) for i in range(8)]
    acc = [kb.sb("acc%d" % i, [128, 512], F32) for i in range(2)]
    qkm = [kb.sb("qkm%d" % i, [128, 8, 512], BF16) for i in range(1)] * 2
    sqa = kb.sb("sqa", [128, 512], BF16)
    rs2 = kb.sb("rs2", [128, 512], F32)
    qab = [kb.sb("qab%d" % i, [128, 8, 512], BF16) for i in range(1)] * 2
    qa32 = [kb.sb("qa32_%d" % i, [128, 4, 512], F32) for i in range(1)] * 2
    kn32 = kb.sb("kn32", [128, 512], F32)
    tmv = [kb.sb("tmv%d" % i, [128, 4, 1024], BF16) for i in range(1)] * 2
    tmo = [kb.sb("tmo%d" % i, [128, 4, 512], F32) for i in range(1)] * 2
    tmg = [kb.sb("tmg%d" % i, [128, 4, 8], F32) for i in range(1)] * 2
    xTv = self.xT.rearrange("(c p) t -> p c t", p=128)
    for c in range(8):
        kb.memset(pq[c][:, 0:3], 0.0)
    for it in range(NT):
        b = it % 2
        xt = xt_b[b]
        kb.dma(xt[:], xTv[:, :, it * 512:(it + 1) * 512], "x%d" % b, r=[("xT", it)])
        self.rmsnorm_tile(xt, uT, l * 3 + 2, ("gains", l, 2), PS[0], sq, rstd)
        for c in range(8):
            ps = PS[1 + c % 2]
            col0 = c * 128
            for kc in range(8):
                kb.mm(ps[:, :], win[:, kc, col0:col0 + 128], uT[:, kc, :], start=(kc == 0), stop=(kc == 7))
            if it > 0:
                kb.cp(pq[c][:, 0:3], pq[c][:, 512:515], eng="pool")
            kb.cp(pq[c][:, 3:515], ps[:, :], eng="act")
            a = acc[c % 2]
            kb.ts(a[:], pq[c][:, 0:512], cw[:, c, 0:1], ALU.mult, cb[:, c:c + 1], ALU.add)
            for k in range(1, 4):
                kb.stt(a[:], pq[c][:, k:k + 512], cw[:, c, k:k + 1], a[:], ALU.mult, ALU.add)
            if c < 4:
                kb.act(a[:], a[:], AF.Silu)
                kb.ts(qkm[b][:, c, :], a[:], 128 ** -0.5, ALU.mult, eng="pool")
            else:
                kb.act(qkm[b][:, c, :], a[:], AF.Silu)
        kb.dma(sc["qkmT"].rearrange("(c p) t -> p c t", p=128)[:, :, it * 512:(it + 1) * 512], qkm[b][:], "qkmo%d" % b,
               w=[("qkmT", it)], eng="pool")
        for c in range(8):
            ps = PS[3 + c % 2]
            ps2 = PS[5 + c % 2]
            col0 = 2056 + c * 128
            for kc in range(8):
                kb.mm(ps[:, :], win[:, kc, col0:col0 + 128], uT[:, kc, :], start=(kc == 0), stop=(kc == 7))
            kb.act(sqa[:], ps[:, :], AF.Square)
            kb.mm(ps2[:, :], self.ones_bf[:], sqa[:])
            self.rstd_from_sumsq(rs2[:], ps2[:, :], 1.0 / 128, EPS)
            if c < 4:
                kb.stt(qa32[b][:, c, :], ps[:, :], gq[:, 0:1], rs2[:], ALU.mult, ALU.mult)
                kb.cp(qab[b][:, c, :], qa32[b][:, c, :], eng="pool")
            else:
                kb.stt(kn32[:], ps[:, :], gq[:, 1:2], rs2[:], ALU.mult, ALU.mult)
                kb.cp(qab[b][:, c, :], kn32[:], eng="pool")
                kb.red(self.kmean[:, c - 4, 2 * it:2 * it + 2], kn32[:].rearrange("p (n k) -> p n k", k=256), ALU.add)
        kb.dma(sc["qkaT"].rearrange("(c p) t -> p c t", p=128)[:, :, it * 512:(it + 1) * 512], qab[b][:], "qabo%d" % b,
               w=[("qkaT", it)], eng="pool")
        kb.dma(sc["qa32"].rearrange("(c p) t -> p c t", p=128)[:, :, it * 512:(it + 1) * 512], qa32[b][:], "qa32o%d" % b,
               w=[("qa32", it)], eng="pool")
        for s in range(4):
            lt = [uT[:, kc, s * 128:(s + 1) * 128] for kc in range(8)]
            for gi_, (col0, ncol) in enumerate(((1024, 512), (1536, 512), (2048, 8), (3080, 512))):
                ps = PS[1 + (s * 4 + gi_) % 4]
                for kc in range(8):
                    kb.mm(ps[:, 0:ncol], lt[kc], win[:, kc, col0:col0 + ncol], start=(kc == 0), stop=(kc == 7))
                if gi_ == 0:
                    kb.cp(tmv[b][:, s, 0:512], ps[:, 0:512], eng="act")
                elif gi_ == 1:
                    kb.act(tmo[b][:, s, :], ps[:, 0:512], AF.Sigmoid)
                elif gi_ == 2:
                    kb.cp(tmg[b][:, s, :], ps[:, 0:8], eng="dve")
                else:
                    kb.cp(tmv[b][:, s, 512:1024], ps[:, 0:512], eng="dve")
        tsl = slice(it * 512, (it + 1) * 512)
        kb.dma(sc["tmv"][tsl, :].rearrange("(s p) c -> p s c", p=128), tmv[b][:], "tmvo%d" % b, w=[("tmv", it)], eng="pool")
        kb.dma(sc["tmo"][tsl, :].rearrange("(s p) c -> p s c", p=128), tmo[b][:], "tmoo%d" % b, w=[("tmo", it)], eng="pool")
        kb.dma(sc["tmg"][tsl, :].rearrange("(s p) c -> p s c", p=128), tmg[b][:], "tmgo%d" % b, w=[("tmg", it)], eng="pool", slow=True)
    kb.ts(self.kmean[:], self.kmean[:], 1.0 / 256, ALU.mult)


Model.ab_inproj = _ab_inproj
Model._ab_inproj_body = _ab_inproj_body


def _ab_mlstm(self, l):
    with self.kb.phase("abm%d" % l):
        self._ab_mlstm_body(l)


def _ab_mlstm_body(self, l):
    kb = self.kb
    S = self.S
    j = l // 2
    NT = S // 512
    PS = self.PS
    I = self.inp
    sc = self.scr
    triu = kb.sb("triu", [128, 128], F32)
    kb.dma(triu[:], I["triu"][:, :], "c1")
    identb = kb.sb("identb", [128, 128], BF16)
    kb.cp(identb[:], self.ident[:])
    ones32 = kb.sb("ones32", [128, 128], F32)
    kb.memset(ones32[:], 1.0)
    bif = kb.sb("bif", [128, 8], F32)
    kb.dma(bif[:, 0:4], I["mlstm_b_i"][j:j + 1, :].broadcast_to([128, 4]), "c2", slow=True)
    kb.dma(bif[:, 4:8], I["mlstm_b_f"][j:j + 1, :].broadcast_to([128, 4]), "c3", slow=True)
    hn = kb.sb("hn", [128, 512], F32)
    kb.dma(hn[:], I["mlstm_h_norm"][j:j + 1, :].broadcast_to([128, 512]), "c4", slow=True)
    C = [kb.sb("C%d" % h, [128, 129], F32) for h in range(4)]
    Cb = [kb.sb("Cb%d" % h, [128, 129], BF16) for h in range(4)]
    for h in range(4):
        kb.memset(C[h][:], 0.0)
        kb.memset(Cb[h][:], 0.0, eng="pool")
    qk = [kb.sb("qk%d" % i, [128, 8, 512], BF16) for i in range(2)]
    vm = [kb.sb("vm%d" % i, [128, 4, 1024], BF16) for i in range(2)]
    so = [kb.sb("so%d" % i, [128, 4, 512], F32) for i in range(2)]
    gt = [kb.sb("gt%d" % i, [128, 4, 8], F32) for i in range(2)]
    ipre = kb.sb("ipre", [128, 4], F32)
    lf = kb.sb("lf", [128, 4], F32)
    acol = kb.sb("acol", [128, 4], F32)
    ccol = kb.sb("ccol", [128, 4], F32)
    dec = kb.sb("dec", [128, 4], F32)
    gate = kb.sb("gate", [128, 512], F32)
    vext = kb.sb("vext", [128, 4, 129], BF16)
    va = [kb.sb("va%d" % i, [128, 129], BF16) for i in range(2)]
    sm = [kb.sb("sm%d" % i, [128, 128], BF16) for i in range(2)]
    ktok = [kb.sb("ktok%d" % i, [128, 128], BF16) for i in range(2)]
    den = kb.sb("den", [128, 4], F32)
    den2 = kb.sb("den2", [128, 4], F32)
    hh = [kb.sb("hh%d" % i, [128, 128], F32) for i in range(2)]
    junk = kb.sb("junk", [128, 128], F32)
    ss = kb.sb("ss", [128, 4], F32)
    y = kb.sb("y", [128, 512], BF16)
    yT = [kb.sb("yT%d" % i, [128, 4, 512], BF16) for i in range(2)]
    PSB = self.PSB
    for it in range(NT):
        b = it % 2
        tsl = slice(it * 512, (it + 1) * 512)
        kb.dma(qk[b][:], sc["qkmT"].rearrange("(c p) t -> p c t", p=128)[:, :, tsl], "qk%d" % b, r=[("qkmT", it)])
        kb.dma(vm[b][:], sc["tmv"][tsl, :].rearrange("(s p) c -> p s c", p=128), "vm%d" % b, r=[("tmv", it)])
        kb.dma(so[b][:], sc["tmo"][tsl, :].rearrange("(s p) c -> p s c", p=128), "so%d" % b, r=[("tmo", it)])
        kb.dma(gt[b][:], sc["tmg"][tsl, :].rearrange("(s p) c -> p s c", p=128), "gt%d" % b, r=[("tmg", it)], slow=True)
        for s in range(4):
            csl = slice(s * 128, (s + 1) * 128)
            kb.tt(ipre[:], gt[b][:, s, 0:4], bif[:, 0:4], ALU.add)
            kb.tt(lf[:], gt[b][:, s, 4:8], bif[:, 4:8], ALU.add)
            kb.act(lf[:], lf[:], AF.Exp, scale=-1.0)
            kb.act(lf[:], lf[:], AF.Ln, bias=1.0)
            kb.ts(lf[:], lf[:], -1.0, ALU.mult)
            pg = PS[0]
            kb.mm(pg[:, 0:4], triu[:], lf[:])
            kb.mm(pg[:, 8:12], ones32[:], lf[:])
            kb.act(ccol[:], pg[:, 0:4], AF.Exp)
            kb.act(dec[:], pg[:, 8:12], AF.Exp)
            kb.tt(acol[:], ipre[:], pg[:, 0:4], ALU.subtract)
            kb.act(acol[:], acol[:], AF.Exp)
            kb.cp(vext[:, :, 0:128], vm[b][:, s, 0:512].rearrange("p (h d) -> p h d", d=128), eng="pool")
            kb.memset(vext[:, :, 128:129], 1.0, eng="pool")
            kb.tt(gate[:], so[b][:, s, :], hn[:], ALU.mult, eng="pool")
            for h in range(4):
                qT = qk[b][:, h, csl]
                kT = qk[b][:, 4 + h, csl]
                pS = PS[1 + h % 2]
                kb.mm(pS[:, 0:128], kT, qT)
                kb.tt(sm[h % 2][:], pS[:, 0:128], triu[:], ALU.mult)
                kb.ts(va[h % 2][:], vext[:, h, :], acol[:, h:h + 1], ALU.mult, eng="pool")
                pN = PS[3 + h % 2]
                kb.mm(pN[:, 0:129], sm[h % 2][:], va[h % 2][:], start=True, stop=False)
                kb.mm(pN[:, 0:129], qT, Cb[h][:], start=False, stop=True)
                pT = PSB[h % 2]
                kb.tr(pT[:, 0:128], kT, identb[:])
                kb.cp(ktok[h % 2][:], pT[:, 0:128], eng="act")
                pC = PS[5 + h % 2]
                kb.mm(pC[:, 0:129], ktok[h % 2][:], va[h % 2][:])
                kb.tt(C[h][:], C[h][:], pC[:, 0:129], ALU.add)
                kb.ts(C[h][:], C[h][:], dec[:, h:h + 1], ALU.mult)
                kb.cp(Cb[h][:], C[h][:], eng="act")
                kb.ts(den[:, h:h + 1], pN[:, 128:129], ccol[:, h:h + 1], ALU.mult)
                kb.ts(den2[:, h:h + 1], den[:, h:h + 1], -1.0, ALU.mult)
                kb.tt(den[:, h:h + 1], den[:, h:h + 1], den2[:, h:h + 1], ALU.max)
                kb.ts(den[:, h:h + 1], den[:, h:h + 1], 1.0, ALU.max)
                kb.p.op("dve", (lambda o: (lambda e: e.reciprocal(out=o, in_=o)))(den[:, h:h + 1]), reads=_keys([den]), writes=_keys([den]))
                kb.tt(den[:, h:h + 1], den[:, h:h + 1], ccol[:, h:h + 1], ALU.mult)
                kb.act(hh[h % 2][:], pN[:, 0:128], AF.Copy, scale=den[:, h:h + 1])
                kb.act(junk[:], hh[h % 2][:], AF.Square, accum_out=ss[:, h:h + 1])
                self.rstd_from_sumsq(ss[:, h:h + 1], ss[:, h:h + 1], 1.0 / 128, EPS)
                kb.stt(y[:, h * 128:(h + 1) * 128], hh[h % 2][:], ss[:, h:h + 1], gate[:, h * 128:(h + 1) * 128], ALU.mult, ALU.mult)
            for h in range(4):
                pT = PSB[h % 2]
                kb.tr(pT[:, 0:128], y[:, h * 128:(h + 1) * 128], identb[:])
                kb.cp(yT[b][:, h, csl], pT[:, 0:128], eng="act" if h % 2 else "dve")
        kb.dma(sc["yT"].rearrange("(c p) t -> p c t", p=128)[:, 0:4, tsl], yT[b][:], "yTo%d" % b, w=[("yTm", it)], eng="pool")


Model.ab_mlstm = _ab_mlstm
Model._ab_mlstm_body = _ab_mlstm_body


def _rstd_lnexp(self, out, ssq, scale, eps_ap):
    kb = self.kb
    kb.act(out, ssq, AF.Ln, bias=eps_ap, scale=scale)
    kb.act(out, out, AF.Exp, scale=-0.5)


Model.rstd_lnexp = _rstd_lnexp


def _ab_moba(self, l):
    with self.kb.phase("aba%d" % l):
        self._ab_moba_body(l)


def _ab_moba_body(self, l):
    kb = self.kb
    S = self.S
    NT = S // 512
    NB = S // 256
    PS = self.PS
    PSB = self.PSB
    I = self.inp
    sc = self.scr
    rb = kb.sb("rb", [128, 128], F32)
    kb.memset(rb[:], 0.0)
    kb.memset(rb[32:64, 0:4], -30000.0)
    kb.dma(rb[0:32, 0:4], I["rel_bias"][:, :], "c1", slow=True)
    oh = kb.sb("oh", [128, TV_N], F32)
    kb.memset(oh[:], 0.0)
    kb.dma(oh[0:33, :], I["onehot"][:, :], "c2")
    tv = kb.sb("tv", [4, TV_N], F32)
    for q in range(TV_N // 512):
        kb.mm(PS[q][:, :], rb[:], oh[:, q * 512:(q + 1) * 512])
        kb.cp(tv[:, q * 512:(q + 1) * 512], PS[q][0:4, :])
    kb.dma(sc["tvec"][:, :], tv[:], "c3", w=["tvec"])
    b31 = kb.sb("b31", [128, 4], F32)
    kb.dma(b31[:], I["rel_bias"][31:32, :].broadcast_to([128, 4]), "c4", slow=True)
    antib = kb.sb("antib", [128, 128], BF16)
    anti32 = kb.sb("anti32", [128, 128], F32)
    kb.dma(anti32[:], I["antiid"][:, :], "c5")
    kb.cp(antib[:], anti32[:])
    ident_b = kb.sb("identb", [128, 128], BF16)
    kb.cp(ident_b[:], self.ident[:])
    sel = kb.sb("sel", [128, 32, 128], BF16)
    kb.memset(sel[:], 0.0)
    kb.cp(sel[0:32, :, :], self.ident[0:32, 0:32].unsqueeze(2).broadcast_to([32, 32, 128]))
    deltas = list(range(-384, 897, 128))
    toep = kb.sb("toep", [128, len(deltas), 512], BF16)
    ttmp = [kb.sb("ttmp%d" % i, [128, 512], F32) for i in range(2)]
    kT = kb.sb("kT", [128, S], BF16)
    V = kb.sb("V", [128, S // 128, 128], BF16)
    qT = [kb.sb("qT%d" % i, [128, 512], BF16) for i in range(2)]
    q32 = [kb.sb("q32_%d" % i, [128, 512], F32) for i in range(2)]
    g = kb.sb("g", [128, 32], F32)
    m8 = kb.sb("m8", [128, 8], F32)
    negm = kb.sb("negm", [128, 128], F32)
    kb.memset(negm[:], 0.0)
    negT = kb.sb("negT", [128, 512], BF16)
    kb.memset(negT[:], 0.0)
    pT = [kb.sb("pT%d" % i, [128, 512], BF16) for i in range(3)]
    rec = kb.sb("rec", [128, 512], F32)
    yo = [kb.sb("yo%d" % i, [128, 512], BF16) for i in range(2)]
    tvh = sc["tvec"].tensor
    for h in range(4):
        for di, dl in enumerate(deltas):
            off = h * TV_N + (dl - 127 - TV_LO)
            src = bass.AP(tensor=tvh, offset=off, ap=[[1, 128], [1, 512]])
            kb.dma(ttmp[di % 2][:], src, "ttmp%d" % (di % 2), r=["tvec"])
            kb.cp(toep[:, di, :], ttmp[di % 2][:], eng="pool" if di % 2 else "dve")
        kb.dma(kT[:], sc["qkaT"][(4 + h) * 128:(5 + h) * 128, :], "kT", r=[("qkaT", i) for i in range(NT)])
        kb.dma(V[:], sc["tmv"][:, 512 + h * 128:512 + (h + 1) * 128].rearrange("(n p) d -> p n d", p=128), "V",
               r=[("tmv", i) for i in range(NT)])
        for it in range(NT):
            b = it % 2
            tsl = slice(it * 512, (it + 1) * 512)
            kb.dma(qT[b][:], sc["qkaT"][h * 128:(h + 1) * 128, tsl], "qT%d" % b, r=[("qkaT", it)])
            kb.dma(q32[b][:], sc["qa32"][h * 128:(h + 1) * 128, tsl], "q32%d" % b, r=[("qa32", it)])
            for s in range(4):
                own = 2 * it + s // 2
                kb.memset(negm[:, 0:32], -30000.0)
                if own > 0:
                    kb.mm(PS[0][:, 0:NB], q32[b][:, s * 128:(s + 1) * 128], self.kmean[:, h, 0:NB], r=["kmean"])
                    kb.memset(g[:], -1e30)
                    kb.cp(g[:, 0:own], PS[0][:, 0:own])
                    kb.p.op("dve", (lambda o, i_: (lambda e: e.max(out=o, in_=i_)))(m8[:], g[:]), reads=_keys([g]), writes=_keys([m8]))
                    kb.ts(negm[:, 0:own], g[:, 0:own], m8[:, 2:3], ALU.is_ge, 30000.0, ALU.mult)
                    kb.ts(negm[:, 0:own], negm[:, 0:own], -30000.0, ALU.add)
                kb.memset(negm[:, own:own + 1], 0.0)
                kb.tr(PS[1][:, 0:128], negm[:], self.ident[:])
                kb.cp(negT[0:32, s * 128:(s + 1) * 128], PS[1][0:32, 0:128])
            nkt = 4 * (it + 1)
            pO = PS[5]
            pL = PS[6]
            def scores(kt):
                pS = PS[2 + kt % 3]
                dl = it * 512 - kt * 128
                near = dl <= 896
                kb.mm(pS[:, :], kT[:, kt * 128:(kt + 1) * 128], qT[b][:], start=True, stop=False)
                kb.mm(pS[:, :], sel[:, kt // 2, :], negT[:], start=False, stop=not near)
                if near:
                    kb.mm(pS[:, :], antib[:], toep[:, deltas.index(dl), :], start=False, stop=True)

            def softmax_pv(kt):
                pS = PS[2 + kt % 3]
                dl = it * 512 - kt * 128
                near = dl <= 896
                if near:
                    kb.act(pT[kt % 3][:], pS[:, :], AF.Exp)
                else:
                    kb.act(pT[kt % 3][:], pS[:, :], AF.Exp, bias=b31[:, h:h + 1])
                kb.mm(pO[:, :], V[:, kt, :], pT[kt % 3][:], start=(kt == 0), stop=(kt == nkt - 1))
                kb.mm(pL[:, :], self.ones_bf[:], pT[kt % 3][:], start=(kt == 0), stop=(kt == nkt - 1))
            scores(0)
            if nkt > 1:
                scores(1)
            for kt in range(nkt):
                if kt + 2 < nkt:
                    scores(kt + 2)
                softmax_pv(kt)
            kb.p.op("dve", (lambda o, i_: (lambda e: e.reciprocal(out=o, in_=i_)))(rec[:], pL[:, :]), reads=_keys([pL]), writes=_keys([rec]))
            kb.tt(yo[b][:], pO[:, :], rec[:], ALU.mult)
            kb.dma(sc["yT"][512 + h * 128:512 + (h + 1) * 128, tsl], yo[b][:], "yo%d" % b, w=[("yTa", it, h)], eng="pool")


Model.ab_moba = _ab_moba
Model._ab_moba_body = _ab_moba_body


def _outproj(self, l, wname):
    with self.kb.phase("op%d" % l):
        self._outproj_body(l, wname)


def _outproj_body(self, l, wname):
    kb = self.kb
    S = self.S
    NT = S // 512
    PS = self.PS
    j = l // 2
    wo = kb.sb("wo", [128, 8, D], BF16)
    tmp = [kb.sb("wtmp%d" % i, [128, 1024], F32) for i in range(2)]
    self.load_win(self.inp[wname][j], D, wo, tmp)
    xt_b = [kb.sb("x%d" % i, [128, 8, 512], F32) for i in range(2)]
    yt_b = [kb.sb("y%d" % i, [128, 8, 512], BF16) for i in range(2)]
    xTv = self.xT.rearrange("(c p) t -> p c t", p=128)
    yTv = self.scr["yT"].rearrange("(c p) t -> p c t", p=128)
    for it in range(NT):
        b = it % 2
        tsl = slice(it * 512, (it + 1) * 512)
        kb.dma(xt_b[b][:], xTv[:, :, tsl], "x%d" % b, r=[("xT", it)])
        kb.dma(yt_b[b][:], yTv[:, :, tsl], "y%d" % b, r=[("yTm", it)] + [("yTa", it, h) for h in range(4)])
        for dc in range(8):
            ps = PS[dc % 4]
            for e_ in range(8):
                kb.mm(ps[:, :], wo[:, e_, dc * 128:(dc + 1) * 128], yt_b[b][:, e_, :], start=(e_ == 0), stop=(e_ == 7))
            kb.tt(xt_b[b][:, dc, :], xt_b[b][:, dc, :], ps[:, :], ALU.add)
        kb.dma(xTv[:, :, tsl], xt_b[b][:], "xo%d" % b, w=[("xT", it)], eng="pool")


Model.outproj = _outproj
Model._outproj_body = _outproj_body


def _cd_setup(self, j):
    nc = self.nc
    if "cd_w_in" in self.inp:
        return
    NO = max(self.depth // 2, 1)

    def ein(name, shape):
        if name not in self.inp:
            self.inp[name] = nc.dram_tensor(name, list(shape), F32, kind="ExternalInput").ap()
    ein("cd_w_in", [NO, D, CD_IN]); ein("cd_w_out", [NO, D, D])
    ein("ssm_conv_w", [NO, 4, 1024]); ein("ssm_conv_b", [NO, 1024])
    ein("ssm_dt_bias", [NO, 8]); ein("ssm_A_log", [NO, 8]); ein("ssm_D", [NO, 8]); ein("ssm_norm", [NO, 512])
    ein("rwkv_mu", [NO, 1792]); ein("rwkv_w0", [NO, 512]); ein("rwkv_w2", [NO, 64, 512])
    ein("rwkv_a0", [NO, 512]); ein("rwkv_a2", [NO, 64, 512]); ein("rwkv_g2", [NO, 128, 512])
    ein("rwkv_k_k", [NO, 512]); ein("rwkv_k_a", [NO, 512]); ein("rwkv_r_k", [NO, 8, 64])
    ein("rwkv_ln_g", [NO, 512]); ein("rwkv_ln_b", [NO, 512])
    ein("triu", [128, 128]); ein("tril_s", [128, 128])
    S = self.S
    kb = self.kb
    view = self.view
    self.scr.update(
        xbcT=view(0, [1024, S], BF16), sz=view(1024, [S, 512], BF16), gg=view(1536, [S, 512], BF16),
        rkv=view(2048, [S, 1536], F32), dtt=view(5120, [S, 8], F32),
        lw=self.out[:, 0:512], aa=self.out[:, 512:1024])


Model.cd_setup = _cd_setup


def _cd_all(self, l):
    stop = int(os.environ.get("CD_STOP", "9"))
    with self.kb.phase("cdin%d" % l):
        self._cd_inproj_body(l)
    if stop >= 2:
        with self.kb.phase("cds%d" % l):
            self._cd_ssd_body(l)
    if stop >= 3:
        with self.kb.phase("cdr%d" % l):
            self._cd_rwkv_body(l)
    if stop >= 4:
        self.outproj(l, "cd_w_out")


Model.cd_all = _cd_all


def _cd_inproj_body(self, l):
    kb = self.kb
    S = self.S
    j = l // 2
    NT = S // 512
    PS = self.PS
    I = self.inp
    sc = self.scr
    NA = 1544
    win = kb.sb("win", [128, 8, NA], BF16)
    w1 = kb.sb("w1", [128, 8, 1792], BF16)
    w2 = kb.sb("w2", [128, 8, 1792], BF16)
    tmp = [kb.sb("wtmp%d" % i, [128, 896], F32) for i in range(2)]
    self.load_win(I["cd_w_in"][j][:, 0:NA], NA, win, tmp)
    mu = kb.sb("mu", [128, 1792], F32)
    kb.dma(mu[:], I["rwkv_mu"][j:j + 1, :].broadcast_to([128, 1792]), "c1", slow=True)
    tm2 = kb.sb("tm2", [128, 896], F32)
    ci = 0
    for kc in range(8):
        for half in range(2):
            b = ci % 2
            ci += 1
            cs = slice(half * 896, (half + 1) * 896)
            kb.dma(tmp[b][:], I["cd_w_in"][j][kc * 128:(kc + 1) * 128, NA + half * 896:NA + (half + 1) * 896], "wtmp%d" % b)
            kb.tt(tm2[:], tmp[b][:], mu[:, cs], ALU.mult)
            kb.cp(w2[:, kc, cs], tm2[:], eng="pool")
            kb.tt(w1[:, kc, cs], tmp[b][:], tm2[:], ALU.subtract)
    cw = kb.sb("cw", [128, 8, 4], F32)
    cb = kb.sb("cb", [128, 8], F32)
    for k in range(4):
        kb.dma(cw[:, :, k], I["ssm_conv_w"][j, k, :].rearrange("(c p) -> p c", p=128), "c2", slow=True)
    kb.dma(cb[:], I["ssm_conv_b"][j].rearrange("(c p) -> p c", p=128), "c3", slow=True)
    dtb = kb.sb("dtb", [128, 8], F32)
    kb.dma(dtb[:], I["ssm_dt_bias"][j:j + 1, :].broadcast_to([128, 8]), "c4", slow=True)
    rep = {}
    for nm_ in ("rwkv_w0", "rwkv_a0"):
        rep[nm_] = kb.sb(nm_, [128, 512], F32)
        kb.dma(rep[nm_][:], I[nm_][j:j + 1, :].broadcast_to([128, 512]), "c5", slow=True)
    lw2 = kb.sb("lw2", [128, 512], F32)
    kb.memset(lw2[:], 0.0)
    la2 = kb.sb("la2", [128, 512], F32)
    kb.memset(la2[:], 0.0)
    lg2 = kb.sb("lg2", [128, 512], F32)
    kb.dma(lw2[0:64, :], I["rwkv_w2"][j], "c6")
    kb.dma(la2[64:128, :], I["rwkv_a2"][j], "c7")
    kb.dma(lg2[:], I["rwkv_g2"][j], "c8")
    xt = kb.sb("x", [128, 8, 512], F32)
    sq2 = [kb.sb("sq%d" % i, [128, 512], BF16) for i in range(2)]
    uT = kb.sb("uT", [128, 8, 516], BF16)
    rstd = kb.sb("rstd", [128, 512], F32)
    pq = [kb.sb("pq%d" % i, [128, 515], F32) for i in range(8)]
    acc = [kb.sb("acc%d" % i, [128, 512], F32) for i in range(2)]
    xbc = kb.sb("xbc", [128, 8, 512], BF16)
    tsz = kb.sb("tsz", [128, 4, 512], BF16)
    tdt = kb.sb("tdt", [128, 4, 8], F32)
    trkv = kb.sb("trkv", [128, 1536], F32)
    lora = kb.sb("lora", [128, 2, 512], F32)
    tl = [kb.sb("tl%d" % i, [128, 512], F32) for i in range(2)] + [kb.sb("tl2", [128, 512], BF16)]
    xTv = self.xT.rearrange("(c p) t -> p c t", p=128)
    for c in range(8):
        kb.memset(pq[c][:, 0:3], 0.0)
        kb.memset(uT[:, c, 0:4], 0.0)
    for it in range(NT):
        tsl = slice(it * 512, (it + 1) * 512)
        if it > 0:
            for c in range(8):
                kb.cp(uT[:, c, 0:4], uT[:, c, 512:516], eng="dve")
        kb.dma(xt[:], xTv[:, :, tsl], "x0", r=[("xT", it)])
        for c in range(8):
            kb.act(sq2[c % 2][:], xt[:, c, :], AF.Square)
            kb.mm(PS[0][:, :], self.ones_bf[:], sq2[c % 2][:], start=(c == 0), stop=(c == 7))
        self.rstd_from_sumsq(rstd[:], PS[0][:, :], 1.0 / D, EPS)
        for c in range(8):
            kb.stt(uT[:, c, 4:516], xt[:, c, :], self.gains[:, l * 3 + 2, c:c + 1], rstd[:], ALU.mult, ALU.mult, r=[("gains", l, 2)])
        cut = int(os.environ.get("CD_CUT", "9"))
        if it > 0 and cut <= 1:
            continue
        for c in range(8):
            ps = PS[1 + c % 2]
            col0 = 512 + c * 128
            for kc in range(8):
                kb.mm(ps[:, :], win[:, kc, col0:col0 + 128], uT[:, kc, 4:516], start=(kc == 0), stop=(kc == 7))
            if it > 0:
                kb.cp(pq[c][:, 0:3], pq[c][:, 512:515], eng="pool")
            kb.cp(pq[c][:, 3:515], ps[:, :], eng="act")
            a = acc[c % 2]
            kb.ts(a[:], pq[c][:, 0:512], cw[:, c, 0:1], ALU.mult, cb[:, c:c + 1], ALU.add)
            for k in range(1, 4):
                kb.stt(a[:], pq[c][:, k:k + 512], cw[:, c, k:k + 1], a[:], ALU.mult, ALU.add)
            kb.act(xbc[:, c, :], a[:], AF.Silu)
        kb.dma(sc["xbcT"].rearrange("(c p) t -> p c t", p=128)[:, :, tsl], xbc[:], "xbco", w=[("xbcT", it)], eng="pool")
        if it > 0 and cut <= 2:
            continue
        for c in range(2):
            ps = PS[3 + c]
            cs = slice(1536 + c * 128, 1536 + (c + 1) * 128)
            for kc in range(8):
                kb.mm(ps[:, :], w1[:, kc, cs], uT[:, kc, 4:516], start=(kc == 0), stop=False)
            for kc in range(8):
                kb.mm(ps[:, :], w2[:, kc, cs], uT[:, kc, 3:515], start=False, stop=(kc == 7))
        kb.act(lora[0:64, 0, :], PS[3][0:64, :], AF.Tanh)
        kb.cp(lora[64:128, 0, :], PS[3][64:128, :], eng="dve")
        kb.act(lora[:, 1, :], PS[4][:, :], AF.Sigmoid)
        if it > 0 and cut <= 3:
            continue
        for s in range(4):
            ssl = slice(s * 128, (s + 1) * 128)
            lt = [uT[:, kc, 4 + s * 128:4 + (s + 1) * 128] for kc in range(8)]
            lp = [uT[:, kc, 3 + s * 128:3 + (s + 1) * 128] for kc in range(8)]
            ps = PS[1]
            for kc in range(8):
                kb.mm(ps[:, :], lt[kc], win[:, kc, 0:512], start=(kc == 0), stop=(kc == 7))
            kb.act(tsz[:, s, :], ps[:, :], AF.Silu)
            ps = PS[2]
            for kc in range(8):
                kb.mm(ps[:, 0:8], lt[kc], win[:, kc, 1536:1544], start=(kc == 0), stop=(kc == 7))
            kb.tt(tdt[:, s, :], ps[:, 0:8], dtb[:], ALU.add)
            kb.act(tdt[:, s, :], tdt[:, s, :], AF.Exp)
            kb.act(tdt[:, s, :], tdt[:, s, :], AF.Ln, bias=1.0)
            if it > 0 and cut <= 4:
                continue
            for q in range(3):
                ps = PS[5 + q % 2]
                cs = slice(q * 512, (q + 1) * 512)
                for kc in range(8):
                    kb.mm(ps[:, :], lt[kc], w1[:, kc, cs], start=(kc == 0), stop=False)
                for kc in range(8):
                    kb.mm(ps[:, :], lp[kc], w2[:, kc, cs], start=False, stop=(kc == 7))
                kb.cp(trkv[:, cs], ps[:, :], eng="act" if q % 2 else "dve")
            kb.dma(sc["rkv"][it * 512 + s * 128:it * 512 + (s + 1) * 128, :], trkv[:], "rkvo", w=[("rkv", it, s)], eng="sp")
            if it > 0 and cut <= 5:
                continue
            kb.mm(PS[1][:, :], lora[:, 0, ssl], lw2[:])
            kb.tt(tl[0][:], PS[1][:, :], rep["rwkv_w0"][:], ALU.add)
            kb.act(tl[0][:], tl[0][:], AF.Sigmoid)
            kb.ts(tl[0][:], tl[0][:], -float(np.exp(-0.5)), ALU.mult)
            kb.dma(sc["lw"][it * 512 + s * 128:it * 512 + (s + 1) * 128, :], tl[0][:], "lwo", w=[("lw", it, s)], eng="sp")
            kb.mm(PS[2][:, :], lora[:, 0, ssl], la2[:])
            kb.tt(tl[1][:], PS[2][:, :], rep["rwkv_a0"][:], ALU.add)
            kb.act(tl[1][:], tl[1][:], AF.Sigmoid)
            kb.dma(sc["aa"][it * 512 + s * 128:it * 512 + (s + 1) * 128, :], tl[1][:], "aao", w=[("aa", it, s)], eng="sp")
            kb.mm(PS[5][:, :], lora[:, 1, ssl], lg2[:])
            kb.cp(tl[2][:], PS[5][:, :], eng="act")
            kb.dma(sc["gg"][it * 512 + s * 128:it * 512 + (s + 1) * 128, :], tl[2][:], "ggo", w=[("gg", it, s)], eng="sp")
        kb.dma(sc["sz"][tsl, :].rearrange("(s p) c -> p s c", p=128), tsz[:], "szo", w=[("sz", it)], eng="pool")
        kb.dma(sc["dtt"][tsl, :].rearrange("(s p) c -> p s c", p=128), tdt[:], "dtto", w=[("dtt", it)], eng="pool", slow=True)


Model._cd_inproj_body = _cd_inproj_body


def _cd_ssd_body(self, l):
    kb = self.kb
    S = self.S
    j = l // 2
    NT = S // 512
    PS = self.PS
    PSB = self.PSB
    I = self.inp
    sc = self.scr
    triu = kb.sb("triu", [128, 128], F32)
    trils = kb.sb("trils", [128, 128], F32)
    kb.dma(triu[:], I["triu"][:, :], "c1")
    kb.dma(trils[:], I["tril_s"][:, :], "c2")
    identb = kb.sb("identb", [128, 128], BF16)
    kb.cp(identb[:], self.ident[:])
    ones32 = kb.sb("ones32", [128, 128], F32)
    kb.memset(ones32[:], 1.0)
    Arep = kb.sb("Arep", [128, 8], F32)
    kb.dma(Arep[:], I["ssm_A_log"][j:j + 1, :].broadcast_to([128, 8]), "c3", slow=True)
    kb.act(Arep[:], Arep[:], AF.Exp)
    kb.ts(Arep[:], Arep[:], -1.0, ALU.mult)
    Drep = kb.sb("Drep", [128, 8], F32)
    kb.dma(Drep[:], I["ssm_D"][j:j + 1, :].broadcast_to([128, 8]), "c4", slow=True)
    nrep = kb.sb("nrep", [128, 512], F32)
    kb.dma(nrep[:], I["ssm_norm"][j:j + 1, :].broadcast_to([128, 512]), "c5", slow=True)
    H = [kb.sb("H%d" % g, [128, 256], F32) for g in range(2)]
    Hb = [kb.sb("Hb%d" % g, [128, 256], BF16) for g in range(2)]
    for g in range(2):
        kb.memset(H[g][:], 0.0)
        kb.memset(Hb[g][:], 0.0, eng="pool")
    xbc = [kb.sb("xbc%d" % i, [128, 8, 512], BF16) for i in range(2)]
    sz = [kb.sb("sz%d" % i, [128, 4, 512], BF16) for i in range(2)]
    dtt = [kb.sb("dtt%d" % i, [128, 4, 8], F32) for i in range(2)]
    a = kb.sb("a", [128, 8], F32)
    R = kb.sb("R", [128, 8, 128], F32)
    Lm = kb.sb("Lm", [128, 8, 128], F32)
    W = kb.sb("W", [128, 8, 128], BF16)
    CBm = [kb.sb("CBm%d" % g, [128, 128], F32) for g in range(2)]
    xtok = kb.sb("xtok", [128, 512], BF16)
    btok = kb.sb("btok", [128, 2, 128], BF16)
    et = kb.sb("et", [128, 8], F32)
    decs = kb.sb("decs", [128, 8], F32)
    decL = kb.sb("decL", [128, 8], F32)
    xdt = kb.sb("xdt", [128, 8, 64], BF16)
    xdd = kb.sb("xdd", [128, 8, 64], BF16)
    t1 = kb.sb("t1", [128, 8, 64], F32)
    yv = kb.sb("yv", [128, 8, 64], F32)
    junk = kb.sb("junk", [128, 256], F32)
    ss = kb.sb("ss", [128, 2], F32)
    yb = kb.sb("yb", [128, 512], F32)
    yT = [kb.sb("yT%d" % i, [128, 4, 512], BF16) for i in range(2)]
    for it in range(int(os.environ.get("SSD_T0", "0")), min(NT, int(os.environ.get("SSD_NT", "999")))):
        b = 0
        tsl = slice(it * 512, (it + 1) * 512)
        kb.dma(xbc[b][:], sc["xbcT"].rearrange("(c p) t -> p c t", p=128)[:, :, tsl], "xbc%d" % b, r=[("xbcT", it)])
        kb.dma(sz[b][:], sc["sz"][tsl, :].rearrange("(s p) c -> p s c", p=128), "sz%d" % b, r=[("sz", it)])
        kb.dma(dtt[b][:], sc["dtt"][tsl, :].rearrange("(s p) c -> p s c", p=128), "dtt%d" % b, r=[("dtt", it)], slow=True)
        scut = int(os.environ.get("SSD_CUT", "99"))
        for s in range(4):
            csl = slice(s * 128, (s + 1) * 128)
            dt = dtt[b][:, s, :]
            if it > 0 and scut <= 0:
                continue
            kb.tt(a[:], dt, Arep[:], ALU.mult)
            kb.tt(R[:], triu[:].unsqueeze(1).broadcast_to([128, 8, 128]), a[:].unsqueeze(2).broadcast_to([128, 8, 128]), ALU.mult)
            for hh in range(2):
                kb.mm(PS[hh][:, :], trils[:], R[:, hh * 4:(hh + 1) * 4, :].rearrange("p h t -> p (h t)"))
                kb.act(Lm[:, hh * 4:(hh + 1) * 4, :].rearrange("p h t -> p (h t)"), PS[hh][:, :], AF.Exp)
            if it > 0 and scut <= 1:
                continue
            kb.mm(PS[2][:, 0:8], triu[:], a[:])
            kb.mm(PS[2][:, 16:24], ones32[:], a[:])
            kb.act(et[:], PS[2][:, 0:8], AF.Exp)
            kb.act(decL[:], PS[2][:, 16:24], AF.Exp)
            kb.cp(decs[:], PS[2][:, 0:8])
            kb.tt(decs[:], PS[2][:, 16:24], decs[:], ALU.subtract)
            kb.act(decs[:], decs[:], AF.Exp)
            if it > 0 and scut <= 2:
                continue
            for g in range(2):
                kb.mm(PS[3][:, g * 128:(g + 1) * 128], xbc[b][:, 4 + g, csl], xbc[b][:, 6 + g, csl])
                kb.tt(CBm[g][:], PS[3][:, g * 128:(g + 1) * 128], triu[:], ALU.mult)
                kb.tt(W[:, g * 4:(g + 1) * 4, :], Lm[:, g * 4:(g + 1) * 4, :], CBm[g][:].unsqueeze(1).broadcast_to([128, 4, 128]), ALU.mult)
            if it > 0 and scut <= 3:
                continue
            for c in range(4):
                kb.tr(PSB[0][:, 0:128], xbc[b][:, c, csl], identb[:])
                kb.cp(xtok[:, c * 128:(c + 1) * 128], PSB[0][:, 0:128], eng="act")
            for g in range(2):
                kb.tr(PSB[1][:, 0:128], xbc[b][:, 4 + g, csl], identb[:])
                kb.cp(btok[:, g, :], PSB[1][:, 0:128], eng="act")
            if it > 0 and scut <= 4:
                continue
            xv = xtok[:].rearrange("p (h d) -> p h d", d=64)
            kb.tt(xdt[:], xv, dt.unsqueeze(2).broadcast_to([128, 8, 64]), ALU.mult)
            kb.tt(xdd[:], xdt[:], decs[:].unsqueeze(2).broadcast_to([128, 8, 64]), ALU.mult)
            for h in range(8):
                kb.mm(PS[4][:, h * 64:(h + 1) * 64], W[:, h, :], xdt[:, h, :])
            if it > 0 and scut <= 5:
                continue
            for g in range(2):
                kb.mm(PS[5][:, g * 256:(g + 1) * 256], xbc[b][:, 6 + g, csl], Hb[g][:])
            kb.tt(t1[:], PS[5][:, :].rearrange("p (h d) -> p h d", d=64), et[:].unsqueeze(2).broadcast_to([128, 8, 64]), ALU.mult)
            kb.tt(yv[:], t1[:], PS[4][:, :].rearrange("p (h d) -> p h d", d=64), ALU.add)
            if it > 0 and scut <= 6:
                continue
            for g in range(2):
                kb.mm(PS[6][:, g * 256:(g + 1) * 256], btok[:, g, :], xdd[:, g * 4:(g + 1) * 4, :].rearrange("p h d -> p (h d)"))
                Hv = H[g][:].rearrange("p (h d) -> p h d", d=64)
                kb.tt(Hv, Hv, decL[:, g * 4:(g + 1) * 4].unsqueeze(2).broadcast_to([128, 4, 64]), ALU.mult)
                kb.tt(H[g][:], H[g][:], PS[6][:, g * 256:(g + 1) * 256], ALU.add)
                kb.cp(Hb[g][:], H[g][:], eng="act")
            if it > 0 and scut <= 7:
                continue
            kb.tt(t1[:], xv, Drep[:].unsqueeze(2).broadcast_to([128, 8, 64]), ALU.mult)
            kb.tt(yv[:], yv[:], t1[:], ALU.add)
            yf = yv[:].rearrange("p h d -> p (h d)")
            kb.tt(yf, yf, sz[b][:, s, :], ALU.mult)
            for g in range(2):
                kb.act(junk[:], yf[:, g * 256:(g + 1) * 256], AF.Square, accum_out=ss[:, g:g + 1])
            self.rstd_lnexp(ss[:], ss[:], 1.0 / 256, self.epsc[:, 0:1])
            for g in range(2):
                kb.stt(yb[:, g * 256:(g + 1) * 256], yf[:, g * 256:(g + 1) * 256], ss[:, g:g + 1], nrep[:, g * 256:(g + 1) * 256], ALU.mult, ALU.mult)
            if it > 0 and scut <= 8:
                continue
            for c in range(4):
                kb.tr(PS[3][:, c * 128:(c + 1) * 128], yb[:, c * 128:(c + 1) * 128], self.ident[:])
            kb.cp(yT[b][:, :, csl], PS[3][:, :].rearrange("p (c t) -> p c t", t=128), eng="act")
        kb.dma(sc["yT"].rearrange("(c p) t -> p c t", p=128)[:, 0:4, tsl], yT[b][:], "yTo%d" % b, w=[("yTm", it)], eng="pool")


Model._cd_ssd_body = _cd_ssd_body


def _cd_rwkv_body(self, l):
    kb = self.kb
    S = self.S
    j = l // 2
    NT = S // 512
    PS = self.PS
    I = self.inp
    sc = self.scr
    HD = 64
    id32 = self.ident
    triu = kb.sb("triu", [128, 128], F32)
    trils = kb.sb("trils", [128, 128], F32)
    kb.dma(triu[:], I["triu"][:, :], "c1")
    kb.dma(trils[:], I["tril_s"][:, :], "c2")
    up_i = triu[0:64, 0:64]
    lo_s = trils[0:64, 0:64]
    up_s = kb.sb("up_s", [64, 64], F32)
    kb.tt(up_s[:], up_i, id32[0:64, 0:64], ALU.subtract)
    ones32 = kb.sb("ones32", [64, 1], F32)
    kb.memset(ones32[:], 1.0)
    rep = {}
    for nm_ in ("rwkv_k_k", "rwkv_k_a", "rwkv_ln_g", "rwkv_ln_b"):
        rep[nm_] = kb.sb(nm_, [64, 512], F32)
        kb.dma(rep[nm_][:], I[nm_][j:j + 1, :].broadcast_to([64, 512]), "c3", slow=True)
    rep["rwkv_r_k"] = kb.sb("rwkv_r_k", [64, 512], F32)
    kb.dma(rep["rwkv_r_k"][:], I["rwkv_r_k"][j:j + 1].rearrange("o h d -> o (h d)").broadcast_to([64, 512]), "c4", slow=True)
    eps24 = kb.sb("eps24", [64, 1], F32)
    kb.memset(eps24[:], 1e-24)
    epsln = kb.sb("epsln", [64, 1], F32)
    kb.memset(epsln[:], 64e-5)
    Sst = kb.sb("Sst", [64, 8, 64], F32)
    kb.memset(Sst[:], 0.0)

    def T(name, shape=(64, 512)):
        return kb.sb(name, list(shape), F32)

    def T2(name, shape=(64, 512)):
        return [T(name + "0", shape), T(name + "1", shape)]
    rkv = T2("rkv", (64, 1536)); lw = T2("lw"); aa = T2("aa")
    gg = [kb.sb("gg%d" % i, [64, 512], BF16) for i in range(2)]
    kk = T("kk"); kp = T("kp"); t1 = T("t1"); t2 = T("t2")
    Ep = T("Ep"); Em = T("Em"); Epx = T("Epx"); At = T("At"); Rt = T("Rt")
    X = T("X"); XT = T("XT"); ss = T("ss", (64, 8))
    Xb = kb.sb("Xb", [64, 512], BF16); XTb = kb.sb("XTb", [64, 512], BF16); Pb = kb.sb("Pb", [64, 512], BF16)
    Bt = T2("Bt"); Kt = T2("Kt"); bon = T2("bon", (64, 8)); PL = T2("PL", (64, 8))
    fmA = T2("fmA"); fmR = T2("fmR"); fmB = T2("fmB"); fmK = T2("fmK")
    Pm = T2("Pm"); Mrb = T2("Mrb"); Mak = T2("Mak"); Mrk = T2("Mrk")
    rhsu = T("rhsu"); U = T("U"); y = T("y"); yc = T("yc"); t1b = T("t1b"); t2b = T("t2b")
    mean = T("mean", (64, 8)); ssb = T("ssb", (64, 8)); yo = T("yo")
    yT = [kb.sb("yT%d" % i, [64, 8, 512], BF16) for i in range(2)]
    NCH = S // 64
    b3 = lambda ap: ap.rearrange("p (h d) -> p h d", d=64)
    bc = lambda ap: ap.unsqueeze(2).broadcast_to([64, 8, 64])
    mb = lambda m: m.unsqueeze(1).broadcast_to([64, 8, 64])
    hs = lambda t_, h: t_[:, h * 64:(h + 1) * 64]
    I64 = id32[0:64, 0:64]

    def genA(ch):
        p = ch % 2
        it, s8 = ch // 8, ch % 8
        rows = slice(ch * 64, (ch + 1) * 64)
        kb.dma(rkv[p][:], sc["rkv"][rows, :], "rkv%d" % p, r=[("rkv", it, s8 // 2)])
        kb.dma(lw[p][:], sc["lw"][rows, :], "lw%d" % p, r=[("lw", it, s8 // 2)])
        kb.dma(aa[p][:], sc["aa"][rows, :], "aa%d" % p, r=[("aa", it, s8 // 2)])
        kb.dma(gg[p][:], sc["gg"][rows, :], "gg%d" % p, r=[("gg", it, s8 // 2)])
        r_ = rkv[p][:, 0:512]; k_ = rkv[p][:, 512:1024]
        a_ = aa[p][:]
        kb.mm(PS[0][0:64, :], up_i, lw[p][:])
        for h in range(8):
            kb.mm(PS[1][0:64, h:h + 1], lw[p][:, h * 64:(h + 1) * 64], ones32[:])
        kb.tt(kk[:], k_, rep["rwkv_k_k"][:], ALU.mult)
        kb.tt(t1[:], kk[:], kk[:], ALU.mult, eng="pool")
        kb.red(ss[:], b3(t1[:]), ALU.add)
        yield
        kb.act(Ep[:], PS[0][0:64, :], AF.Exp)
        kb.act(Em[:], PS[0][0:64, :], AF.Exp, scale=-1.0)
        kb.tt(Epx[:], PS[0][0:64, :], lw[p][:], ALU.subtract)
        kb.act(Epx[:], Epx[:], AF.Exp)
        kb.act(PL[p][:], PS[1][0:64, 0:8], AF.Exp)
        self.rstd_lnexp(ss[:], ss[:], 1.0, eps24[:, 0:1])
        kb.tt(b3(kk[:]), b3(kk[:]), bc(ss[:]), ALU.mult)
        kb.stt(t2[:], a_, -1.0, rep["rwkv_k_a"][:], ALU.add, ALU.mult)
        kb.ts(t2[:], t2[:], 1.0, ALU.add)
        kb.tt(kp[:], k_, t2[:], ALU.mult)
        yield
        kb.stt(At[:], kk[:], -1.0, Epx[:], ALU.mult, ALU.mult)
        kb.tt(Rt[:], r_, Ep[:], ALU.mult, eng="pool")
        kb.tt(t1[:], kk[:], a_, ALU.mult, eng="pool")
        kb.tt(Bt[p][:], t1[:], Em[:], ALU.mult)
        kb.tt(Kt[p][:], kp[:], Em[:], ALU.mult, eng="pool")
        kb.tt(t2[:], r_, kp[:], ALU.mult, eng="pool")
        kb.tt(t2[:], t2[:], rep["rwkv_r_k"][:], ALU.mult, eng="pool")
        kb.red(bon[p][:], b3(t2[:]), ALU.add)
        yield
        for qi, (src, dst) in enumerate(((At, fmA[p]), (Rt, fmR[p]), (Bt[p], fmB[p]), (Kt[p], fmK[p]))):
            ps = PS[qi]
            for h in range(8):
                kb.tr(ps[0:64, h * 64:(h + 1) * 64], src[:, h * 64:(h + 1) * 64], I64)
        yield
        for qi, dst in enumerate((fmA[p], fmR[p], fmB[p], fmK[p])):
            kb.cp(dst[:], PS[qi][0:64, :], eng="act" if qi % 2 else "dve")
        yield
        for h in range(8):
            c = slice(h * 64, (h + 1) * 64)
            kb.mm(PS[0][0:64, c], hs(fmA[p], h), hs(fmB[p], h))
            kb.mm(PS[1][0:64, c], hs(fmB[p], h), hs(fmA[p], h))
        for h in range(8):
            c = slice(h * 64, (h + 1) * 64)
            kb.mm(PS[2][0:64, c], hs(fmB[p], h), hs(fmR[p], h))
            kb.mm(PS[3][0:64, c], hs(fmK[p], h), hs(fmA[p], h))
        yield
        kb.tt(b3(Xb[:]), b3(PS[0][0:64, :]), mb(lo_s), ALU.mult)
        kb.tt(b3(XT[:]), b3(PS[1][0:64, :]), mb(up_s[:]), ALU.mult)
        kb.cp(XTb[:], XT[:], eng="pool")
        kb.tt(b3(Mrb[p][:]), b3(PS[2][0:64, :]), mb(up_i), ALU.mult)
        kb.tt(b3(Mak[p][:]), b3(PS[3][0:64, :]), mb(up_s[:]), ALU.mult)
        for h in range(8):
            c = slice(h * 64, (h + 1) * 64)
            kb.mm(PS[0][0:64, c], hs(fmK[p], h), hs(fmR[p], h))
        kb.tt(b3(Pm[p][:]), b3(XT[:]), mb(I64), ALU.add)
        kb.cp(Pb[:], Pm[p][:], eng="pool")
        yield
        kb.tt(b3(Mrk[p][:]), b3(PS[0][0:64, :]), mb(up_i), ALU.mult)
        for lev in range(5):
            for h in range(8):
                c = slice(h * 64, (h + 1) * 64)
                kb.mm(PS[1][0:64, c], hs(Xb, h), hs(XTb, h))
                kb.mm(PS[2][0:64, c], hs(XTb, h), hs(Xb, h))
            yield
            kb.cp(XTb[:], PS[1][0:64, :], eng="act")
            kb.cp(Xb[:], PS[2][0:64, :], eng="dve")
            for h in range(8):
                c = slice(h * 64, (h + 1) * 64)
                kb.mm(PS[3][0:64, c], hs(Xb, h), hs(Pb, h))
            yield
            kb.tt(Pm[p][:], Pm[p][:], PS[3][0:64, :], ALU.add)
            if lev < 4:
                kb.cp(Pb[:], Pm[p][:], eng="pool")

    def genB(ch):
        p = ch % 2
        it, s8 = ch // 8, ch % 8
        v_ = rkv[p][:, 1024:1536]
        for h in range(8):
            c = slice(h * 64, (h + 1) * 64)
            kb.mm(PS[4][0:64, c], hs(fmA[p], h), Sst[:, h, :], start=True, stop=False)
            kb.mm(PS[4][0:64, c], hs(Mak[p], h), v_[:, c], start=False, stop=True)
        yield
        kb.cp(rhsu[:], PS[4][0:64, :], eng="act")
        for h in range(8):
            c = slice(h * 64, (h + 1) * 64)
            kb.mm(PS[5][0:64, c], hs(Pm[p], h), hs(rhsu, h))
        yield
        kb.cp(U[:], PS[5][0:64, :], eng="dve")
        for h in range(8):
            c = slice(h * 64, (h + 1) * 64)
            kb.mm(PS[6][0:64, c], hs(fmR[p], h), Sst[:, h, :], start=True, stop=False)
            kb.mm(PS[6][0:64, c], hs(Mrb[p], h), hs(U, h), start=False, stop=False)
            kb.mm(PS[6][0:64, c], hs(Mrk[p], h), v_[:, c], start=False, stop=True)
        for h in range(8):
            c = slice(h * 64, (h + 1) * 64)
            kb.mm(PS[4][0:64, c], hs(Bt[p], h), hs(U, h), start=True, stop=False)
            kb.mm(PS[4][0:64, c], hs(Kt[p], h), v_[:, c], start=False, stop=True)
        yield
        kb.cp(y[:], PS[6][0:64, :], eng="act")
        Sf = Sst[:].rearrange("p h d -> p (h d)")
        kb.tt(Sf, Sf, PS[4][0:64, :], ALU.add)
        kb.tt(Sst[:], Sst[:], bc(PL[p][:]), ALU.mult)
        kb.red(mean[:], b3(y[:]), ALU.add)
        kb.ts(mean[:], mean[:], 1.0 / 64, ALU.mult)
        kb.tt(b3(yc[:]), b3(y[:]), bc(mean[:]), ALU.subtract)
        kb.tt(t1b[:], yc[:], yc[:], ALU.mult, eng="pool")
        kb.red(ssb[:], b3(t1b[:]), ALU.add)
        yield
        self.rstd_lnexp(ssb[:], ssb[:], 1.0 / 64, epsln[:, 0:1])
        kb.tt(b3(yc[:]), b3(yc[:]), bc(ssb[:]), ALU.mult)
        kb.tt(yc[:], yc[:], rep["rwkv_ln_g"][:], ALU.mult)
        kb.tt(yc[:], yc[:], rep["rwkv_ln_b"][:], ALU.add)
        kb.tt(b3(t2b[:]), b3(v_), bc(bon[p][:]), ALU.mult, eng="pool")
        kb.tt(yc[:], yc[:], t2b[:], ALU.add)
        kb.tt(yo[:], yc[:], gg[p][:], ALU.mult)
        yield
        yb = yT[(ch // 8) % 2]
        for h in range(8):
            kb.tr(PS[5][0:64, h * 64:(h + 1) * 64], yo[:, h * 64:(h + 1) * 64], I64)
        yield
        kb.cp(yb[:, :, s8 * 64:(s8 + 1) * 64], PS[5][0:64, :].rearrange("p (h t) -> p h t", t=64), eng="act")
        if s8 == 7:
            tsl = slice(it * 512, (it + 1) * 512)
            kb.dma(sc["yT"][512:1024, tsl].rearrange("(h d) t -> d h t", d=64), yb[:], "yTo%d" % (it % 2),
                   w=[("yTa", it, h) for h in range(4)], eng="pool")

    def drain(g):
        for _ in g:
            pass
    mode = os.environ.get("RW_MODE", "1")
    if mode == "0":
        import itertools
        na = int(os.environ.get("RW_A", "99")); nb = int(os.environ.get("RW_B", "99"))
        for ch in range(NCH):
            drain(itertools.islice(genA(ch), na))
            drain(itertools.islice(genB(ch), nb))
    else:
        drain(genA(0))
    for ch in range(NCH if mode != "0" else 0):
        gs = [genB(ch)]
        if ch + 1 < NCH:
            gs.append(genA(ch + 1))
        while gs:
            for g in list(gs):
                try:
                    next(g)
                except StopIteration:
                    gs.remove(g)


Model._cd_rwkv_body = _cd_rwkv_body


_NC_CACHE = {}


def _get_model(S):
    if S not in _NC_CACHE:
        m = Model(S, depth=4, parts="fm")
        nc = m.build()
        _NC_CACHE[S] = (m, nc)
    return _NC_CACHE[S]


def kernel(**inputs):
    x = np.asarray(inputs["x"])
    B, S, _ = x.shape
    m, nc = _get_model(S)
    hc = host_consts()
    shared = {}
    for k in m.inp:
        if k == "x":
            continue
        if k in hc:
            shared[k] = hc[k]
        else:
            shared[k] = np.ascontiguousarray(np.asarray(inputs[k], dtype=np.float32))
    in_maps = []
    for b in range(B):
        d = dict(shared)
        d["x"] = np.ascontiguousarray(x[b].astype(np.float32))
        in_maps.append(d)
    res = run_bass_kernel_spmd(nc, in_maps, core_ids=list(range(B)))
    out = np.stack([np.asarray(res.results[b]["out"]) for b in range(B)], axis=0)
    return out.astype(np.float32)
```
